# Optimizing a Trainium2 kernel written in Bass

```python
import math
import jax, jax.numpy as jnp
from jax import lax
import numpy as np

D_MODEL = 2048
BATCH = 2
SEQ = 4096
DEPTH = 4

N_MEM = 256
D_FF = 5632
RMS_EPS = 1e-5

WINDOW = 128
HEAD_DIM = 64
N_Q_HEADS = 16
N_KV_HEADS = 4
GQA_REP = N_Q_HEADS // N_KV_HEADS
Q_WIDTH = N_Q_HEADS * HEAD_DIM
KV_WIDTH = N_KV_HEADS * HEAD_DIM

SSM_WIDTH = 1024
SSM_GROUP = 16
SSM_GROUPS = SSM_WIDTH // SSM_GROUP
SSM_STATE = 64
DT_MIN = 1e-3
DT_MAX = 1e-1

MEM_HEADS = 4
MEM_HEAD_DIM = 256
MEM_WIDTH = MEM_HEADS * MEM_HEAD_DIM

N_BRANCHES = 3
IN_SPLITS = (Q_WIDTH, KV_WIDTH, KV_WIDTH, SSM_WIDTH, MEM_WIDTH, N_BRANCHES * D_MODEL)
IN_WIDTH = sum(IN_SPLITS)
NEG_INF = -1e30

kernel_name = "macaron_hybrid_swa_s5_memory_gated"


def rms_norm(x, w):
    xf = x.astype(jnp.float32)
    y = xf * lax.rsqrt(jnp.mean(xf * xf, axis=-1, keepdims=True) + RMS_EPS)
    return (y * w.astype(jnp.float32)).astype(x.dtype)


def swiglu(x, w_in, w_out):
    g, u = jnp.split(x @ w_in, 2, axis=-1)
    return (jax.nn.silu(g) * u) @ w_out


def sliding_window_attention(q, k, v, sinks):
    B, L = q.shape[0], q.shape[1]
    W = WINDOW
    nb = L // W
    qb = q.reshape(B, nb, W, N_KV_HEADS, GQA_REP, HEAD_DIM)
    kb = k.reshape(B, nb, W, N_KV_HEADS, HEAD_DIM)
    vb = v.reshape(B, nb, W, N_KV_HEADS, HEAD_DIM)
    pad = jnp.zeros_like(kb[:, :1])
    kk = jnp.concatenate([jnp.concatenate([pad, kb[:, :-1]], axis=1), kb], axis=2)
    vv = jnp.concatenate([jnp.concatenate([pad, vb[:, :-1]], axis=1), vb], axis=2)
    scale = HEAD_DIM ** -0.5
    s = jnp.einsum('bnqgrd,bnkgd->bngrqk', qb, kk).astype(jnp.float32) * scale
    qi = jnp.arange(W)[:, None]
    kj = jnp.arange(2 * W)[None, :] - W
    rel = qi - kj
    local = (rel >= 0) & (rel < W)
    blk = jnp.arange(nb)[:, None, None]
    valid = local[None] & ((blk * W + kj[None]) >= 0)
    s = jnp.where(valid[None, :, None, None], s, NEG_INF)
    sink = jnp.broadcast_to(
        sinks.astype(jnp.float32).reshape(N_KV_HEADS, GQA_REP)[None, None, :, :, None, None],
        s.shape[:-1] + (1,))
    p = jax.nn.softmax(jnp.concatenate([s, sink], axis=-1), axis=-1)[..., :-1]
    o = jnp.einsum('bngrqk,bnkgd->bnqgrd', p.astype(v.dtype), vv)
    return o.reshape(B, L, Q_WIDTH)


def memory_cross_attention(q, mem_k, mem_v):
    s = jnp.einsum('blhd,bmhd->bhlm', q, mem_k).astype(jnp.float32) * (MEM_HEAD_DIM ** -0.5)
    p = jax.nn.softmax(s, axis=-1)
    o = jnp.einsum('bhlm,bmhd->blhd', p.astype(mem_v.dtype), mem_v)
    return o.reshape(q.shape[0], q.shape[1], MEM_WIDTH)


def s5_ssm(u, lam_re, lam_im, log_dt, b_re, b_im, c_re, c_im, d_skip):
    Bsz, L = u.shape[0], u.shape[1]
    uf = u.astype(jnp.float32).reshape(Bsz, L, SSM_GROUPS, SSM_GROUP)
    lr = jnp.minimum(lam_re.astype(jnp.float32), -1e-4)
    li = lam_im.astype(jnp.float32)
    dt = jnp.exp(log_dt.astype(jnp.float32))[:, None]
    mag = jnp.exp(lr * dt)
    ar = mag * jnp.cos(li * dt)
    ai = mag * jnp.sin(li * dt)
    nr, ni = ar - 1.0, ai
    den = lr * lr + li * li
    kr = (nr * lr + ni * li) / den
    ki = (ni * lr - nr * li) / den
    br, bi = b_re.astype(jnp.float32), b_im.astype(jnp.float32)
    bbr = kr[..., None] * br - ki[..., None] * bi
    bbi = kr[..., None] * bi + ki[..., None] * br
    xr = jnp.einsum('blgc,gpc->blgp', uf, bbr)
    xi = jnp.einsum('blgc,gpc->blgp', uf, bbi)
    a_r = jnp.broadcast_to(ar, xr.shape)
    a_i = jnp.broadcast_to(ai, xi.shape)

    def combine(e1, e2):
        a1r, a1i, b1r, b1i = e1
        a2r, a2i, b2r, b2i = e2
        return (a1r * a2r - a1i * a2i,
                a1r * a2i + a1i * a2r,
                a2r * b1r - a2i * b1i + b2r,
                a2r * b1i + a2i * b1r + b2i)

    _, _, sr, si = lax.associative_scan(combine, (a_r, a_i, xr, xi), axis=1)
    y = (jnp.einsum('blgp,gcp->blgc', sr, c_re.astype(jnp.float32))
         - jnp.einsum('blgp,gcp->blgc', si, c_im.astype(jnp.float32)))
    y = y + d_skip.astype(jnp.float32).reshape(SSM_GROUPS, SSM_GROUP) * uf
    return y.reshape(Bsz, L, SSM_WIDTH).astype(u.dtype)


def setup_inputs(seed: int = 0) -> dict:
    key = jax.random.key(seed)
    ks = iter(jax.random.split(key, 40))

    def nrm(shape, scale):
        return jax.random.normal(next(ks), shape, jnp.float32) * scale

    def gain(shape):
        return 1.0 + nrm(shape, 0.02)

    Lr, G, P, CH = DEPTH, SSM_GROUPS, SSM_STATE, SSM_GROUP
    lam_re = -0.5 + nrm((Lr, G, P), 0.01)
    lam_im = math.pi * jnp.arange(P, dtype=jnp.float32)[None, None, :] + nrm((Lr, G, P), 0.01)
    log_dt = jax.random.uniform(next(ks), (Lr, G), jnp.float32,
                                math.log(DT_MIN), math.log(DT_MAX))
    return {
        "x": nrm((BATCH, SEQ, D_MODEL), 1.0),
        "mem": nrm((BATCH, N_MEM, D_MODEL), 1.0),
        "ffn1_norm": gain((Lr, D_MODEL)),
        "ffn1_w_in": nrm((Lr, D_MODEL, 2 * D_FF), D_MODEL ** -0.5),
        "ffn1_w_out": nrm((Lr, D_FF, D_MODEL), D_FF ** -0.5),
        "mix_norm": gain((Lr, D_MODEL)),
        "mem_norm": gain((Lr, D_MODEL)),
        "w_in": nrm((Lr, D_MODEL, IN_WIDTH), D_MODEL ** -0.5),
        "sinks": nrm((Lr, N_Q_HEADS), 0.5),
        "w_mem_kv": nrm((Lr, D_MODEL, 2 * MEM_WIDTH), D_MODEL ** -0.5),
        "lam_re": lam_re,
        "lam_im": lam_im,
        "log_dt": log_dt,
        "b_re": nrm((Lr, G, P, CH), (2 * CH) ** -0.5),
        "b_im": nrm((Lr, G, P, CH), (2 * CH) ** -0.5),
        "c_re": nrm((Lr, G, CH, P), (2 * P) ** -0.5),
        "c_im": nrm((Lr, G, CH, P), (2 * P) ** -0.5),
        "d_skip": nrm((Lr, SSM_WIDTH), 1.0),
        "w_ssm_glu": nrm((Lr, SSM_WIDTH, 2 * D_MODEL), SSM_WIDTH ** -0.5),
        "w_swa_up": nrm((Lr, Q_WIDTH, D_MODEL), Q_WIDTH ** -0.5),
        "w_mem_up": nrm((Lr, MEM_WIDTH, D_MODEL), MEM_WIDTH ** -0.5),
        "w_out": nrm((Lr, D_MODEL, D_MODEL), D_MODEL ** -0.5),
        "ffn2_norm": gain((Lr, D_MODEL)),
        "ffn2_w_in": nrm((Lr, D_MODEL, 2 * D_FF), D_MODEL ** -0.5),
        "ffn2_w_out": nrm((Lr, D_FF, D_MODEL), D_FF ** -0.5),
        "final_norm": gain((D_MODEL,)),
    }


def reference(x, mem, ffn1_norm, ffn1_w_in, ffn1_w_out, mix_norm, mem_norm, w_in, sinks,
              w_mem_kv, lam_re, lam_im, log_dt, b_re, b_im, c_re, c_im, d_skip, w_ssm_glu,
              w_swa_up, w_mem_up, w_out, ffn2_norm, ffn2_w_in, ffn2_w_out, final_norm):
    B, L = x.shape[0], x.shape[1]
    M = mem.shape[1]
    split_points = list(np.cumsum(IN_SPLITS)[:-1])
    h = x
    for l in range(DEPTH):
        h = h + 0.5 * swiglu(rms_norm(h, ffn1_norm[l]), ffn1_w_in[l], ffn1_w_out[l])

        u = rms_norm(h, mix_norm[l])
        q, k, v, s_in, mq, gates = jnp.split(u @ w_in[l], split_points, axis=-1)

        y_swa = sliding_window_attention(
            q.reshape(B, L, N_Q_HEADS, HEAD_DIM),
            k.reshape(B, L, N_KV_HEADS, HEAD_DIM),
            v.reshape(B, L, N_KV_HEADS, HEAD_DIM),
            sinks[l]) @ w_swa_up[l]

        y_s = s5_ssm(s_in, lam_re[l], lam_im[l], log_dt[l], b_re[l], b_im[l],
                     c_re[l], c_im[l], d_skip[l])
        ga, gb = jnp.split(y_s @ w_ssm_glu[l], 2, axis=-1)
        y_ssm = ga * jax.nn.sigmoid(gb)

        mk, mv = jnp.split(rms_norm(mem, mem_norm[l]) @ w_mem_kv[l], 2, axis=-1)
        y_mem = memory_cross_attention(
            mq.reshape(B, L, MEM_HEADS, MEM_HEAD_DIM),
            mk.reshape(B, M, MEM_HEADS, MEM_HEAD_DIM),
            mv.reshape(B, M, MEM_HEADS, MEM_HEAD_DIM)) @ w_mem_up[l]

        g = jax.nn.sigmoid(gates.reshape(B, L, N_BRANCHES, D_MODEL))
        merged = g[:, :, 0] * y_swa + g[:, :, 1] * y_ssm + g[:, :, 2] * y_mem
        h = h + merged @ w_out[l]

        h = h + 0.5 * swiglu(rms_norm(h, ffn2_norm[l]), ffn2_w_in[l], ffn2_w_out[l])
    return rms_norm(h, final_norm)
```

```python
import contextlib
import math
import numpy as np
import concourse.bass as bass
import concourse.mybir as mybir
from concourse.bass_utils import run_bass_kernel_spmd

F32 = mybir.dt.float32
BF16 = mybir.dt.bfloat16
AF = mybir.ActivationFunctionType
ALU = mybir.AluOpType

D = 2048
KD = 16
DFF = 5632
NFF = 44
TT = 512
NMEM = 256
INW = 9728
EPS = 1e-5
MAGIC = 12582912.0
NCONST = 1024 + 512 + 128 + 8 + 128
MASK_ROW, MASK_FIRST, IOTA, IDENT, QM, SWAPP = 0, 512, 1024, 1536, 1664, 1672
SEM_ROLL = 30000


class Buf:
    __slots__ = ("w", "r", "dsem", "dcnt", "dkey", "excl")

    def __init__(self):
        self.excl = False
        self.w = None
        self.r = []
        self.dsem = None
        self.dcnt = 0
        self.dkey = None


class Sched:
    def __init__(self, nc):
        self.nc = nc
        self.eng = {"pe": nc.tensor, "act": nc.scalar, "dve": nc.vector, "pool": nc.gpsimd, "sp": nc.sync}
        self.sem = {}
        self.key = {}
        self.cnt = {}
        self.nsem = 0
        for k in ("pe", "act", "dve", "pool"):
            self._roll(k)
        self.waited = {k: {} for k in self.eng}
        self.bufs = []
        self.old = []

    def _newsem(self):
        self.nsem += 1
        return self.nc.alloc_semaphore("sm%d" % self.nsem), "K%d" % self.nsem

    def _roll(self, k):
        if k in self.sem:
            self.old.append((self.key[k], self.sem[k], self.cnt[k]))
        self.sem[k], self.key[k] = self._newsem()
        self.cnt[k] = 0

    def buf(self):
        b = Buf()
        self.bufs.append(b)
        return b

    def _need(self, e, toks):
        eng = self.eng[e]
        best = {}
        for t in toks:
            if t is None:
                continue
            key, sem, val = t
            if e == "pe" and key == self.key.get("pe"):
                continue
            if best.get(key, (None, 0))[1] < val:
                best[key] = (sem, val)
        for key, (sem, val) in best.items():
            if self.waited[e].get(key, 0) < val:
                eng.wait_ge(sem, val)
                self.waited[e][key] = val

    @staticmethod
    def _deps(reads, writes):
        toks = []
        for b in reads:
            toks.append(b.w)
            if b.excl:
                toks.extend(b.r)
        for b in writes:
            toks.append(b.w)
            toks.extend(b.r)
        return toks

    def op(self, e, fn, reads=(), writes=()):
        self._need(e, self._deps(reads, writes))
        if self.cnt[e] >= SEM_ROLL:
            self._roll(e)
        inst = fn()
        self.cnt[e] += 1
        inst.then_inc(self.sem[e], 1)
        tok = (self.key[e], self.sem[e], self.cnt[e])
        for b in writes:
            b.w = tok
            b.r = []
        for b in reads:
            b.r.append(tok)
            if len(b.r) > 64:
                b.r = self._compact(b.r)
        return tok

    @staticmethod
    def _compact(toks):
        best = {}
        for key, sem, val in toks:
            if best.get(key, (None, 0))[1] < val:
                best[key] = (sem, val)
        return [(k, s, v) for k, (s, v) in best.items()]

    def dma(self, e, out, in_, reads=(), writes=(), owner=None, **kw):
        self._need(e, self._deps(reads, writes))
        b = owner if owner is not None else (writes[0] if writes else reads[0])
        if b.dsem is None or b.dcnt + 16 > SEM_ROLL:
            if b.dsem is not None:
                self.old.append((b.dkey, b.dsem, b.dcnt))
            b.dsem, b.dkey = self._newsem()
            b.dcnt = 0
        b.dcnt += 16
        self.eng[e].dma_start(out=out, in_=in_, **kw).then_inc(b.dsem, 16)
        tok = (b.dkey, b.dsem, b.dcnt)
        for w in writes:
            w.w = tok
            w.r = []
        for r in reads:
            r.r.append(tok)
            if len(r.r) > 64:
                r.r = self._compact(r.r)
        return tok

    def all_tokens(self):
        toks = [(self.key[k], self.sem[k], self.cnt[k]) for k in self.sem if self.cnt[k] > 0]
        toks += [t for t in self.old if t[2] > 0]
        for b in self.bufs:
            if b.dsem is not None and b.dcnt > 0:
                toks.append((b.dkey, b.dsem, b.dcnt))
        return toks

    def barrier(self):
        toks = self.all_tokens()
        for e in self.eng:
            self._need(e, toks)

    def wait_everything(self, e):
        self._need(e, self.all_tokens())


class _Stop(Exception):
    pass


def build(NT, NL, dbg=False, stop=99):
    LT = NT * TT
    nc = bass.Bass("TRN2", target_bir_lowering=False)
    S = Sched(nc)

    def din(name, shape, dt=F32):
        return nc.dram_tensor(name, list(shape), dt, kind="ExternalInput").ap()

    xT = din("xT", [D, LT])
    memT = din("memT", [D, NMEM])
    outT = nc.dram_tensor("outT", [D, LT], F32, kind="ExternalOutput").ap()
    if dbg:
        dbg_o = nc.dram_tensor("dbg_o", [3, 1024, LT], F32, kind="ExternalOutput").ap()
    W = {}
    for nm, shp in [("ffn1_w_in", [NL, D, 2 * DFF]), ("ffn1_w_out", [NL, DFF, D]), ("w_in", [NL, D, INW]),
                    ("w_mem_kv", [NL, D, 2048]), ("w_ssm_glu", [NL, 1024, 4096]), ("w_swa_up", [NL, 1024, D]),
                    ("w_mem_up", [NL, 1024, D]), ("w_out", [NL, D, D]), ("ffn2_w_in", [NL, D, 2 * DFF]),
                    ("ffn2_w_out", [NL, DFF, D])]:
        W[nm] = din(nm, shp)
    norms = din("norms", [128, NL, 4, KD])
    fnorm = din("fnorm", [128, KD])
    sinkc = din("sinkc", [128, NL, 8])
    ssmB = din("ssmB", [128, NL, 5, 8, 64])
    ssmS = din("ssmS", [128, NL, 3, 64])
    ssmC = din("ssmC", [128, NL, 2, 64, 16])
    dskip = din("dskip", [128, NL, 8])
    consts = din("consts", [128, NCONST])
    hs = nc.dram_tensor("hs", [D, LT], F32)
    tabs = nc.dram_tensor("tabs", [64, 128, 2, TT], F32)

    es = contextlib.ExitStack()

    def sb(name, shape, dt):
        return es.enter_context(nc.sbuf_tensor(name, list(shape), dt))

    hT = sb("hT", [128, KD, TT], F32)
    uT = sb("uT", [128, KD, TT], BF16)
    cst = sb("cst", [128, NCONST], F32)
    maskb = sb("maskb", [128, 1024], BF16)
    identb = sb("identb", [128, 128], BF16)
    onesb = sb("onesb", [128, 192], BF16)
    onesf_t = sb("onesf", [128, 128], BF16)
    onesf = onesf_t[:]
    epsc = sb("epsc", [128, 1], F32)
    nrm = sb("nrm", [128, NL, 4, KD], F32)
    fnrm = sb("fnrm", [128, KD], F32)
    skc = sb("skc", [128, NL, 8], F32)
    esk = sb("esk", [128, 8], F32)
    sq = sb("sq", [128, 2, TT], BF16)
    rs = sb("rs", [128, TT], F32)
    wgu = sb("wgu", [128, 4, KD, 128], BF16)
    kT2 = sb("kT2", [128, 2, 4, 128 + TT], BF16)
    vaug = sb("vaug", [128, 5, 4, 192], BF16)
    mkT = sb("mkT", [128, 8, NMEM], BF16)
    mv = sb("mv", [128, 2, 1024], BF16)
    BA = sb("BA", [128, 8, 4, 128], BF16)
    BB = sb("BB", [128, 8, 4, 128], BF16)
    C1 = sb("C1", [128, 64, 64], BF16)
    C2 = sb("C2", [128, 64, 64], BF16)
    prmS = sb("prmS", [128, 3, 64], F32)
    tS = sb("tS", [128, 10, 64], F32)
    dsk = sb("dsk", [128, 8], F32)
    state = sb("state", [128, 64], F32)
    slast = sb("slast", [128, 64], F32)
    swapf = sb("swapf", [128, 128], F32)
    psum = es.enter_context(nc.psum_tensor("psum", [128, 8, TT], F32))

    b = {}

    def B(name):
        if name not in b:
            b[name] = S.buf()
        return b[name]

    psb = [S.buf() for _ in range(8)]
    for _b in psb:
        _b.excl = True
    psi = [0]
    psl = [0]

    def PS():
        i = psi[0] % 6
        psi[0] += 1
        return psb[i], psum[:, i, :]

    def PSL():
        i = 6 + psl[0] % 2
        psl[0] += 1
        return psb[i], psum[:, i, :]

    @contextlib.contextmanager
    def scope(*specs):
        with contextlib.ExitStack() as st:
            scope.n += 1
            ts = [st.enter_context(nc.sbuf_tensor("%s_%d" % (n, scope.n), list(s), d)) for n, s, d in specs]
            yield ts
            S.barrier()

    scope.n = 0
    V = nc.vector
    A = nc.scalar

    def vop(fn, reads, writes, e="dve"):
        S.op(e, fn, reads=[B(x) if isinstance(x, str) else x for x in reads],
             writes=[B(x) if isinstance(x, str) else x for x in writes])

    S.dma("sp", cst[:], consts[:, :], writes=[B("cst")])
    S.dma("sp", nrm[:], norms[:, :, :, :], writes=[B("nrm")])
    S.dma("sp", fnrm[:], fnorm[:, :], writes=[B("fnrm")])
    S.dma("sp", skc[:], sinkc[:, :, :], writes=[B("skc")])
    S.dma("sp", swapf[:], consts[:, SWAPP:SWAPP + 128], writes=[B("swapf")])
    vop(lambda: V.tensor_copy(maskb[:], cst[:, 0:1024]), ["cst"], ["maskb"])
    vop(lambda: V.tensor_copy(identb[:], cst[:, IDENT:IDENT + 128]), ["cst"], ["identb"])
    vop(lambda: V.memset(onesb[:], 1.0), [], ["onesb"])
    vop(lambda: V.memset(onesb[:, 64:128], 0.0), [], ["onesb"])
    vop(lambda: V.memset(onesf_t[:], 1.0), [], ["onesf"])
    vop(lambda: V.memset(epsc[:], EPS), [], ["epsc"])
    vop(lambda: V.memset(vaug[:], 0.0), [], ["vaug"])
    vop(lambda: V.memset(kT2[:], 0.0), [], ["kT2"])
    vop(lambda: V.memset(BA[:], 0.0), [], ["BA"])
    vop(lambda: V.memset(BB[:], 0.0), [], ["BB"])
    vop(lambda: V.memset(C1[:], 0.0), [], ["C1"])
    vop(lambda: V.memset(C2[:], 0.0), [], ["C2"])

    wslot = [0]

    def wslot_next():
        i = wslot[0] % 4
        wslot[0] += 1
        return i, B("wgu%d" % i)

    def load_w(dram_view):
        i, wb = wslot_next()
        S.dma("pool", wgu[:, i, :, :], dram_view, writes=[wb])
        return wb, wgu[:, i, :, :]

    def colview(wl, c0, n=128):
        return wl.rearrange("(kt p) n -> p kt n", p=128)[:, :, c0:c0 + n]

    def rmsnorm(gain_ap, gbuf, src, srcbuf, dst, dstbuf, nk=KD, ncol=TT):
        pb, pa = PS()
        for k in range(nk):
            sl = k % 2
            S.op("act", lambda: A.activation(out=sq[:, sl, 0:ncol], in_=src[:, k, :], func=AF.Square),
                 reads=[srcbuf], writes=[B("sq%d" % sl)])
            S.op("pe", lambda: nc.tensor.matmul(pa[:, 0:ncol], onesf, sq[:, sl, 0:ncol], start=(k == 0), stop=(k == nk - 1)),
                 reads=[B("sq%d" % sl), B("onesf")], writes=[pb])
        S.op("act", lambda: A.activation(out=rs[:, 0:ncol], in_=pa[:, 0:ncol], func=AF.Sqrt, bias=epsc[:, 0:1],
                                         scale=1.0 / D), reads=[pb, B("epsc")], writes=[B("rs")])
        S.op("dve", lambda: V.reciprocal(rs[:, 0:ncol], rs[:, 0:ncol]), reads=[B("rs")], writes=[B("rs")])
        for k in range(nk):
            S.op("dve", lambda: V.scalar_tensor_tensor(out=dst[:, k, :], in0=src[:, k, :], scalar=gain_ap[:, k:k + 1],
                                                       in1=rs[:, 0:ncol], op0=ALU.mult, op1=ALU.mult),
                 reads=[srcbuf, B("rs"), gbuf], writes=[dstbuf])

    def ffn(l, w_in_name, w_out_name, which):
        win = W[w_in_name][l]
        wout = W[w_out_name][l].rearrange("(j p) n -> p j n", p=128)
        rmsnorm(nrm[:, l, which, :], B("nrm"), hT, B("hT"), uT, B("uT"))
        with scope(("hid", [128, 2, 4, TT], BF16), ("sg", [128, 2, TT], F32), ("wo", [128, 2, 4, D], BF16)) as (hid, sg, wo):
            for c in range(NFF // 4):
                hs_ = c % 2
                wob = B("wo%d" % hs_)
                S.dma("pool", wo[:, hs_, :, :], wout[:, 4 * c:4 * c + 4, :], writes=[wob])
                woa = wo[:, hs_, :, :]
                for jj in range(4):
                    j = 4 * c + jj
                    wgb, wga = load_w(colview(win, j * 128))
                    wub, wua = load_w(colview(win, DFF + j * 128))
                    pgb, pg = PS()
                    pub, pu = PS()

                    def mm_g():
                        last = None
                        for k in range(KD):
                            last = nc.tensor.matmul(pg, wga[:, k, :], uT[:, k, :], start=(k == 0), stop=(k == KD - 1))
                        return last

                    def mm_u():
                        last = None
                        for k in range(KD):
                            last = nc.tensor.matmul(pu, wua[:, k, :], uT[:, k, :], start=(k == 0), stop=(k == KD - 1))
                        return last
                    S.op("pe", mm_g, reads=[wgb, B("uT")], writes=[pgb])
                    S.op("pe", mm_u, reads=[wub, B("uT")], writes=[pub])
                    sl = j % 2
                    S.op("act", lambda: A.activation(out=sg[:, sl, :], in_=pg, func=AF.Silu), reads=[pgb], writes=[B("sg%d" % sl)])
                    S.op("dve", lambda: V.tensor_tensor(out=hid[:, hs_, jj, :], in0=pu, in1=sg[:, sl, :], op=ALU.mult),
                         reads=[pub, B("sg%d" % sl)], writes=[B("hid%d" % hs_)])
                for i in range(KD):
                    pob, po = PS()

                    def mm_o():
                        last = None
                        for jj in range(4):
                            last = nc.tensor.matmul(po, woa[:, jj, i * 128:(i + 1) * 128], hid[:, hs_, jj, :], start=(jj == 0),
                                                    stop=(jj == 3))
                        return last
                    S.op("pe", mm_o, reads=[wob, B("hid%d" % hs_)], writes=[pob])
                    S.op("dve", lambda: V.scalar_tensor_tensor(out=hT[:, i, :], in0=po, scalar=0.5, in1=hT[:, i, :],
                                                               op0=ALU.mult, op1=ALU.add), reads=[pob], writes=[B("hT")])

    def mem_prep(l):
        with scope(("memf", [128, KD, NMEM], F32), ("memn", [128, KD, NMEM], BF16)) as (memf, memn):
            S.dma("sp", memf[:], memT.rearrange("(kt p) n -> p kt n", p=128), writes=[B("memf")])
            rmsnorm(nrm[:, l, 2, :], B("nrm"), memf, B("memf"), memn, B("memn"), ncol=NMEM)
            wk = W["w_mem_kv"][l]
            for i in range(8):
                wb, wa = load_w(colview(wk, i * 128))
                pb, pa = PS()

                def mm():
                    last = None
                    for k in range(KD):
                        last = nc.tensor.matmul(pa[:, 0:NMEM], wa[:, k, :], memn[:, k, :], start=(k == 0), stop=(k == KD - 1))
                    return last
                S.op("pe", mm, reads=[wb, B("memn")], writes=[pb])
                S.op("act", lambda: A.copy(mkT[:, i, :], pa[:, 0:NMEM]), reads=[pb], writes=[B("mkT")])
            for i in range(8):
                wb, wa = load_w(colview(wk, 1024 + i * 128))
                for mt in range(2):
                    pb, pa = PS()

                    def mm():
                        last = None
                        for k in range(KD):
                            last = nc.tensor.matmul(pa[:, 0:128], memn[:, k, mt * 128:(mt + 1) * 128], wa[:, k, :], start=(k == 0),
                                                    stop=(k == KD - 1))
                        return last
                    S.op("pe", mm, reads=[wb, B("memn")], writes=[pb])
                    S.op("dve", lambda: V.tensor_copy(mv[:, mt, i * 128:(i + 1) * 128], pa[:, 0:128]), reads=[pb], writes=[B("mv")])

    def disc(prm, tt, R, Wt):
        vop(lambda: V.tensor_scalar_min(out=tt[0], in0=prm[0], scalar1=-1e-4), R, Wt)
        vop(lambda: A.activation(out=tt[1], in_=prm[2], func=AF.Exp), R, Wt, e="act")
        vop(lambda: V.tensor_tensor(out=tt[8], in0=tt[0], in1=tt[1], op=ALU.mult), Wt, Wt)
        vop(lambda: A.activation(out=tt[2], in_=tt[8], func=AF.Exp), Wt, Wt, e="act")
        vop(lambda: V.scalar_tensor_tensor(out=tt[3], in0=prm[1], scalar=1.0 / (2 * math.pi), in1=tt[1], op0=ALU.mult,
                                           op1=ALU.mult), R + Wt, Wt)
        sincos(tt[3], tt[5], tt[4], tt[8], tt[9], Wt)

    def sincos(f, s_out, c_out, t1, t2, Wt):
        vop(lambda: V.tensor_scalar(out=t1, in0=f, scalar1=MAGIC, scalar2=MAGIC, op0=ALU.add, op1=ALU.subtract), Wt, Wt)
        vop(lambda: V.tensor_tensor(out=t1, in0=f, in1=t1, op=ALU.subtract), Wt, Wt)
        vop(lambda: A.activation(out=s_out, in_=t1, func=AF.Sin, scale=2 * math.pi), Wt, Wt, e="act")
        vop(lambda: V.tensor_scalar_add(out=t2, in0=f, scalar1=0.25), Wt, Wt)
        vop(lambda: V.tensor_scalar(out=t1, in0=t2, scalar1=MAGIC, scalar2=MAGIC, op0=ALU.add, op1=ALU.subtract), Wt, Wt)
        vop(lambda: V.tensor_tensor(out=t1, in0=t2, in1=t1, op=ALU.subtract), Wt, Wt)
        vop(lambda: A.activation(out=c_out, in_=t1, func=AF.Sin, scale=2 * math.pi), Wt, Wt, e="act")

    def ssm_prep(l):
        with scope(("prmB", [128, 5, 8, 64], F32), ("tB", [128, 10, 8, 64], F32), ("cc", [128, 2, 64, 16], F32),
                   ("targ", [128, 2, TT], F32), ("tabw", [128, 2, 2, TT], F32)) as (prmB, tB, cc, targ, tabw):
            S.dma("sp", prmB[:], ssmB[:, l, :, :, :], writes=[B("Bp")])
            S.dma("sp", prmS[:], ssmS[:, l, :, :], writes=[B("Sp")])
            S.dma("sp", cc[:], ssmC[:, l, :, :, :], writes=[B("cc")])
            S.dma("sp", dsk[:], dskip[:, l, :], writes=[B("dsk")])
            pB = [prmB[:, i, :, :] for i in range(5)]
            tb = [tB[:, i, :, :] for i in range(10)]
            R, Wt = ["Bp", "Bt"], ["Bt"]
            disc(pB, tb, ["Bp"], ["Bt"])
            vop(lambda: V.tensor_tensor(out=tb[4], in0=tb[4], in1=tb[2], op=ALU.mult), Wt, Wt)
            vop(lambda: V.tensor_tensor(out=tb[5], in0=tb[5], in1=tb[2], op=ALU.mult), Wt, Wt)
            vop(lambda: V.tensor_scalar_add(out=tb[4], in0=tb[4], scalar1=-1.0), Wt, Wt)
            vop(lambda: V.tensor_tensor(out=tb[8], in0=tb[0], in1=tb[0], op=ALU.mult), Wt, Wt)
            vop(lambda: V.tensor_tensor(out=tb[9], in0=pB[1], in1=pB[1], op=ALU.mult), R, Wt)
            vop(lambda: V.tensor_tensor(out=tb[8], in0=tb[8], in1=tb[9], op=ALU.add), Wt, Wt)
            vop(lambda: V.reciprocal(tb[8], tb[8]), Wt, Wt)
            vop(lambda: V.tensor_tensor(out=tb[6], in0=tb[4], in1=tb[0], op=ALU.mult), Wt, Wt)
            vop(lambda: V.tensor_tensor(out=tb[9], in0=tb[5], in1=pB[1], op=ALU.mult), R, Wt)
            vop(lambda: V.tensor_tensor(out=tb[6], in0=tb[6], in1=tb[9], op=ALU.add), Wt, Wt)
            vop(lambda: V.tensor_tensor(out=tb[6], in0=tb[6], in1=tb[8], op=ALU.mult), Wt, Wt)
            vop(lambda: V.tensor_tensor(out=tb[7], in0=tb[5], in1=tb[0], op=ALU.mult), Wt, Wt)
            vop(lambda: V.tensor_tensor(out=tb[9], in0=tb[4], in1=pB[1], op=ALU.mult), R, Wt)
            vop(lambda: V.tensor_tensor(out=tb[7], in0=tb[7], in1=tb[9], op=ALU.subtract), Wt, Wt)
            vop(lambda: V.tensor_tensor(out=tb[7], in0=tb[7], in1=tb[8], op=ALU.mult), Wt, Wt)
            vop(lambda: V.tensor_tensor(out=tb[2], in0=tb[6], in1=pB[3], op=ALU.mult), R, Wt)
            vop(lambda: V.tensor_tensor(out=tb[9], in0=tb[7], in1=pB[4], op=ALU.mult), R, Wt)
            vop(lambda: V.tensor_tensor(out=tb[2], in0=tb[2], in1=tb[9], op=ALU.subtract), Wt, Wt)
            vop(lambda: V.tensor_tensor(out=tb[3], in0=tb[6], in1=pB[4], op=ALU.mult), R, Wt)
            vop(lambda: V.tensor_tensor(out=tb[9], in0=tb[7], in1=pB[3], op=ALU.mult), R, Wt)
            vop(lambda: V.tensor_tensor(out=tb[3], in0=tb[3], in1=tb[9], op=ALU.add), Wt, Wt)
            vop(lambda: V.tensor_scalar_mul(out=tb[9], in0=tb[2], scalar1=-1.0), Wt, Wt)
            for v in range(4):
                qm = cst[:, QM + v:QM + v + 1]
                vop(lambda: V.tensor_scalar(out=BA[:, :, v, 0:64], in0=tb[2], scalar1=qm, scalar2=None, op0=ALU.mult), ["Bt", "cst"], ["BA"])
                vop(lambda: V.tensor_scalar(out=BA[:, :, v, 64:128], in0=tb[3], scalar1=qm, scalar2=None, op0=ALU.mult), ["Bt", "cst"], ["BA"])
                vop(lambda: V.tensor_scalar(out=BB[:, :, v, 0:64], in0=tb[3], scalar1=qm, scalar2=None, op0=ALU.mult), ["Bt", "cst"], ["BB"])
                vop(lambda: V.tensor_scalar(out=BB[:, :, v, 64:128], in0=tb[9], scalar1=qm, scalar2=None, op0=ALU.mult), ["Bt", "cst"], ["BB"])
            pS = [prmS[:, i, :] for i in range(3)]
            ts_ = [tS[:, i, :] for i in range(10)]
            disc(pS, ts_, ["Sp"], ["St"])
            Wt = ["St"]
            vop(lambda: V.tensor_scalar_mul(out=ts_[6], in0=ts_[3], scalar1=float(TT)), Wt, Wt)
            sincos(ts_[6], ts_[5], ts_[4], ts_[8], ts_[9], Wt)
            vop(lambda: V.tensor_scalar(out=ts_[5], in0=ts_[5], scalar1=cst[:, QM + 4:QM + 5], scalar2=None, op0=ALU.mult), ["St", "cst"], Wt)
            for v in range(4):
                vop(lambda: V.tensor_scalar(out=C1[:, v::4, 16 * v:16 * v + 16], in0=cc[:, 0, v::4, :], scalar1=cst[:, QM + 5:QM + 6],
                                            scalar2=None, op0=ALU.mult), ["cc", "cst"], ["C1"])
                vop(lambda: V.tensor_scalar_mul(out=C2[:, v::4, 16 * v:16 * v + 16], in0=cc[:, 1, v::4, :], scalar1=-1.0), ["cc"], ["C2"])
            for g in range(64):
                sl = g % 2
                fcol = tS[:, 3, g:g + 1]
                tw = ["targ"]
                vop(lambda: V.tensor_scalar(out=targ[:, 0, :], in0=cst[:, IOTA:IOTA + TT], scalar1=fcol, scalar2=None, op0=ALU.mult),
                    ["St", "cst", "targ"], tw)
                vop(lambda: V.tensor_scalar(out=targ[:, 1, :], in0=targ[:, 0, :], scalar1=MAGIC, scalar2=MAGIC, op0=ALU.add,
                                            op1=ALU.subtract), tw, tw)
                vop(lambda: V.tensor_tensor(out=targ[:, 1, :], in0=targ[:, 0, :], in1=targ[:, 1, :], op=ALU.subtract), tw, tw)
                vop(lambda: A.activation(out=tabw[:, sl, 1, :], in_=targ[:, 1, :], func=AF.Sin, scale=2 * math.pi), tw,
                    ["tabw%d" % sl], e="act")
                vop(lambda: V.tensor_scalar_add(out=targ[:, 0, :], in0=targ[:, 0, :], scalar1=0.25), tw, tw)
                vop(lambda: V.tensor_scalar(out=targ[:, 1, :], in0=targ[:, 0, :], scalar1=MAGIC, scalar2=MAGIC, op0=ALU.add,
                                            op1=ALU.subtract), tw, tw)
                vop(lambda: V.tensor_tensor(out=targ[:, 1, :], in0=targ[:, 0, :], in1=targ[:, 1, :], op=ALU.subtract), tw, tw)
                vop(lambda: A.activation(out=tabw[:, sl, 0, :], in_=targ[:, 1, :], func=AF.Sin, scale=2 * math.pi), tw,
                    ["tabw%d" % sl], e="act")
                S.dma("sp", tabs.ap()[g], tabw[:, sl, :, :], reads=[B("tabw%d" % sl)], writes=[B("tabs")], owner=B("tabs"))
            vop(lambda: V.memset(state[:], 0.0), [], ["state"])

    def proj_fm(wl, c0, dst_ap, dstbuf, evac, dst2=None):
        wb, wa = load_w(colview(wl, c0))
        pb, pa = PS()

        def mm():
            last = None
            for k in range(KD):
                last = nc.tensor.matmul(pa, wa[:, k, :], uT[:, k, :], start=(k == 0), stop=(k == KD - 1))
            return last
        S.op("pe", mm, reads=[wb, B("uT")], writes=[pb])
        if evac == "act":
            S.op("act", lambda: A.copy(dst_ap, pa), reads=[pb], writes=[dstbuf])
        else:
            S.op("dve", lambda: V.tensor_copy(dst_ap, pa), reads=[pb], writes=[dstbuf])
        if dst2 is not None:
            S.op("dve", lambda: V.tensor_scalar(out=dst2[0], in0=pa, scalar1=1.0, scalar2=None, op0=ALU.mult), reads=[pb], writes=[dst2[1]])

    def dump(idx, src, t):
        S.dma("pool", dbg_o[idx, :, t * TT:(t + 1) * TT].rearrange("(kt p) n -> p kt n", p=128), src[:],
              reads=[B("qT"), B("mqT"), B("sinT")], writes=[B("dbgo")], owner=B("dbgo"))

    def mixer(l, t, seq_start):
        wl = W["w_in"][l]
        rmsnorm(nrm[:, l, 1, :], B("nrm"), hT, B("hT"), uT, B("uT"))
        with scope(("qT", [128, 8, TT], BF16), ("mqT", [128, 8, TT], BF16), ("sinT", [128, 8, TT], BF16),
                   ("sinF", [128, 8, TT], F32)) as (qT, mqT, sinT, sinF):
            with scope(("pT", [128, 2, TT], BF16), ("rden", [128, 2, TT], F32), ("wk", [128, KD, 128], BF16)) as (pT, rden, wk):
                if stop == 39:
                    return True
                for i in range(8):
                    proj_fm(wl, i * 128, qT[:, i, :], B("qT"), "act")
                if stop == 38:
                    return True
                for i in range(8):
                    proj_fm(wl, 2560 + i * 128, mqT[:, i, :], B("mqT"), "dve")
                if stop == 37:
                    return True
                for i in range(8):
                    proj_fm(wl, 1536 + i * 128, sinT[:, i, :], B("sinT"), "act", dst2=(sinF[:, i, :], B("sinF")))
                if stop == 40:
                    return True
                for gp in range(2):
                    S.dma("pool", wk[:], colview(wl, 1024 + gp * 128), writes=[B("wk")])
                    for gg in range(2):
                        g = 2 * gp + gg
                        for x in range(2):
                            i, wb = wslot_next()
                            S.op("dve", lambda: V.memset(wgu[:, i, :, (1 - x) * 64:(1 - x) * 64 + 64], 0.0), writes=[wb])
                            S.op("dve", lambda: V.tensor_scalar(out=wgu[:, i, :, x * 64:x * 64 + 64], in0=wk[:, :, gg * 64:(gg + 1) * 64],
                                                                scalar1=1.0, scalar2=None, op0=ALU.mult), reads=[B("wk")], writes=[wb])
                            pb, pa = PS()

                            def mm():
                                last = None
                                for k in range(KD):
                                    last = nc.tensor.matmul(pa, wgu[:, i, k, :], uT[:, k, :], start=(k == 0), stop=(k == KD - 1))
                                return last
                            S.op("pe", mm, reads=[wb, B("uT")], writes=[pb])
                            S.op("act", lambda: A.copy(kT2[:, x, g, 128:128 + TT], pa), reads=[pb], writes=[B("kT2")])
                if stop == 41:
                    return True
                i0, wbv0 = wslot_next()
                S.dma("pool", wgu[:, i0, :, :], colview(wl, 1280, 128), writes=[wbv0])
                i1, wbv1 = wslot_next()
                S.dma("pool", wgu[:, i1, :, :], colview(wl, 1408, 128), writes=[wbv1])
                for bq in range(4):
                    pb, pa = PS()

                    def mm():
                        last = None
                        for half, wi in ((0, i0), (1, i1)):
                            for k in range(KD):
                                last = nc.tensor.matmul(pa[:, half * 128:(half + 1) * 128], uT[:, k, bq * 128:(bq + 1) * 128],
                                                        wgu[:, wi, k, :], start=(k == 0), stop=(k == KD - 1))
                        return last
                    S.op("pe", mm, reads=[wbv0, wbv1, B("uT")], writes=[pb])
                    pv = pa[:, 0:256].rearrange("p (g d) -> p g d", g=4)
                    S.op("act", lambda: A.copy(vaug[:, 1 + bq, :, 0:64], pv), reads=[pb], writes=[B("vaug")])
                    S.op("dve", lambda: V.tensor_copy(vaug[:, 1 + bq, :, 128:192], pv), reads=[pb], writes=[B("vaug")])
                if stop == 42:
                    return True
                S.op("act", lambda: A.activation(out=esk[:], in_=skc[:, l, :], func=AF.Exp), reads=[B("skc")], writes=[B("esk")])
                for bq in range(4):
                    for g in range(4):
                        for hp in range(2):
                            psb_, ps_ = PS()
                            m0 = MASK_FIRST if (seq_start and bq == 0) else MASK_ROW
                            qt = 2 * g + hp

                            def mm_s():
                                nc.tensor.matmul(ps_, identb[:], maskb[:, m0:m0 + 512], start=True, stop=False)
                                last = None
                                for kb in range(2):
                                    for r in range(2):
                                        c0 = kb * 256 + r * 128
                                        last = nc.tensor.matmul(ps_[:, c0:c0 + 128],
                                                                kT2[:, r, g, (bq + kb) * 128:(bq + kb + 1) * 128],
                                                                qT[:, qt, bq * 128:(bq + 1) * 128], start=False,
                                                                stop=(kb == 1 and r == 1))
                                return last
                            S.op("pe", mm_s, reads=[B("identb"), B("maskb"), B("kT2"), B("qT")], writes=[psb_])
                            sl = (bq * 8 + g * 2 + hp) % 2
                            S.op("act", lambda: A.activation(out=pT[:, sl, :], in_=ps_, func=AF.Exp, scale=0.125),
                                 reads=[psb_], writes=[B("pT%d" % sl)])
                            pob, po = PS()
                            pdb, pd = PS()

                            def mm_o():
                                n = 0
                                for kb in range(2):
                                    for r in range(2):
                                        lo = 0 if r == 0 else 64
                                        c0 = kb * 256 + r * 128
                                        nc.tensor.matmul(po[:, 0:128], vaug[:, bq + kb, g, lo:lo + 128], pT[:, sl, c0:c0 + 128],
                                                         start=(n == 0), stop=(n == 3))
                                        n += 1
                                n = 0
                                last = None
                                for kb in range(2):
                                    for r in range(2):
                                        lo = 0 if r == 0 else 64
                                        c0 = kb * 256 + r * 128
                                        last = nc.tensor.matmul(pd[:, 0:128], onesb[:, lo:lo + 128], pT[:, sl, c0:c0 + 128],
                                                                start=(n == 0), stop=(n == 3))
                                        n += 1
                                return last
                            S.op("pe", mm_o, reads=[B("vaug"), B("onesb"), B("pT%d" % sl)], writes=[pob, pdb])
                            idx = g * 2 + hp
                            S.op("dve", lambda: V.tensor_scalar(out=rden[:, sl, 0:128], in0=pd[:, 0:128], scalar1=esk[:, idx:idx + 1],
                                                                scalar2=None, op0=ALU.add), reads=[pdb, B("esk")],
                                 writes=[B("rden%d" % sl)])
                            S.op("dve", lambda: V.reciprocal(rden[:, sl, 0:128], rden[:, sl, 0:128]), reads=[B("rden%d" % sl)],
                                 writes=[B("rden%d" % sl)])
                            S.op("dve", lambda: V.tensor_tensor(out=qT[:, qt, bq * 128:(bq + 1) * 128], in0=po[:, 0:128],
                                                                in1=rden[:, sl, 0:128], op=ALU.mult),
                                 reads=[pob, B("rden%d" % sl)], writes=[B("qT")])
                if stop == 43:
                    return True
                vop(lambda: V.tensor_scalar(out=kT2[:, 0, :, 0:128], in0=kT2[:, 0, :, TT:TT + 128], scalar1=1.0, scalar2=None, op0=ALU.mult), ["kT2"], ["kT2"])
                vop(lambda: V.tensor_scalar(out=kT2[:, 1, :, 0:128], in0=kT2[:, 1, :, TT:TT + 128], scalar1=1.0, scalar2=None, op0=ALU.mult), ["kT2"], ["kT2"])
                vop(lambda: V.tensor_scalar(out=vaug[:, 0, :, :], in0=vaug[:, 4, :, :], scalar1=1.0, scalar2=None, op0=ALU.mult), ["vaug"], ["vaug"])
                for hm in range(4):
                    for mt in range(2):
                        pb, pa = PS()

                        def mm():
                            last = None
                            for dt_ in range(2):
                                last = nc.tensor.matmul(pa, mkT[:, 2 * hm + dt_, mt * 128:(mt + 1) * 128], mqT[:, 2 * hm + dt_, :],
                                                        start=(dt_ == 0), stop=(dt_ == 1))
                            return last
                        S.op("pe", mm, reads=[B("mkT"), B("mqT")], writes=[pb])
                        S.op("act", lambda: A.activation(out=pT[:, mt, :], in_=pa, func=AF.Exp, scale=1.0 / 16.0), reads=[pb],
                             writes=[B("pT%d" % mt)])
                    pdb, pd = PS()

                    def mm_d():
                        last = None
                        for mt in range(2):
                            last = nc.tensor.matmul(pd, onesf, pT[:, mt, :], start=(mt == 0), stop=(mt == 1))
                        return last
                    S.op("pe", mm_d, reads=[B("onesf"), B("pT0"), B("pT1")], writes=[pdb])
                    S.op("dve", lambda: V.reciprocal(rden[:, 0, :], pd), reads=[pdb], writes=[B("rden0")])
                    for dt_ in range(2):
                        pob, po = PS()

                        def mm_o():
                            last = None
                            for mt in range(2):
                                last = nc.tensor.matmul(po, mv[:, mt, (2 * hm + dt_) * 128:(2 * hm + dt_ + 1) * 128], pT[:, mt, :],
                                                        start=(mt == 0), stop=(mt == 1))
                            return last
                        S.op("pe", mm_o, reads=[B("mv"), B("pT0"), B("pT1")], writes=[pob])
                        S.op("dve", lambda: V.tensor_tensor(out=mqT[:, 2 * hm + dt_, :], in0=po, in1=rden[:, 0, :], op=ALU.mult),
                             reads=[pob, B("rden0")], writes=[B("mqT")])
            if stop == 44:
                return True
            if stop == 4:
                if dbg:
                    dump(0, qT, t)
                    dump(1, mqT, t)
                    dump(2, sinT, t)
                return True
            with scope(("tab", [128, 2, 2, TT], F32), ("xm", [128, 2, TT], F32), ("d12", [128, 2, 2, TT], BF16),
                       ("abr", [128, 2, TT], F32)) as (tab, xm, d12, abr):
                vop(lambda: V.memset(abr[:], 0.0), [], ["abr0", "abr1"])
                cur = None
                for g in range(64):
                    j, m = g // 8, g % 8
                    qd, v = m // 4, m % 4
                    r0 = 64 * qd
                    sl = g % 2
                    tb_ = B("tab%d" % sl)
                    S.dma("sp", tab[:, sl, :, :], tabs.ap()[g], reads=[B("tabs")], writes=[tb_], owner=tb_)
                    pab, pa = PS()
                    pbb, pb_ = PS()
                    S.op("pe", lambda: nc.tensor.matmul(pa, BA[r0:r0 + 64, j, v, :], sinT[r0:r0 + 64, j, :], start=True, stop=True),
                         reads=[B("BA"), B("sinT")], writes=[pab])
                    S.op("pe", lambda: nc.tensor.matmul(pb_, BB[r0:r0 + 64, j, v, :], sinT[r0:r0 + 64, j, :], start=True, stop=True),
                         reads=[B("BB"), B("sinT")], writes=[pbb])
                    S.op("dve", lambda: V.tensor_tensor(out=xm[:, 0, :], in0=pa, in1=tab[:, sl, 0, :], op=ALU.mult), reads=[pab, tb_],
                         writes=[B("xm0")])
                    S.op("dve", lambda: V.tensor_tensor(out=xm[:, 1, :], in0=pb_, in1=tab[:, sl, 1, :], op=ALU.mult), reads=[pbb, tb_],
                         writes=[B("xm1")])
                    S.op("dve", lambda: V.tensor_tensor(out=xm[:, 0, :], in0=xm[:, 0, :], in1=xm[:, 1, :], op=ALU.add), reads=[B("xm1")],
                         writes=[B("xm0")])
                    S.op("act", lambda: A.activation(out=abr[:, sl, :], in_=abr[:, sl, :], func=AF.Identity, scale=0.0,
                                                     bias=tS[:, 2, g:g + 1]), reads=[B("St")], writes=[B("abr%d" % sl)])
                    S.op("dve", lambda: V.tensor_tensor_scan(out=xm[:, 1, :], data0=abr[:, sl, :], data1=xm[:, 0, :],
                                                             initial=state[:, g:g + 1], op0=ALU.mult, op1=ALU.add),
                         reads=[B("abr%d" % sl), B("xm0"), B("state")], writes=[B("xm1")])
                    S.op("act", lambda: A.copy(slast[:, g:g + 1], xm[:, 1, TT - 1:TT]), reads=[B("xm1")], writes=[B("slast")])
                    S.op("dve", lambda: V.tensor_tensor(out=d12[:, sl, 0, :], in0=xm[:, 1, :], in1=tab[:, sl, 0, :], op=ALU.mult),
                         reads=[B("xm1"), tb_], writes=[B("d12%d" % sl)])
                    S.op("dve", lambda: V.tensor_tensor(out=d12[:, sl, 1, :], in0=xm[:, 1, :], in1=tab[:, sl, 1, :], op=ALU.mult),
                         reads=[B("xm1"), tb_], writes=[B("d12%d" % sl)])
                    if v == 0:
                        cur = PSL()
                    pyb, py = cur

                    def mm_y():
                        nc.tensor.matmul(py[r0:r0 + 64, :], C1[:, g, :], d12[:, sl, 0, :], start=(v == 0), stop=False)
                        return nc.tensor.matmul(py[r0:r0 + 64, :], C2[:, g, :], d12[:, sl, 1, :], start=False, stop=(v == 3))
                    S.op("pe", mm_y, reads=[B("C1"), B("C2"), B("d12%d" % sl)], writes=[pyb])
                    if v == 3:
                        S.op("dve", lambda: V.scalar_tensor_tensor(out=sinT[r0:r0 + 64, j, :], in0=sinF[r0:r0 + 64, j, :],
                                                                   scalar=dsk[r0:r0 + 64, j:j + 1], in1=py[r0:r0 + 64, :],
                                                                   op0=ALU.mult, op1=ALU.add),
                             reads=[B("sinF"), B("dsk"), pyb], writes=[B("sinT")])
                pb, pa = PS()
                S.op("pe", lambda: nc.tensor.matmul(pa[:, 0:64], swapf[:], slast[:], start=True, stop=True), reads=[B("slast"), B("swapf")],
                     writes=[pb])
                vop(lambda: V.tensor_tensor(out=state[:], in0=slast[:], in1=tS[:, 4, :], op=ALU.mult), ["slast", "St"], ["state"])
                S.op("dve", lambda: V.tensor_tensor(out=slast[:], in0=pa[:, 0:64], in1=tS[:, 5, :], op=ALU.mult), reads=[pb, B("St")],
                     writes=[B("slast")])
                vop(lambda: V.tensor_tensor(out=state[:], in0=state[:], in1=slast[:], op=ALU.add), ["slast"], ["state"])
            if dbg:
                dump(0, qT, t)
                dump(1, mqT, t)
                dump(2, sinT, t)
            if stop == 5:
                return True
            wglu = W["w_ssm_glu"][l].rearrange("(j p) n -> p j n", p=128)
            wsw = W["w_swa_up"][l].rearrange("(j p) n -> p j n", p=128)
            wmu = W["w_mem_up"][l].rearrange("(j p) n -> p j n", p=128)
            with scope(("mrg", [128, KD, TT], BF16), ("gsb", [128, 2, TT], F32), ("sg2", [128, 2, TT], F32)) as (mrg, gsb, sg2):
                def up(wview, c0, src, srcbuf):
                    i, wb = wslot_next()
                    S.dma("pool", wgu[:, i, 0:8, :], wview[:, :, c0:c0 + 128], writes=[wb])
                    pb, pa = PS()

                    def mm():
                        last = None
                        for k in range(8):
                            last = nc.tensor.matmul(pa, wgu[:, i, k, :], src[:, k, :], start=(k == 0), stop=(k == 7))
                        return last
                    S.op("pe", mm, reads=[wb, srcbuf], writes=[pb])
                    return pb, pa

                def gate(bidx, i):
                    wb, wa = load_w(colview(wl, 3584 + bidx * D + i * 128))
                    pb, pa = PS()

                    def mm():
                        last = None
                        for k in range(KD):
                            last = nc.tensor.matmul(pa, wa[:, k, :], uT[:, k, :], start=(k == 0), stop=(k == KD - 1))
                        return last
                    S.op("pe", mm, reads=[wb, B("uT")], writes=[pb])
                    S.op("act", lambda: A.activation(out=gsb[:, 0, :], in_=pa, func=AF.Sigmoid), reads=[pb], writes=[B("gsb0")])

                for i in range(KD):
                    gate(0, i)
                    pb, pa = up(wsw, i * 128, qT, B("qT"))
                    S.op("dve", lambda: V.tensor_tensor(out=gsb[:, 1, :], in0=pa, in1=gsb[:, 0, :], op=ALU.mult),
                         reads=[pb, B("gsb0")], writes=[B("gsb1")])
                    gate(1, i)
                    pb2, pa2 = up(wglu, 2048 + i * 128, sinT, B("sinT"))
                    S.op("act", lambda: A.activation(out=sg2[:, 0, :], in_=pa2, func=AF.Sigmoid), reads=[pb2], writes=[B("sg0")])
                    S.op("dve", lambda: V.tensor_tensor(out=sg2[:, 0, :], in0=sg2[:, 0, :], in1=gsb[:, 0, :], op=ALU.mult),
                         reads=[B("sg0"), B("gsb0")], writes=[B("sg0")])
                    pb3, pa3 = up(wglu, i * 128, sinT, B("sinT"))
                    S.op("dve", lambda: V.tensor_tensor(out=sg2[:, 0, :], in0=pa3, in1=sg2[:, 0, :], op=ALU.mult),
                         reads=[pb3, B("sg0")], writes=[B("sg0")])
                    S.op("dve", lambda: V.tensor_tensor(out=gsb[:, 1, :], in0=gsb[:, 1, :], in1=sg2[:, 0, :], op=ALU.add),
                         reads=[B("sg0"), B("gsb1")], writes=[B("gsb1")])
                    gate(2, i)
                    pb4, pa4 = up(wmu, i * 128, mqT, B("mqT"))
                    S.op("dve", lambda: V.tensor_tensor(out=sg2[:, 1, :], in0=pa4, in1=gsb[:, 0, :], op=ALU.mult),
                         reads=[pb4, B("gsb0")], writes=[B("sg1")])
                    S.op("dve", lambda: V.tensor_tensor(out=mrg[:, i, :], in0=gsb[:, 1, :], in1=sg2[:, 1, :], op=ALU.add),
                         reads=[B("sg1"), B("gsb1")], writes=[B("mrg")])
                wov = W["w_out"][l]
                for i in range(KD):
                    wb, wa = load_w(colview(wov, i * 128))
                    pb, pa = PS()

                    def mm():
                        last = None
                        for k in range(KD):
                            last = nc.tensor.matmul(pa, wa[:, k, :], mrg[:, k, :], start=(k == 0), stop=(k == KD - 1))
                        return last
                    S.op("pe", mm, reads=[wb, B("mrg")], writes=[pb])
                    S.op("dve", lambda: V.tensor_tensor(out=hT[:, i, :], in0=pa, in1=hT[:, i, :], op=ALU.add), reads=[pb],
                         writes=[B("hT")])

    def emit_all():
        for l in range(NL):
            for t in range(NT):
                src = xT if l == 0 else hs.ap()
                S.dma("sp", hT[:], src[:, t * TT:(t + 1) * TT].rearrange("(kt p) n -> p kt n", p=128), writes=[B("hT")],
                      reads=[B("hs")] if l > 0 else [])
                if stop == 0:
                    raise _Stop()
                if t == 0:
                    mem_prep(l)
                    if stop == 1:
                        raise _Stop()
                    ssm_prep(l)
                    if stop == 2:
                        raise _Stop()
                ffn(l, "ffn1_w_in", "ffn1_w_out", 0)
                if stop == 3:
                    raise _Stop()
                if mixer(l, t, seq_start=(t == 0)):
                    raise _Stop()
                ffn(l, "ffn2_w_in", "ffn2_w_out", 3)
                if l < NL - 1:
                    S.dma("sp", hs.ap()[:, t * TT:(t + 1) * TT].rearrange("(kt p) n -> p kt n", p=128), hT[:], reads=[B("hT")],
                          writes=[B("hs")], owner=B("hs"))
                else:
                    with scope(("outF", [128, KD, TT], F32)) as (outF,):
                        rmsnorm(fnrm, B("fnrm"), hT, B("hT"), outF, B("outF"))
                        S.dma("sp", outT[:, t * TT:(t + 1) * TT].rearrange("(kt p) n -> p kt n", p=128), outF[:], reads=[B("outF")],
                              writes=[B("outT")], owner=B("outT"))
    try:
        emit_all()
    except _Stop:
        S.barrier()
        S.dma("sp", outT[:, 0:TT].rearrange("(kt p) n -> p kt n", p=128), hT[:], reads=[B("hT")], writes=[B("outT")], owner=B("outT"))
    S.wait_everything("sp")
    es.close()
    return nc


def _lay_kt(v):
    v = np.asarray(v, np.float32)
    lead = v.shape[:-1]
    return np.ascontiguousarray(np.moveaxis(v.reshape(*lead, KD, 128), -1, 0))


def host_small(NL, ffn1_norm, mix_norm, mem_norm, ffn2_norm, final_norm, sinks, lam_re, lam_im, log_dt, b_re, b_im, c_re, c_im,
               d_skip):
    norms = np.stack([_lay_kt(ffn1_norm), _lay_kt(mix_norm), _lay_kt(mem_norm), _lay_kt(ffn2_norm)], axis=2)
    fnorm = _lay_kt(final_norm)
    p = np.arange(128)
    sk = np.asarray(sinks, np.float32)
    sinkc = np.stack([sk[:, 4 * (i // 2) + 2 * (i % 2) + (p // 64)] for i in range(8)], axis=-1).transpose(1, 0, 2)
    P_, CH = 64, 16

    def blay(a):
        a = np.asarray(a, np.float32).reshape(NL, 8, 8, P_)
        a = np.repeat(a[:, :, :, None, :], CH, axis=3)
        return a.transpose(2, 3, 0, 1, 4).reshape(128, NL, 8, P_)

    def bmat(a):
        a = np.asarray(a, np.float32).reshape(NL, 8, 8, P_, CH)
        return a.transpose(2, 4, 0, 1, 3).reshape(128, NL, 8, P_)
    ldt = np.repeat(np.asarray(log_dt, np.float32)[:, :, None], P_, axis=2)
    ssmB = np.stack([blay(lam_re), blay(lam_im), blay(ldt), bmat(b_re), bmat(b_im)], axis=2)

    def slay(a):
        a = np.asarray(a, np.float32).transpose(2, 0, 1)
        return np.concatenate([a, a], axis=0)
    ssmS = np.stack([slay(lam_re), slay(lam_im), slay(ldt)], axis=2)
    cr = np.asarray(c_re, np.float32).transpose(3, 0, 1, 2)
    ci = np.asarray(c_im, np.float32).transpose(3, 0, 1, 2)
    ssmC = np.stack([np.concatenate([cr, ci], 0), np.concatenate([ci, cr], 0)], axis=2)
    dsk = np.ascontiguousarray(np.asarray(d_skip, np.float32).reshape(NL, 8, 128).transpose(2, 0, 1))
    consts = np.zeros((128, NCONST), np.float32)
    kk = np.arange(128)[:, None]
    qq = np.arange(128)[None, :]
    mprev = np.where(kk > qq, 0.0, -30000.0).astype(np.float32)
    mcur = np.where(kk <= qq, 0.0, -30000.0).astype(np.float32)
    neg = np.full((128, 128), -30000.0, np.float32)
    consts[:, 0:512] = np.concatenate([mprev, mprev, mcur, mcur], 1)
    consts[:, 512:1024] = np.concatenate([neg, neg, mcur, mcur], 1)
    consts[:, IOTA:IOTA + TT] = np.arange(1, TT + 1, dtype=np.float32)[None, :]
    consts[:, IDENT:IDENT + 128] = np.eye(128, dtype=np.float32)
    for v in range(4):
        consts[:, QM + v] = ((p // 16) % 4 == v).astype(np.float32)
    consts[:, QM + 4] = np.where(p < 64, -1.0, 1.0)
    consts[:, QM + 5] = np.where(p < 64, 1.0, -1.0)
    consts[p, SWAPP + (p + 64) % 128] = 1.0
    return {"norms": np.ascontiguousarray(norms), "fnorm": fnorm, "sinkc": np.ascontiguousarray(sinkc),
            "ssmB": np.ascontiguousarray(ssmB), "ssmS": np.ascontiguousarray(ssmS), "ssmC": np.ascontiguousarray(ssmC),
            "dskip": dsk, "consts": consts}


def kernel(x, mem, ffn1_norm, ffn1_w_in, ffn1_w_out, mix_norm, mem_norm, w_in, sinks, w_mem_kv, lam_re, lam_im, log_dt,
           b_re, b_im, c_re, c_im, d_skip, w_ssm_glu, w_swa_up, w_mem_up, w_out, ffn2_norm, ffn2_w_in, ffn2_w_out, final_norm):
    x = np.asarray(x, np.float32)
    Bsz, L, _ = x.shape
    NL = np.asarray(ffn1_w_in).shape[0]
    NT = L // TT
    nc = build(NT, NL)
    small = host_small(NL, ffn1_norm, mix_norm, mem_norm, ffn2_norm, final_norm, sinks, lam_re, lam_im, log_dt, b_re, b_im,
                       c_re, c_im, d_skip)
    f = lambda a: np.asarray(a, np.float32)
    in_maps = []
    for b_ in range(Bsz):
        m = {"xT": np.ascontiguousarray(x[b_].T), "memT": np.ascontiguousarray(f(mem)[b_].T),
             "ffn1_w_in": f(ffn1_w_in), "ffn1_w_out": f(ffn1_w_out), "w_in": f(w_in), "w_mem_kv": f(w_mem_kv),
             "w_ssm_glu": f(w_ssm_glu), "w_swa_up": f(w_swa_up), "w_mem_up": f(w_mem_up), "w_out": f(w_out),
             "ffn2_w_in": f(ffn2_w_in), "ffn2_w_out": f(ffn2_w_out)}
        m.update(small)
        in_maps.append(m)
    res = run_bass_kernel_spmd(nc, in_maps, core_ids=list(range(Bsz)))
    return np.stack([res.results[b_]["outT"].T for b_ in range(Bsz)], axis=0).astype(np.float32)
```

```python
import contextlib
import math
import numpy as np
import concourse.bass as bass
import concourse.mybir as mybir
from concourse.bass_utils import run_bass_kernel_spmd

F32 = mybir.dt.float32
BF16 = mybir.dt.bfloat16
AF = mybir.ActivationFunctionType
ALU = mybir.AluOpType

D = 2048
KD = 16
DFF = 5632
NFF = 44
TT = 512
NMEM = 256
INW = 9728
EPS = 1e-5
MAGIC = 12582912.0
NCONST = 1024 + 512 + 128 + 8 + 128
MASK_ROW, MASK_FIRST, IOTA, IDENT, QM, SWAPP = 0, 512, 1024, 1536, 1664, 1672
SEM_ROLL = 30000


class Buf:
    __slots__ = ("w", "r", "dsem", "dcnt", "dkey", "excl")

    def __init__(self):
        self.excl = False
        self.w = None
        self.r = []
        self.dsem = None
        self.dcnt = 0
        self.dkey = None


class Sched:
    def __init__(self, nc):
        self.nc = nc
        self.eng = {"pe": nc.tensor, "act": nc.scalar, "dve": nc.vector, "pool": nc.gpsimd, "sp": nc.sync}
        self.sem = {}
        self.key = {}
        self.cnt = {}
        self.nsem = 0
        for k in ("pe", "act", "dve", "pool"):
            self._roll(k)
        self.waited = {k: {} for k in self.eng}
        self.bufs = []
        self.old = []

    def _newsem(self):
        self.nsem += 1
        return self.nc.alloc_semaphore("sm%d" % self.nsem), "K%d" % self.nsem

    def _roll(self, k):
        if k in self.sem:
            self.old.append((self.key[k], self.sem[k], self.cnt[k]))
        self.sem[k], self.key[k] = self._newsem()
        self.cnt[k] = 0

    def buf(self):
        b = Buf()
        self.bufs.append(b)
        return b

    def _need(self, e, toks):
        eng = self.eng[e]
        best = {}
        for t in toks:
            if t is None:
                continue
            key, sem, val = t
            if e == "pe" and key == self.key.get("pe"):
                continue
            if best.get(key, (None, 0))[1] < val:
                best[key] = (sem, val)
        for key, (sem, val) in best.items():
            if self.waited[e].get(key, 0) < val:
                eng.wait_ge(sem, val)
                self.waited[e][key] = val

    @staticmethod
    def _deps(reads, writes):
        toks = []
        for b in reads:
            toks.append(b.w)
            if b.excl:
                toks.extend(b.r)
        for b in writes:
            toks.append(b.w)
            toks.extend(b.r)
        return toks

    def op(self, e, fn, reads=(), writes=()):
        self._need(e, self._deps(reads, writes))
        if self.cnt[e] >= SEM_ROLL:
            self._roll(e)
        inst = fn()
        self.cnt[e] += 1
        inst.then_inc(self.sem[e], 1)
        tok = (self.key[e], self.sem[e], self.cnt[e])
        for b in writes:
            b.w = tok
            b.r = []
        for b in reads:
            b.r.append(tok)
            if len(b.r) > 64:
                b.r = self._compact(b.r)
        return tok

    @staticmethod
    def _compact(toks):
        best = {}
        for key, sem, val in toks:
            if best.get(key, (None, 0))[1] < val:
                best[key] = (sem, val)
        return [(k, s, v) for k, (s, v) in best.items()]

    def dma(self, e, out, in_, reads=(), writes=(), owner=None, **kw):
        self._need(e, self._deps(reads, writes))
        b = owner if owner is not None else (writes[0] if writes else reads[0])
        if b.dsem is None or b.dcnt + 16 > SEM_ROLL:
            if b.dsem is not None:
                self.old.append((b.dkey, b.dsem, b.dcnt))
            b.dsem, b.dkey = self._newsem()
            b.dcnt = 0
        b.dcnt += 16
        self.eng[e].dma_start(out=out, in_=in_, **kw).then_inc(b.dsem, 16)
        tok = (b.dkey, b.dsem, b.dcnt)
        for w in writes:
            w.w = tok
            w.r = []
        for r in reads:
            r.r.append(tok)
            if len(r.r) > 64:
                r.r = self._compact(r.r)
        return tok

    def all_tokens(self):
        toks = [(self.key[k], self.sem[k], self.cnt[k]) for k in self.sem if self.cnt[k] > 0]
        toks += [t for t in self.old if t[2] > 0]
        for b in self.bufs:
            if b.dsem is not None and b.dcnt > 0:
                toks.append((b.dkey, b.dsem, b.dcnt))
        return toks

    def barrier(self):
        toks = self.all_tokens()
        for e in self.eng:
            self._need(e, toks)

    def wait_everything(self, e):
        self._need(e, self.all_tokens())


class _Stop(Exception):
    pass


def build(NT, NL, dbg=False, stop=99):
    LT = NT * TT
    nc = bass.Bass("TRN2", target_bir_lowering=False)
    S = Sched(nc)

    def din(name, shape, dt=F32):
        return nc.dram_tensor(name, list(shape), dt, kind="ExternalInput").ap()

    xT = din("xT", [D, LT])
    memT = din("memT", [D, NMEM])
    outT = nc.dram_tensor("outT", [D, LT], F32, kind="ExternalOutput").ap()
    if dbg:
        dbg_o = nc.dram_tensor("dbg_o", [3, 1024, LT], F32, kind="ExternalOutput").ap()
    W = {}
    for nm, shp in [("ffn1_w_in", [NL, 88, 128, 2048]), ("ffn1_w_out", [NL, DFF, D]), ("w_in", [NL, 76, 128, 2048]),
                    ("w_mem_kv", [NL, 16, 128, 2048]), ("w_ssm_glu", [NL, 32, 128, 1024]), ("w_swa_up", [NL, 16, 128, 1024]),
                    ("w_mem_up", [NL, 16, 128, 1024]), ("w_out", [NL, 16, 128, 2048]), ("ffn2_w_in", [NL, 88, 128, 2048]),
                    ("ffn2_w_out", [NL, DFF, D])]:
        W[nm] = din(nm, shp)
    norms = din("norms", [128, NL, 4, KD])
    fnorm = din("fnorm", [128, KD])
    sinkc = din("sinkc", [128, NL, 8])
    ssmB = din("ssmB", [128, NL, 5, 8, 64])
    ssmS = din("ssmS", [128, NL, 3, 64])
    ssmC = din("ssmC", [128, NL, 2, 64, 16])
    dskip = din("dskip", [128, NL, 8])
    consts = din("consts", [128, NCONST])
    hs = nc.dram_tensor("hs", [D, LT], F32)
    tabs = nc.dram_tensor("tabs", [64, 128, 2, TT], F32)

    es = contextlib.ExitStack()

    def sb(name, shape, dt):
        return es.enter_context(nc.sbuf_tensor(name, list(shape), dt))

    hT = sb("hT", [128, KD, TT], F32)
    uT = sb("uT", [128, KD, TT], BF16)
    cst = sb("cst", [128, NCONST - 1024], F32)
    maskb = sb("maskb", [128, 1024], BF16)
    identb = sb("identb", [128, 128], BF16)
    onesb = sb("onesb", [128, 192], BF16)
    onesf_t = sb("onesf", [128, 128], BF16)
    onesf = onesf_t[:]
    epsc = sb("epsc", [128, 1], F32)
    nrm = sb("nrm", [128, NL, 4, KD], F32)
    fnrm = sb("fnrm", [128, KD], F32)
    skc = sb("skc", [128, NL, 8], F32)
    esk = sb("esk", [128, 8], F32)
    sq = sb("sq", [128, 2, TT], BF16)
    rs = sb("rs", [128, TT], F32)
    wgu = sb("wgu", [128, 4, KD, 128], BF16)
    kT2 = sb("kT2", [128, 2, 4, 128 + TT], BF16)
    vaug = sb("vaug", [128, 5, 4, 192], BF16)
    mkT = sb("mkT", [128, 8, NMEM], BF16)
    mv = sb("mv", [128, 2, 1024], BF16)
    BA = sb("BA", [128, 8, 4, 128], BF16)
    BB = sb("BB", [128, 8, 4, 128], BF16)
    C1 = sb("C1", [128, 64, 64], BF16)
    C2 = sb("C2", [128, 64, 64], BF16)
    prmS = sb("prmS", [128, 3, 64], F32)
    tS = sb("tS", [128, 10, 64], F32)
    dsk = sb("dsk", [128, 8], F32)
    state = sb("state", [128, 64], F32)
    slast = sb("slast", [128, 64], F32)
    swapf = sb("swapf", [128, 128], F32)
    psum = es.enter_context(nc.psum_tensor("psum", [128, 8, TT], F32))

    b = {}

    def B(name):
        if name not in b:
            b[name] = S.buf()
        return b[name]

    psb = [S.buf() for _ in range(8)]
    for _b in psb:
        _b.excl = True
    psi = [0]
    psl = [0]

    def PS():
        i = psi[0] % 6
        psi[0] += 1
        return psb[i], psum[:, i, :]

    def PSL():
        i = 6 + psl[0] % 2
        psl[0] += 1
        return psb[i], psum[:, i, :]

    @contextlib.contextmanager
    def scope(*specs):
        with contextlib.ExitStack() as st:
            scope.n += 1
            ts = [st.enter_context(nc.sbuf_tensor("%s_%d" % (n, scope.n), list(s), d)) for n, s, d in specs]
            yield ts
            S.barrier()

    scope.n = 0
    V = nc.vector
    A = nc.scalar

    def vop(fn, reads, writes, e="dve"):
        S.op(e, fn, reads=[B(x) if isinstance(x, str) else x for x in reads],
             writes=[B(x) if isinstance(x, str) else x for x in writes])

    S.dma("sp", cst[:], consts[:, 1024:NCONST], writes=[B("cst")])
    S.dma("pool", maskb[:], consts[:, 0:1024], writes=[B("maskb")])
    S.dma("sp", nrm[:], norms[:, :, :, :], writes=[B("nrm")])
    S.dma("sp", fnrm[:], fnorm[:, :], writes=[B("fnrm")])
    S.dma("sp", skc[:], sinkc[:, :, :], writes=[B("skc")])
    S.dma("sp", swapf[:], consts[:, SWAPP:SWAPP + 128], writes=[B("swapf")])
    vop(lambda: V.tensor_copy(identb[:], cst[:, IDENT - 1024:IDENT - 1024 + 128]), ["cst"], ["identb"])
    vop(lambda: V.memset(onesb[:], 1.0), [], ["onesb"])
    vop(lambda: V.memset(onesb[:, 64:128], 0.0), [], ["onesb"])
    vop(lambda: V.memset(onesf_t[:], 1.0), [], ["onesf"])
    vop(lambda: V.memset(epsc[:], EPS), [], ["epsc"])
    vop(lambda: V.memset(vaug[:], 0.0), [], ["vaug"])
    vop(lambda: V.memset(kT2[:], 0.0), [], ["kT2"])
    vop(lambda: V.memset(BA[:], 0.0), [], ["BA"])
    vop(lambda: V.memset(BB[:], 0.0), [], ["BB"])
    vop(lambda: V.memset(C1[:], 0.0), [], ["C1"])
    vop(lambda: V.memset(C2[:], 0.0), [], ["C2"])

    wslot = [0]

    def wslot_next():
        i = wslot[0] % 4
        wslot[0] += 1
        return i, B("wgu%d" % i)

    def load_w(dram_view):
        i, wb = wslot_next()
        S.dma("pool", wgu[:, i, :, :], dram_view, writes=[wb])
        return wb, wgu[:, i, :, :]

    def colview(wl, c0, n=128):
        assert c0 % 128 == 0 and n == 128
        return wl[c0 // 128].rearrange("p (kt c) -> p kt c", c=128)

    def rmsnorm(gain_ap, gbuf, src, srcbuf, dst, dstbuf, nk=KD, ncol=TT):
        pb, pa = PS()
        for k in range(nk):
            sl = k % 2
            S.op("act", lambda: A.activation(out=sq[:, sl, 0:ncol], in_=src[:, k, :], func=AF.Square),
                 reads=[srcbuf], writes=[B("sq%d" % sl)])
            S.op("pe", lambda: nc.tensor.matmul(pa[:, 0:ncol], onesf, sq[:, sl, 0:ncol], start=(k == 0), stop=(k == nk - 1)),
                 reads=[B("sq%d" % sl), B("onesf")], writes=[pb])
        S.op("act", lambda: A.activation(out=rs[:, 0:ncol], in_=pa[:, 0:ncol], func=AF.Sqrt, bias=epsc[:, 0:1],
                                         scale=1.0 / D), reads=[pb, B("epsc")], writes=[B("rs")])
        S.op("dve", lambda: V.reciprocal(rs[:, 0:ncol], rs[:, 0:ncol]), reads=[B("rs")], writes=[B("rs")])
        for k in range(nk):
            S.op("dve", lambda: V.scalar_tensor_tensor(out=dst[:, k, :], in0=src[:, k, :], scalar=gain_ap[:, k:k + 1],
                                                       in1=rs[:, 0:ncol], op0=ALU.mult, op1=ALU.mult),
                 reads=[srcbuf, B("rs"), gbuf], writes=[dstbuf])

    def ffn(l, w_in_name, w_out_name, which):
        win = W[w_in_name][l]
        wout = W[w_out_name][l].rearrange("(j p) n -> p j n", p=128)
        rmsnorm(nrm[:, l, which, :], B("nrm"), hT, B("hT"), uT, B("uT"))
        with scope(("hid", [128, 2, 4, TT], BF16), ("sg", [128, 2, TT], F32), ("wo", [128, 2, 4, D], BF16)) as (hid, sg, wo):
            for c in range(NFF // 4):
                hs_ = c % 2
                wob = B("wo%d" % hs_)
                S.dma("pool", wo[:, hs_, :, :], wout[:, 4 * c:4 * c + 4, :], writes=[wob])
                woa = wo[:, hs_, :, :]
                for jj in range(4):
                    j = 4 * c + jj
                    wgb, wga = load_w(colview(win, j * 128))
                    wub, wua = load_w(colview(win, DFF + j * 128))
                    pgb, pg = PS()
                    pub, pu = PS()

                    def mm_g():
                        last = None
                        for k in range(KD):
                            last = nc.tensor.matmul(pg, wga[:, k, :], uT[:, k, :], start=(k == 0), stop=(k == KD - 1))
                        return last

                    def mm_u():
                        last = None
                        for k in range(KD):
                            last = nc.tensor.matmul(pu, wua[:, k, :], uT[:, k, :], start=(k == 0), stop=(k == KD - 1))
                        return last
                    S.op("pe", mm_g, reads=[wgb, B("uT")], writes=[pgb])
                    S.op("pe", mm_u, reads=[wub, B("uT")], writes=[pub])
                    sl = j % 2
                    S.op("act", lambda: A.activation(out=sg[:, sl, :], in_=pg, func=AF.Silu), reads=[pgb], writes=[B("sg%d" % sl)])
                    S.op("dve", lambda: V.tensor_tensor(out=hid[:, hs_, jj, :], in0=pu, in1=sg[:, sl, :], op=ALU.mult),
                         reads=[pub, B("sg%d" % sl)], writes=[B("hid%d" % hs_)])
                for i in range(KD):
                    pob, po = PS()

                    def mm_o():
                        last = None
                        for jj in range(4):
                            last = nc.tensor.matmul(po, woa[:, jj, i * 128:(i + 1) * 128], hid[:, hs_, jj, :], start=(jj == 0),
                                                    stop=(jj == 3))
                        return last
                    S.op("pe", mm_o, reads=[wob, B("hid%d" % hs_)], writes=[pob])
                    S.op("dve", lambda: V.scalar_tensor_tensor(out=hT[:, i, :], in0=po, scalar=0.5, in1=hT[:, i, :],
                                                               op0=ALU.mult, op1=ALU.add), reads=[pob], writes=[B("hT")])

    def mem_prep(l):
        with scope(("memf", [128, KD, NMEM], F32), ("memn", [128, KD, NMEM], BF16)) as (memf, memn):
            S.dma("sp", memf[:], memT.rearrange("(kt p) n -> p kt n", p=128), writes=[B("memf")])
            rmsnorm(nrm[:, l, 2, :], B("nrm"), memf, B("memf"), memn, B("memn"), ncol=NMEM)
            wk = W["w_mem_kv"][l]
            for i in range(8):
                wb, wa = load_w(colview(wk, i * 128))
                pb, pa = PS()

                def mm():
                    last = None
                    for k in range(KD):
                        last = nc.tensor.matmul(pa[:, 0:NMEM], wa[:, k, :], memn[:, k, :], start=(k == 0), stop=(k == KD - 1))
                    return last
                S.op("pe", mm, reads=[wb, B("memn")], writes=[pb])
                S.op("act", lambda: A.copy(mkT[:, i, :], pa[:, 0:NMEM]), reads=[pb], writes=[B("mkT")])
            for i in range(8):
                wb, wa = load_w(colview(wk, 1024 + i * 128))
                for mt in range(2):
                    pb, pa = PS()

                    def mm():
                        last = None
                        for k in range(KD):
                            last = nc.tensor.matmul(pa[:, 0:128], memn[:, k, mt * 128:(mt + 1) * 128], wa[:, k, :], start=(k == 0),
                                                    stop=(k == KD - 1))
                        return last
                    S.op("pe", mm, reads=[wb, B("memn")], writes=[pb])
                    S.op("dve", lambda: V.tensor_copy(mv[:, mt, i * 128:(i + 1) * 128], pa[:, 0:128]), reads=[pb], writes=[B("mv")])

    def disc(prm, tt, R, Wt):
        vop(lambda: V.tensor_scalar_min(out=tt[0], in0=prm[0], scalar1=-1e-4), R, Wt)
        vop(lambda: A.activation(out=tt[1], in_=prm[2], func=AF.Exp), R, Wt, e="act")
        vop(lambda: V.tensor_tensor(out=tt[8], in0=tt[0], in1=tt[1], op=ALU.mult), Wt, Wt)
        vop(lambda: A.activation(out=tt[2], in_=tt[8], func=AF.Exp), Wt, Wt, e="act")
        vop(lambda: V.scalar_tensor_tensor(out=tt[3], in0=prm[1], scalar=1.0 / (2 * math.pi), in1=tt[1], op0=ALU.mult,
                                           op1=ALU.mult), R + Wt, Wt)
        sincos(tt[3], tt[5], tt[4], tt[8], tt[9], Wt)

    def sincos(f, s_out, c_out, t1, t2, Wt):
        vop(lambda: V.tensor_scalar(out=t1, in0=f, scalar1=MAGIC, scalar2=MAGIC, op0=ALU.add, op1=ALU.subtract), Wt, Wt)
        vop(lambda: V.tensor_tensor(out=t1, in0=f, in1=t1, op=ALU.subtract), Wt, Wt)
        vop(lambda: A.activation(out=s_out, in_=t1, func=AF.Sin, scale=2 * math.pi), Wt, Wt, e="act")
        vop(lambda: V.tensor_scalar_add(out=t2, in0=f, scalar1=0.25), Wt, Wt)
        vop(lambda: V.tensor_scalar(out=t1, in0=t2, scalar1=MAGIC, scalar2=MAGIC, op0=ALU.add, op1=ALU.subtract), Wt, Wt)
        vop(lambda: V.tensor_tensor(out=t1, in0=t2, in1=t1, op=ALU.subtract), Wt, Wt)
        vop(lambda: A.activation(out=c_out, in_=t1, func=AF.Sin, scale=2 * math.pi), Wt, Wt, e="act")

    def ssm_prep(l):
        with scope(("prmB", [128, 5, 8, 64], F32), ("tB", [128, 10, 8, 64], F32), ("cc", [128, 2, 64, 16], F32),
                   ("targ", [128, 2, TT], F32), ("tabw", [128, 2, 2, TT], F32)) as (prmB, tB, cc, targ, tabw):
            S.dma("sp", prmB[:], ssmB[:, l, :, :, :], writes=[B("Bp")])
            S.dma("sp", prmS[:], ssmS[:, l, :, :], writes=[B("Sp")])
            S.dma("sp", cc[:], ssmC[:, l, :, :, :], writes=[B("cc")])
            S.dma("sp", dsk[:], dskip[:, l, :], writes=[B("dsk")])
            pB = [prmB[:, i, :, :] for i in range(5)]
            tb = [tB[:, i, :, :] for i in range(10)]
            R, Wt = ["Bp", "Bt"], ["Bt"]
            disc(pB, tb, ["Bp"], ["Bt"])
            vop(lambda: V.tensor_tensor(out=tb[4], in0=tb[4], in1=tb[2], op=ALU.mult), Wt, Wt)
            vop(lambda: V.tensor_tensor(out=tb[5], in0=tb[5], in1=tb[2], op=ALU.mult), Wt, Wt)
            vop(lambda: V.tensor_scalar_add(out=tb[4], in0=tb[4], scalar1=-1.0), Wt, Wt)
            vop(lambda: V.tensor_tensor(out=tb[8], in0=tb[0], in1=tb[0], op=ALU.mult), Wt, Wt)
            vop(lambda: V.tensor_tensor(out=tb[9], in0=pB[1], in1=pB[1], op=ALU.mult), R, Wt)
            vop(lambda: V.tensor_tensor(out=tb[8], in0=tb[8], in1=tb[9], op=ALU.add), Wt, Wt)
            vop(lambda: V.reciprocal(tb[8], tb[8]), Wt, Wt)
            vop(lambda: V.tensor_tensor(out=tb[6], in0=tb[4], in1=tb[0], op=ALU.mult), Wt, Wt)
            vop(lambda: V.tensor_tensor(out=tb[9], in0=tb[5], in1=pB[1], op=ALU.mult), R, Wt)
            vop(lambda: V.tensor_tensor(out=tb[6], in0=tb[6], in1=tb[9], op=ALU.add), Wt, Wt)
            vop(lambda: V.tensor_tensor(out=tb[6], in0=tb[6], in1=tb[8], op=ALU.mult), Wt, Wt)
            vop(lambda: V.tensor_tensor(out=tb[7], in0=tb[5], in1=tb[0], op=ALU.mult), Wt, Wt)
            vop(lambda: V.tensor_tensor(out=tb[9], in0=tb[4], in1=pB[1], op=ALU.mult), R, Wt)
            vop(lambda: V.tensor_tensor(out=tb[7], in0=tb[7], in1=tb[9], op=ALU.subtract), Wt, Wt)
            vop(lambda: V.tensor_tensor(out=tb[7], in0=tb[7], in1=tb[8], op=ALU.mult), Wt, Wt)
            vop(lambda: V.tensor_tensor(out=tb[2], in0=tb[6], in1=pB[3], op=ALU.mult), R, Wt)
            vop(lambda: V.tensor_tensor(out=tb[9], in0=tb[7], in1=pB[4], op=ALU.mult), R, Wt)
            vop(lambda: V.tensor_tensor(out=tb[2], in0=tb[2], in1=tb[9], op=ALU.subtract), Wt, Wt)
            vop(lambda: V.tensor_tensor(out=tb[3], in0=tb[6], in1=pB[4], op=ALU.mult), R, Wt)
            vop(lambda: V.tensor_tensor(out=tb[9], in0=tb[7], in1=pB[3], op=ALU.mult), R, Wt)
            vop(lambda: V.tensor_tensor(out=tb[3], in0=tb[3], in1=tb[9], op=ALU.add), Wt, Wt)
            vop(lambda: V.tensor_scalar_mul(out=tb[9], in0=tb[2], scalar1=-1.0), Wt, Wt)
            for v in range(4):
                qm = cst[:, QM - 1024 + v:QM - 1024 + v + 1]
                vop(lambda: V.tensor_scalar(out=BA[:, :, v, 0:64], in0=tb[2], scalar1=qm, scalar2=None, op0=ALU.mult), ["Bt", "cst"], ["BA"])
                vop(lambda: V.tensor_scalar(out=BA[:, :, v, 64:128], in0=tb[3], scalar1=qm, scalar2=None, op0=ALU.mult), ["Bt", "cst"], ["BA"])
                vop(lambda: V.tensor_scalar(out=BB[:, :, v, 0:64], in0=tb[3], scalar1=qm, scalar2=None, op0=ALU.mult), ["Bt", "cst"], ["BB"])
                vop(lambda: V.tensor_scalar(out=BB[:, :, v, 64:128], in0=tb[9], scalar1=qm, scalar2=None, op0=ALU.mult), ["Bt", "cst"], ["BB"])
            pS = [prmS[:, i, :] for i in range(3)]
            ts_ = [tS[:, i, :] for i in range(10)]
            disc(pS, ts_, ["Sp"], ["St"])
            Wt = ["St"]
            vop(lambda: V.tensor_scalar_mul(out=ts_[6], in0=ts_[3], scalar1=float(TT)), Wt, Wt)
            sincos(ts_[6], ts_[5], ts_[4], ts_[8], ts_[9], Wt)
            vop(lambda: V.tensor_scalar(out=ts_[5], in0=ts_[5], scalar1=cst[:, QM - 1024 + 4:QM - 1024 + 5], scalar2=None, op0=ALU.mult), ["St", "cst"], Wt)
            for v in range(4):
                vop(lambda: V.tensor_scalar(out=C1[:, v::4, 16 * v:16 * v + 16], in0=cc[:, 0, v::4, :], scalar1=cst[:, QM - 1024 + 5:QM - 1024 + 6],
                                            scalar2=None, op0=ALU.mult), ["cc", "cst"], ["C1"])
                vop(lambda: V.tensor_scalar_mul(out=C2[:, v::4, 16 * v:16 * v + 16], in0=cc[:, 1, v::4, :], scalar1=-1.0), ["cc"], ["C2"])
            for g in range(64):
                sl = g % 2
                fcol = tS[:, 3, g:g + 1]
                tw = ["targ"]
                vop(lambda: V.tensor_scalar(out=targ[:, 0, :], in0=cst[:, IOTA - 1024:IOTA - 1024 + TT], scalar1=fcol, scalar2=None, op0=ALU.mult),
                    ["St", "cst", "targ"], tw)
                vop(lambda: V.tensor_scalar(out=targ[:, 1, :], in0=targ[:, 0, :], scalar1=MAGIC, scalar2=MAGIC, op0=ALU.add,
                                            op1=ALU.subtract), tw, tw)
                vop(lambda: V.tensor_tensor(out=targ[:, 1, :], in0=targ[:, 0, :], in1=targ[:, 1, :], op=ALU.subtract), tw, tw)
                vop(lambda: A.activation(out=tabw[:, sl, 1, :], in_=targ[:, 1, :], func=AF.Sin, scale=2 * math.pi), tw,
                    ["tabw%d" % sl], e="act")
                vop(lambda: V.tensor_scalar_add(out=targ[:, 0, :], in0=targ[:, 0, :], scalar1=0.25), tw, tw)
                vop(lambda: V.tensor_scalar(out=targ[:, 1, :], in0=targ[:, 0, :], scalar1=MAGIC, scalar2=MAGIC, op0=ALU.add,
                                            op1=ALU.subtract), tw, tw)
                vop(lambda: V.tensor_tensor(out=targ[:, 1, :], in0=targ[:, 0, :], in1=targ[:, 1, :], op=ALU.subtract), tw, tw)
                vop(lambda: A.activation(out=tabw[:, sl, 0, :], in_=targ[:, 1, :], func=AF.Sin, scale=2 * math.pi), tw,
                    ["tabw%d" % sl], e="act")
                S.dma("sp", tabs.ap()[g], tabw[:, sl, :, :], reads=[B("tabw%d" % sl)], writes=[B("tabs")], owner=B("tabs"))
            vop(lambda: V.memset(state[:], 0.0), [], ["state"])

    def proj_fm(wl, c0, dst_ap, dstbuf, evac, dst2=None):
        wb, wa = load_w(colview(wl, c0))
        pb, pa = PS()

        def mm():
            last = None
            for k in range(KD):
                last = nc.tensor.matmul(pa, wa[:, k, :], uT[:, k, :], start=(k == 0), stop=(k == KD - 1))
            return last
        S.op("pe", mm, reads=[wb, B("uT")], writes=[pb])
        if evac == "act":
            S.op("act", lambda: A.copy(dst_ap, pa), reads=[pb], writes=[dstbuf])
        else:
            S.op("dve", lambda: V.tensor_copy(dst_ap, pa), reads=[pb], writes=[dstbuf])
        if dst2 is not None:
            S.op("dve", lambda: V.tensor_scalar(out=dst2[0], in0=pa, scalar1=1.0, scalar2=None, op0=ALU.mult), reads=[pb], writes=[dst2[1]])

    def dump(idx, src, t):
        S.dma("pool", dbg_o[idx, :, t * TT:(t + 1) * TT].rearrange("(kt p) n -> p kt n", p=128), src[:],
              reads=[B("qT"), B("mqT"), B("sinT")], writes=[B("dbgo")], owner=B("dbgo"))

    def mixer(l, t, seq_start):
        wl = W["w_in"][l]
        rmsnorm(nrm[:, l, 1, :], B("nrm"), hT, B("hT"), uT, B("uT"))
        with scope(("qT", [128, 8, TT], BF16), ("mqT", [128, 8, TT], BF16), ("sinT", [128, 8, TT], BF16),
                   ("sinF", [128, 8, TT], F32)) as (qT, mqT, sinT, sinF):
            with scope(("pT", [128, 2, TT], BF16), ("rden", [128, 2, TT], F32), ("wk", [128, KD, 128], BF16)) as (pT, rden, wk):
                if stop == 39:
                    return True
                for i in range(8):
                    proj_fm(wl, i * 128, qT[:, i, :], B("qT"), "act")
                if stop == 38:
                    return True
                for i in range(8):
                    proj_fm(wl, 2560 + i * 128, mqT[:, i, :], B("mqT"), "dve")
                if stop == 37:
                    return True
                for i in range(8):
                    proj_fm(wl, 1536 + i * 128, sinT[:, i, :], B("sinT"), "act", dst2=(sinF[:, i, :], B("sinF")))
                if stop == 40:
                    return True
                for gp in range(2):
                    S.dma("pool", wk[:], colview(wl, 1024 + gp * 128), writes=[B("wk")])
                    for gg in range(2):
                        g = 2 * gp + gg
                        for x in range(2):
                            i, wb = wslot_next()
                            S.op("dve", lambda: V.memset(wgu[:, i, :, (1 - x) * 64:(1 - x) * 64 + 64], 0.0), writes=[wb])
                            S.op("dve", lambda: V.tensor_scalar(out=wgu[:, i, :, x * 64:x * 64 + 64], in0=wk[:, :, gg * 64:(gg + 1) * 64],
                                                                scalar1=1.0, scalar2=None, op0=ALU.mult), reads=[B("wk")], writes=[wb])
                            pb, pa = PS()

                            def mm():
                                last = None
                                for k in range(KD):
                                    last = nc.tensor.matmul(pa, wgu[:, i, k, :], uT[:, k, :], start=(k == 0), stop=(k == KD - 1))
                                return last
                            S.op("pe", mm, reads=[wb, B("uT")], writes=[pb])
                            S.op("act", lambda: A.copy(kT2[:, x, g, 128:128 + TT], pa), reads=[pb], writes=[B("kT2")])
                if stop == 41:
                    return True
                i0, wbv0 = wslot_next()
                S.dma("pool", wgu[:, i0, :, :], colview(wl, 1280, 128), writes=[wbv0])
                i1, wbv1 = wslot_next()
                S.dma("pool", wgu[:, i1, :, :], colview(wl, 1408, 128), writes=[wbv1])
                for bq in range(4):
                    pb, pa = PS()

                    def mm():
                        last = None
                        for half, wi in ((0, i0), (1, i1)):
                            for k in range(KD):
                                last = nc.tensor.matmul(pa[:, half * 128:(half + 1) * 128], uT[:, k, bq * 128:(bq + 1) * 128],
                                                        wgu[:, wi, k, :], start=(k == 0), stop=(k == KD - 1))
                        return last
                    S.op("pe", mm, reads=[wbv0, wbv1, B("uT")], writes=[pb])
                    pv = pa[:, 0:256].rearrange("p (g d) -> p g d", g=4)
                    S.op("act", lambda: A.copy(vaug[:, 1 + bq, :, 0:64], pv), reads=[pb], writes=[B("vaug")])
                    S.op("dve", lambda: V.tensor_copy(vaug[:, 1 + bq, :, 128:192], pv), reads=[pb], writes=[B("vaug")])
                if stop == 42:
                    return True
                S.op("act", lambda: A.activation(out=esk[:], in_=skc[:, l, :], func=AF.Exp), reads=[B("skc")], writes=[B("esk")])
                for bq in range(4):
                    for g in range(4):
                        for hp in range(2):
                            psb_, ps_ = PS()
                            m0 = MASK_FIRST if (seq_start and bq == 0) else MASK_ROW
                            qt = 2 * g + hp

                            def mm_s():
                                nc.tensor.matmul(ps_, identb[:], maskb[:, m0:m0 + 512], start=True, stop=False)
                                last = None
                                for kb in range(2):
                                    for r in range(2):
                                        c0 = kb * 256 + r * 128
                                        last = nc.tensor.matmul(ps_[:, c0:c0 + 128],
                                                                kT2[:, r, g, (bq + kb) * 128:(bq + kb + 1) * 128],
                                                                qT[:, qt, bq * 128:(bq + 1) * 128], start=False,
                                                                stop=(kb == 1 and r == 1))
                                return last
                            S.op("pe", mm_s, reads=[B("identb"), B("maskb"), B("kT2"), B("qT")], writes=[psb_])
                            sl = (bq * 8 + g * 2 + hp) % 2
                            S.op("act", lambda: A.activation(out=pT[:, sl, :], in_=ps_, func=AF.Exp, scale=0.125),
                                 reads=[psb_], writes=[B("pT%d" % sl)])
                            pob, po = PS()
                            pdb, pd = PS()

                            def mm_o():
                                n = 0
                                for kb in range(2):
                                    for r in range(2):
                                        lo = 0 if r == 0 else 64
                                        c0 = kb * 256 + r * 128
                                        nc.tensor.matmul(po[:, 0:128], vaug[:, bq + kb, g, lo:lo + 128], pT[:, sl, c0:c0 + 128],
                                                         start=(n == 0), stop=(n == 3))
                                        n += 1
                                n = 0
                                last = None
                                for kb in range(2):
                                    for r in range(2):
                                        lo = 0 if r == 0 else 64
                                        c0 = kb * 256 + r * 128
                                        last = nc.tensor.matmul(pd[:, 0:128], onesb[:, lo:lo + 128], pT[:, sl, c0:c0 + 128],
                                                                start=(n == 0), stop=(n == 3))
                                        n += 1
                                return last
                            S.op("pe", mm_o, reads=[B("vaug"), B("onesb"), B("pT%d" % sl)], writes=[pob, pdb])
                            idx = g * 2 + hp
                            S.op("dve", lambda: V.tensor_scalar(out=rden[:, sl, 0:128], in0=pd[:, 0:128], scalar1=esk[:, idx:idx + 1],
                                                                scalar2=None, op0=ALU.add), reads=[pdb, B("esk")],
                                 writes=[B("rden%d" % sl)])
                            S.op("dve", lambda: V.reciprocal(rden[:, sl, 0:128], rden[:, sl, 0:128]), reads=[B("rden%d" % sl)],
                                 writes=[B("rden%d" % sl)])
                            S.op("dve", lambda: V.tensor_tensor(out=qT[:, qt, bq * 128:(bq + 1) * 128], in0=po[:, 0:128],
                                                                in1=rden[:, sl, 0:128], op=ALU.mult),
                                 reads=[pob, B("rden%d" % sl)], writes=[B("qT")])
                if stop == 43:
                    return True
                vop(lambda: V.tensor_scalar(out=kT2[:, 0, :, 0:128], in0=kT2[:, 0, :, TT:TT + 128], scalar1=1.0, scalar2=None, op0=ALU.mult), ["kT2"], ["kT2"])
                vop(lambda: V.tensor_scalar(out=kT2[:, 1, :, 0:128], in0=kT2[:, 1, :, TT:TT + 128], scalar1=1.0, scalar2=None, op0=ALU.mult), ["kT2"], ["kT2"])
                vop(lambda: V.tensor_scalar(out=vaug[:, 0, :, :], in0=vaug[:, 4, :, :], scalar1=1.0, scalar2=None, op0=ALU.mult), ["vaug"], ["vaug"])
                for hm in range(4):
                    for mt in range(2):
                        pb, pa = PS()

                        def mm():
                            last = None
                            for dt_ in range(2):
                                last = nc.tensor.matmul(pa, mkT[:, 2 * hm + dt_, mt * 128:(mt + 1) * 128], mqT[:, 2 * hm + dt_, :],
                                                        start=(dt_ == 0), stop=(dt_ == 1))
                            return last
                        S.op("pe", mm, reads=[B("mkT"), B("mqT")], writes=[pb])
                        S.op("act", lambda: A.activation(out=pT[:, mt, :], in_=pa, func=AF.Exp, scale=1.0 / 16.0), reads=[pb],
                             writes=[B("pT%d" % mt)])
                    pdb, pd = PS()

                    def mm_d():
                        last = None
                        for mt in range(2):
                            last = nc.tensor.matmul(pd, onesf, pT[:, mt, :], start=(mt == 0), stop=(mt == 1))
                        return last
                    S.op("pe", mm_d, reads=[B("onesf"), B("pT0"), B("pT1")], writes=[pdb])
                    S.op("dve", lambda: V.reciprocal(rden[:, 0, :], pd), reads=[pdb], writes=[B("rden0")])
                    for dt_ in range(2):
                        pob, po = PS()

                        def mm_o():
                            last = None
                            for mt in range(2):
                                last = nc.tensor.matmul(po, mv[:, mt, (2 * hm + dt_) * 128:(2 * hm + dt_ + 1) * 128], pT[:, mt, :],
                                                        start=(mt == 0), stop=(mt == 1))
                            return last
                        S.op("pe", mm_o, reads=[B("mv"), B("pT0"), B("pT1")], writes=[pob])
                        S.op("dve", lambda: V.tensor_tensor(out=mqT[:, 2 * hm + dt_, :], in0=po, in1=rden[:, 0, :], op=ALU.mult),
                             reads=[pob, B("rden0")], writes=[B("mqT")])
            if stop == 44:
                return True
            if stop == 4:
                if dbg:
                    dump(0, qT, t)
                    dump(1, mqT, t)
                    dump(2, sinT, t)
                return True
            with scope(("tab", [128, 4, 2, TT], F32), ("xm", [128, 2, TT], F32), ("d12", [128, 2, 2, TT], BF16),
                       ("xs", [128, 2, TT], F32)) as (tab, xm, d12, xs):
                grp = {}
                ycur = [None]

                def geo(g):
                    j, m = g // 8, g % 8
                    return j, m // 4, m % 4, 64 * (m // 4), g % 2, g % 4

                def stA(g):
                    j, qd, v, r0, sl, ts4 = geo(g)
                    tb_ = B("tab%d" % ts4)
                    S.dma("sp", tab[:, ts4, :, :], tabs.ap()[g], reads=[B("tabs")], writes=[tb_], owner=tb_)
                    pab, pa = PS()
                    pbb, pb_ = PS()
                    S.op("pe", lambda: nc.tensor.matmul(pa, BA[r0:r0 + 64, j, v, :], sinT[r0:r0 + 64, j, :], start=True, stop=True),
                         reads=[B("BA"), B("sinT")], writes=[pab])
                    S.op("pe", lambda: nc.tensor.matmul(pb_, BB[r0:r0 + 64, j, v, :], sinT[r0:r0 + 64, j, :], start=True, stop=True),
                         reads=[B("BB"), B("sinT")], writes=[pbb])
                    grp[g] = (pab, pa, pbb, pb_)

                def stB(g):
                    j, qd, v, r0, sl, ts4 = geo(g)
                    tb_ = B("tab%d" % ts4)
                    pab, pa, pbb, pb_ = grp.pop(g)
                    S.op("dve", lambda: V.tensor_tensor(out=xm[:, 1, :], in0=pb_, in1=tab[:, ts4, 1, :], op=ALU.mult), reads=[pbb, tb_],
                         writes=[B("xm1")])
                    S.op("dve", lambda: V.tensor_tensor(out=pa, in0=pa, in1=tab[:, ts4, 0, :], op=ALU.mult), reads=[tb_], writes=[pab])
                    S.op("dve", lambda: V.tensor_tensor(out=xm[:, 0, :], in0=pa, in1=xm[:, 1, :], op=ALU.add), reads=[pab, B("xm1")],
                         writes=[B("xm0")])
                    S.op("dve", lambda: V.tensor_tensor_scan(out=xs[:, sl, :], data0=tS[:, 2, g:g + 1].to_broadcast([128, TT]),
                                                             data1=xm[:, 0, :], initial=state[:, g:g + 1], op0=ALU.mult, op1=ALU.add),
                         reads=[B("St"), B("xm0"), B("state")], writes=[B("xs%d" % sl)])
                    S.op("act", lambda: A.copy(slast[:, g:g + 1], xs[:, sl, TT - 1:TT]), reads=[B("xs%d" % sl)], writes=[B("slast")])

                def stC(g):
                    j, qd, v, r0, sl, ts4 = geo(g)
                    tb_ = B("tab%d" % ts4)
                    S.op("pool", lambda: nc.gpsimd.tensor_tensor(out=d12[:, sl, 0, :], in0=xs[:, sl, :], in1=tab[:, ts4, 0, :], op=ALU.mult),
                         reads=[B("xs%d" % sl), tb_], writes=[B("d12%d" % sl)])
                    S.op("pool", lambda: nc.gpsimd.tensor_tensor(out=d12[:, sl, 1, :], in0=xs[:, sl, :], in1=tab[:, ts4, 1, :], op=ALU.mult),
                         reads=[B("xs%d" % sl), tb_], writes=[B("d12%d" % sl)])

                def stD(g):
                    j, qd, v, r0, sl, ts4 = geo(g)
                    if v == 0:
                        ycur[0] = PSL()
                    pyb, py = ycur[0]

                    def mm_y():
                        nc.tensor.matmul(py[r0:r0 + 64, :], C1[:, g, :], d12[:, sl, 0, :], start=(v == 0), stop=False)
                        return nc.tensor.matmul(py[r0:r0 + 64, :], C2[:, g, :], d12[:, sl, 1, :], start=False, stop=(v == 3))
                    S.op("pe", mm_y, reads=[B("C1"), B("C2"), B("d12%d" % sl)], writes=[pyb])
                    if v == 3:
                        S.op("dve", lambda: V.scalar_tensor_tensor(out=sinT[r0:r0 + 64, j, :], in0=sinF[r0:r0 + 64, j, :],
                                                                   scalar=dsk[r0:r0 + 64, j:j + 1], in1=py[r0:r0 + 64, :],
                                                                   op0=ALU.mult, op1=ALU.add),
                             reads=[B("sinF"), B("dsk"), pyb], writes=[B("sinT")])

                for it in range(64 + 2):
                    if it < 64:
                        stA(it)
                    if 0 <= it - 1 < 64:
                        stB(it - 1)
                        stC(it - 1)
                    if 0 <= it - 2 < 64:
                        stD(it - 2)
                pb, pa = PS()
                S.op("pe", lambda: nc.tensor.matmul(pa[:, 0:64], swapf[:], slast[:], start=True, stop=True), reads=[B("slast"), B("swapf")],
                     writes=[pb])
                vop(lambda: V.tensor_tensor(out=state[:], in0=slast[:], in1=tS[:, 4, :], op=ALU.mult), ["slast", "St"], ["state"])
                S.op("dve", lambda: V.tensor_tensor(out=slast[:], in0=pa[:, 0:64], in1=tS[:, 5, :], op=ALU.mult), reads=[pb, B("St")],
                     writes=[B("slast")])
                vop(lambda: V.tensor_tensor(out=state[:], in0=state[:], in1=slast[:], op=ALU.add), ["slast"], ["state"])
            if dbg:
                dump(0, qT, t)
                dump(1, mqT, t)
                dump(2, sinT, t)
            if stop == 5:
                return True
            wglu = W["w_ssm_glu"][l]
            wsw = W["w_swa_up"][l]
            wmu = W["w_mem_up"][l]
            with scope(("mrg", [128, KD, TT], BF16), ("gsb", [128, 2, TT], F32), ("sg2", [128, 2, TT], F32)) as (mrg, gsb, sg2):
                def up(wview, c0, src, srcbuf):
                    i, wb = wslot_next()
                    S.dma("pool", wgu[:, i, 0:8, :], wview[c0 // 128].rearrange("p (kt c) -> p kt c", c=128), writes=[wb])
                    pb, pa = PS()

                    def mm():
                        last = None
                        for k in range(8):
                            last = nc.tensor.matmul(pa, wgu[:, i, k, :], src[:, k, :], start=(k == 0), stop=(k == 7))
                        return last
                    S.op("pe", mm, reads=[wb, srcbuf], writes=[pb])
                    return pb, pa

                def gate(bidx, i):
                    wb, wa = load_w(colview(wl, 3584 + bidx * D + i * 128))
                    pb, pa = PS()

                    def mm():
                        last = None
                        for k in range(KD):
                            last = nc.tensor.matmul(pa, wa[:, k, :], uT[:, k, :], start=(k == 0), stop=(k == KD - 1))
                        return last
                    S.op("pe", mm, reads=[wb, B("uT")], writes=[pb])
                    S.op("act", lambda: A.activation(out=gsb[:, 0, :], in_=pa, func=AF.Sigmoid), reads=[pb], writes=[B("gsb0")])

                for i in range(KD):
                    gate(0, i)
                    pb, pa = up(wsw, i * 128, qT, B("qT"))
                    S.op("dve", lambda: V.tensor_tensor(out=gsb[:, 1, :], in0=pa, in1=gsb[:, 0, :], op=ALU.mult),
                         reads=[pb, B("gsb0")], writes=[B("gsb1")])
                    gate(1, i)
                    pb2, pa2 = up(wglu, 2048 + i * 128, sinT, B("sinT"))
                    S.op("act", lambda: A.activation(out=sg2[:, 0, :], in_=pa2, func=AF.Sigmoid), reads=[pb2], writes=[B("sg0")])
                    S.op("dve", lambda: V.tensor_tensor(out=sg2[:, 0, :], in0=sg2[:, 0, :], in1=gsb[:, 0, :], op=ALU.mult),
                         reads=[B("sg0"), B("gsb0")], writes=[B("sg0")])
                    pb3, pa3 = up(wglu, i * 128, sinT, B("sinT"))
                    S.op("dve", lambda: V.tensor_tensor(out=sg2[:, 0, :], in0=pa3, in1=sg2[:, 0, :], op=ALU.mult),
                         reads=[pb3, B("sg0")], writes=[B("sg0")])
                    S.op("dve", lambda: V.tensor_tensor(out=gsb[:, 1, :], in0=gsb[:, 1, :], in1=sg2[:, 0, :], op=ALU.add),
                         reads=[B("sg0"), B("gsb1")], writes=[B("gsb1")])
                    gate(2, i)
                    pb4, pa4 = up(wmu, i * 128, mqT, B("mqT"))
                    S.op("dve", lambda: V.tensor_tensor(out=sg2[:, 1, :], in0=pa4, in1=gsb[:, 0, :], op=ALU.mult),
                         reads=[pb4, B("gsb0")], writes=[B("sg1")])
                    S.op("dve", lambda: V.tensor_tensor(out=mrg[:, i, :], in0=gsb[:, 1, :], in1=sg2[:, 1, :], op=ALU.add),
                         reads=[B("sg1"), B("gsb1")], writes=[B("mrg")])
                wov = W["w_out"][l]
                for i in range(KD):
                    wb, wa = load_w(colview(wov, i * 128))
                    pb, pa = PS()

                    def mm():
                        last = None
                        for k in range(KD):
                            last = nc.tensor.matmul(pa, wa[:, k, :], mrg[:, k, :], start=(k == 0), stop=(k == KD - 1))
                        return last
                    S.op("pe", mm, reads=[wb, B("mrg")], writes=[pb])
                    S.op("dve", lambda: V.tensor_tensor(out=hT[:, i, :], in0=pa, in1=hT[:, i, :], op=ALU.add), reads=[pb],
                         writes=[B("hT")])

    def emit_all():
        for l in range(NL):
            for t in range(NT):
                src = xT if l == 0 else hs.ap()
                S.dma("sp", hT[:], src[:, t * TT:(t + 1) * TT].rearrange("(kt p) n -> p kt n", p=128), writes=[B("hT")],
                      reads=[B("hs")] if l > 0 else [])
                if stop == 0:
                    raise _Stop()
                if t == 0:
                    mem_prep(l)
                    if stop == 1:
                        raise _Stop()
                    ssm_prep(l)
                    if stop == 2:
                        raise _Stop()
                ffn(l, "ffn1_w_in", "ffn1_w_out", 0)
                if stop == 3:
                    raise _Stop()
                if mixer(l, t, seq_start=(t == 0)):
                    raise _Stop()
                ffn(l, "ffn2_w_in", "ffn2_w_out", 3)
                if l < NL - 1:
                    S.dma("sp", hs.ap()[:, t * TT:(t + 1) * TT].rearrange("(kt p) n -> p kt n", p=128), hT[:], reads=[B("hT")],
                          writes=[B("hs")], owner=B("hs"))
                else:
                    with scope(("outF", [128, KD, TT], F32)) as (outF,):
                        rmsnorm(fnrm, B("fnrm"), hT, B("hT"), outF, B("outF"))
                        S.dma("sp", outT[:, t * TT:(t + 1) * TT].rearrange("(kt p) n -> p kt n", p=128), outF[:], reads=[B("outF")],
                              writes=[B("outT")], owner=B("outT"))
    try:
        emit_all()
    except _Stop:
        S.barrier()
        S.dma("sp", outT[:, 0:TT].rearrange("(kt p) n -> p kt n", p=128), hT[:], reads=[B("hT")], writes=[B("outT")], owner=B("outT"))
    S.wait_everything("sp")
    es.close()
    return nc


def _lay_kt(v):
    v = np.asarray(v, np.float32)
    lead = v.shape[:-1]
    return np.ascontiguousarray(np.moveaxis(v.reshape(*lead, KD, 128), -1, 0))


def host_small(NL, ffn1_norm, mix_norm, mem_norm, ffn2_norm, final_norm, sinks, lam_re, lam_im, log_dt, b_re, b_im, c_re, c_im,
               d_skip):
    norms = np.stack([_lay_kt(ffn1_norm), _lay_kt(mix_norm), _lay_kt(mem_norm), _lay_kt(ffn2_norm)], axis=2)
    fnorm = _lay_kt(final_norm)
    p = np.arange(128)
    sk = np.asarray(sinks, np.float32)
    sinkc = np.stack([sk[:, 4 * (i // 2) + 2 * (i % 2) + (p // 64)] for i in range(8)], axis=-1).transpose(1, 0, 2)
    P_, CH = 64, 16

    def blay(a):
        a = np.asarray(a, np.float32).reshape(NL, 8, 8, P_)
        a = np.repeat(a[:, :, :, None, :], CH, axis=3)
        return a.transpose(2, 3, 0, 1, 4).reshape(128, NL, 8, P_)

    def bmat(a):
        a = np.asarray(a, np.float32).reshape(NL, 8, 8, P_, CH)
        return a.transpose(2, 4, 0, 1, 3).reshape(128, NL, 8, P_)
    ldt = np.repeat(np.asarray(log_dt, np.float32)[:, :, None], P_, axis=2)
    ssmB = np.stack([blay(lam_re), blay(lam_im), blay(ldt), bmat(b_re), bmat(b_im)], axis=2)

    def slay(a):
        a = np.asarray(a, np.float32).transpose(2, 0, 1)
        return np.concatenate([a, a], axis=0)
    ssmS = np.stack([slay(lam_re), slay(lam_im), slay(ldt)], axis=2)
    cr = np.asarray(c_re, np.float32).transpose(3, 0, 1, 2)
    ci = np.asarray(c_im, np.float32).transpose(3, 0, 1, 2)
    ssmC = np.stack([np.concatenate([cr, ci], 0), np.concatenate([ci, cr], 0)], axis=2)
    dsk = np.ascontiguousarray(np.asarray(d_skip, np.float32).reshape(NL, 8, 128).transpose(2, 0, 1))
    consts = np.zeros((128, NCONST), np.float32)
    kk = np.arange(128)[:, None]
    qq = np.arange(128)[None, :]
    mprev = np.where(kk > qq, 0.0, -30000.0).astype(np.float32)
    mcur = np.where(kk <= qq, 0.0, -30000.0).astype(np.float32)
    neg = np.full((128, 128), -30000.0, np.float32)
    consts[:, 0:512] = np.concatenate([mprev, mprev, mcur, mcur], 1)
    consts[:, 512:1024] = np.concatenate([neg, neg, mcur, mcur], 1)
    consts[:, IOTA:IOTA + TT] = np.arange(1, TT + 1, dtype=np.float32)[None, :]
    consts[:, IDENT:IDENT + 128] = np.eye(128, dtype=np.float32)
    for v in range(4):
        consts[:, QM + v] = ((p // 16) % 4 == v).astype(np.float32)
    consts[:, QM + 4] = np.where(p < 64, -1.0, 1.0)
    consts[:, QM + 5] = np.where(p < 64, 1.0, -1.0)
    consts[p, SWAPP + (p + 64) % 128] = 1.0
    return {"norms": np.ascontiguousarray(norms), "fnorm": fnorm, "sinkc": np.ascontiguousarray(sinkc),
            "ssmB": np.ascontiguousarray(ssmB), "ssmS": np.ascontiguousarray(ssmS), "ssmC": np.ascontiguousarray(ssmC),
            "dskip": dsk, "consts": consts}


def _tile_lay(w):
    w = np.asarray(w, np.float32)
    NL_, K_, N_ = w.shape
    return np.ascontiguousarray(w.reshape(NL_, K_ // 128, 128, N_ // 128, 128).transpose(0, 3, 2, 1, 4)).reshape(NL_, N_ // 128, 128, K_)


def host_weights(ffn1_w_in, ffn1_w_out, w_in, w_mem_kv, w_ssm_glu, w_swa_up, w_mem_up, w_out, ffn2_w_in, ffn2_w_out):
    f = lambda a: np.ascontiguousarray(np.asarray(a, np.float32))
    return {"ffn1_w_in": _tile_lay(ffn1_w_in), "ffn1_w_out": f(ffn1_w_out), "w_in": _tile_lay(w_in), "w_mem_kv": _tile_lay(w_mem_kv),
            "w_ssm_glu": _tile_lay(w_ssm_glu), "w_swa_up": _tile_lay(w_swa_up), "w_mem_up": _tile_lay(w_mem_up), "w_out": _tile_lay(w_out),
            "ffn2_w_in": _tile_lay(ffn2_w_in), "ffn2_w_out": f(ffn2_w_out)}


def kernel(x, mem, ffn1_norm, ffn1_w_in, ffn1_w_out, mix_norm, mem_norm, w_in, sinks, w_mem_kv, lam_re, lam_im, log_dt,
           b_re, b_im, c_re, c_im, d_skip, w_ssm_glu, w_swa_up, w_mem_up, w_out, ffn2_norm, ffn2_w_in, ffn2_w_out, final_norm):
    x = np.asarray(x, np.float32)
    Bsz, L, _ = x.shape
    NL = np.asarray(ffn1_w_in).shape[0]
    NT = L // TT
    nc = build(NT, NL)
    small = host_small(NL, ffn1_norm, mix_norm, mem_norm, ffn2_norm, final_norm, sinks, lam_re, lam_im, log_dt, b_re, b_im,
                       c_re, c_im, d_skip)
    wts = host_weights(ffn1_w_in, ffn1_w_out, w_in, w_mem_kv, w_ssm_glu, w_swa_up, w_mem_up, w_out, ffn2_w_in, ffn2_w_out)
    memf_ = np.asarray(mem, np.float32)
    in_maps = []
    for b_ in range(Bsz):
        m = {"xT": np.ascontiguousarray(x[b_].T), "memT": np.ascontiguousarray(memf_[b_].T)}
        m.update(wts)
        m.update(small)
        in_maps.append(m)
    res = run_bass_kernel_spmd(nc, in_maps, core_ids=list(range(Bsz)))
    return np.stack([res.results[b_]["outT"].T for b_ in range(Bsz)], axis=0).astype(np.float32)
```

```python
import contextlib
import math
import numpy as np
import concourse.bass as bass
import concourse.mybir as mybir
from concourse.bass_utils import run_bass_kernel_spmd

F32 = mybir.dt.float32
BF16 = mybir.dt.bfloat16
AF = mybir.ActivationFunctionType
ALU = mybir.AluOpType

D = 2048
KD = 16
DFF = 5632
NFF = 44
TT = 512
NMEM = 256
INW = 9728
EPS = 1e-5
MAGIC = 12582912.0
NCONST = 1024 + 512 + 128 + 8 + 128
MASK_ROW, MASK_FIRST, IOTA, IDENT, QM, SWAPP = 0, 512, 1024, 1536, 1664, 1672
SEM_ROLL = 30000


class Buf:
    __slots__ = ("w", "r", "dsem", "dcnt", "dkey", "excl")

    def __init__(self):
        self.excl = False
        self.w = None
        self.r = []
        self.dsem = None
        self.dcnt = 0
        self.dkey = None


class Sched:
    def __init__(self, nc):
        self.nc = nc
        self.eng = {"pe": nc.tensor, "act": nc.scalar, "dve": nc.vector, "pool": nc.gpsimd, "sp": nc.sync}
        self.sem = {}
        self.key = {}
        self.cnt = {}
        self.nsem = 0
        for k in ("pe", "act", "dve", "pool"):
            self._roll(k)
        self.waited = {k: {} for k in self.eng}
        self.bufs = []
        self.old = []

    def _newsem(self):
        self.nsem += 1
        return self.nc.alloc_semaphore("sm%d" % self.nsem), "K%d" % self.nsem

    def _roll(self, k):
        if k in self.sem:
            self.old.append((self.key[k], self.sem[k], self.cnt[k]))
        self.sem[k], self.key[k] = self._newsem()
        self.cnt[k] = 0

    def buf(self):
        b = Buf()
        self.bufs.append(b)
        return b

    def _need(self, e, toks):
        eng = self.eng[e]
        best = {}
        for t in toks:
            if t is None:
                continue
            key, sem, val = t
            if e == "pe" and key == self.key.get("pe"):
                continue
            if best.get(key, (None, 0))[1] < val:
                best[key] = (sem, val)
        for key, (sem, val) in best.items():
            if self.waited[e].get(key, 0) < val:
                eng.wait_ge(sem, val)
                self.waited[e][key] = val

    @staticmethod
    def _deps(reads, writes):
        toks = []
        for b in reads:
            toks.append(b.w)
            if b.excl:
                toks.extend(b.r)
        for b in writes:
            toks.append(b.w)
            toks.extend(b.r)
        return toks

    def op(self, e, fn, reads=(), writes=()):
        self._need(e, self._deps(reads, writes))
        if self.cnt[e] >= SEM_ROLL:
            self._roll(e)
        inst = fn()
        self.cnt[e] += 1
        inst.then_inc(self.sem[e], 1)
        tok = (self.key[e], self.sem[e], self.cnt[e])
        for b in writes:
            b.w = tok
            b.r = []
        for b in reads:
            b.r.append(tok)
            if len(b.r) > 64:
                b.r = self._compact(b.r)
        return tok

    @staticmethod
    def _compact(toks):
        best = {}
        for key, sem, val in toks:
            if best.get(key, (None, 0))[1] < val:
                best[key] = (sem, val)
        return [(k, s, v) for k, (s, v) in best.items()]

    def dma(self, e, out, in_, reads=(), writes=(), owner=None, **kw):
        self._need(e, self._deps(reads, writes))
        b = owner if owner is not None else (writes[0] if writes else reads[0])
        if b.dsem is None or b.dcnt + 16 > SEM_ROLL:
            if b.dsem is not None:
                self.old.append((b.dkey, b.dsem, b.dcnt))
            b.dsem, b.dkey = self._newsem()
            b.dcnt = 0
        b.dcnt += 16
        self.eng[e].dma_start(out=out, in_=in_, **kw).then_inc(b.dsem, 16)
        tok = (b.dkey, b.dsem, b.dcnt)
        for w in writes:
            w.w = tok
            w.r = []
        for r in reads:
            r.r.append(tok)
            if len(r.r) > 64:
                r.r = self._compact(r.r)
        return tok

    def all_tokens(self):
        toks = [(self.key[k], self.sem[k], self.cnt[k]) for k in self.sem if self.cnt[k] > 0]
        toks += [t for t in self.old if t[2] > 0]
        for b in self.bufs:
            if b.dsem is not None and b.dcnt > 0:
                toks.append((b.dkey, b.dsem, b.dcnt))
        return toks

    def barrier(self):
        toks = self.all_tokens()
        for e in self.eng:
            self._need(e, toks)

    def wait_everything(self, e):
        self._need(e, self.all_tokens())


class _Stop(Exception):
    pass


def build(NT, NL, dbg=False, stop=99):
    LT = NT * TT
    nc = bass.Bass("TRN2", target_bir_lowering=False)
    S = Sched(nc)

    def din(name, shape, dt=F32):
        return nc.dram_tensor(name, list(shape), dt, kind="ExternalInput").ap()

    xT = din("xT", [D, LT])
    memT = din("memT", [D, NMEM])
    outT = nc.dram_tensor("outT", [D, LT], F32, kind="ExternalOutput").ap()
    if dbg:
        dbg_o = nc.dram_tensor("dbg_o", [3, 1024, LT], F32, kind="ExternalOutput").ap()
    W = {}
    for nm, shp in [("ffn1_w_in", [NL, 88, 128, 2048]), ("ffn1_w_out", [NL, DFF, D]), ("w_in", [NL, 76, 128, 2048]),
                    ("w_mem_kv", [NL, 16, 128, 2048]), ("w_ssm_glu", [NL, 32, 128, 1024]), ("w_swa_up", [NL, 16, 128, 1024]),
                    ("w_mem_up", [NL, 16, 128, 1024]), ("w_out", [NL, 16, 128, 2048]), ("ffn2_w_in", [NL, 88, 128, 2048]),
                    ("ffn2_w_out", [NL, DFF, D])]:
        W[nm] = din(nm, shp)
    norms = din("norms", [128, NL, 4, KD])
    fnorm = din("fnorm", [128, KD])
    sinkc = din("sinkc", [128, NL, 8])
    ssmB = din("ssmB", [128, NL, 5, 8, 64])
    ssmS = din("ssmS", [128, NL, 3, 64])
    ssmC = din("ssmC", [128, NL, 2, 64, 16])
    dskip = din("dskip", [128, NL, 8])
    consts = din("consts", [128, NCONST])
    hs = nc.dram_tensor("hs", [D, LT], F32)
    tabs = nc.dram_tensor("tabs", [64, 128, 2, TT], F32)

    es = contextlib.ExitStack()

    def sb(name, shape, dt):
        return es.enter_context(nc.sbuf_tensor(name, list(shape), dt))

    hT = sb("hT", [128, KD, TT], F32)
    uT = sb("uT", [128, KD, TT], BF16)
    cst = sb("cst", [128, NCONST - 1024], F32)
    maskb = sb("maskb", [128, 1024], BF16)
    identb = sb("identb", [128, 128], BF16)
    onesb = sb("onesb", [128, 192], BF16)
    onesf_t = sb("onesf", [128, 128], BF16)
    onesf = onesf_t[:]
    epsc = sb("epsc", [128, 1], F32)
    nrm = sb("nrm", [128, NL, 4, KD], F32)
    fnrm = sb("fnrm", [128, KD], F32)
    skc = sb("skc", [128, NL, 8], F32)
    esk = sb("esk", [128, 8], F32)
    sq = sb("sq", [128, 2, TT], BF16)
    rs = sb("rs", [128, TT], F32)
    wgu = sb("wgu", [128, 4, KD, 128], BF16)
    kT2 = sb("kT2", [128, 2, 4, 128 + TT], BF16)
    vaug = sb("vaug", [128, 5, 4, 192], BF16)
    mkT = sb("mkT", [128, 8, NMEM], BF16)
    mv = sb("mv", [128, 2, 1024], BF16)
    BA = sb("BA", [128, 8, 4, 128], BF16)
    BB = sb("BB", [128, 8, 4, 128], BF16)
    C1 = sb("C1", [128, 64, 64], BF16)
    C2 = sb("C2", [128, 64, 64], BF16)
    prmS = sb("prmS", [128, 3, 64], F32)
    tS = sb("tS", [128, 10, 64], F32)
    dsk = sb("dsk", [128, 8], F32)
    state = sb("state", [128, 64], F32)
    slast = sb("slast", [128, 64], F32)
    swapf = sb("swapf", [128, 128], F32)
    psum = es.enter_context(nc.psum_tensor("psum", [128, 8, TT], F32))

    b = {}

    def B(name):
        if name not in b:
            b[name] = S.buf()
        return b[name]

    psb = [S.buf() for _ in range(8)]
    for _b in psb:
        _b.excl = True
    psi = [0]
    psl = [0]

    def PS():
        i = psi[0] % 6
        psi[0] += 1
        return psb[i], psum[:, i, :]

    def PSL():
        i = 6 + psl[0] % 2
        psl[0] += 1
        return psb[i], psum[:, i, :]

    @contextlib.contextmanager
    def scope(*specs):
        with contextlib.ExitStack() as st:
            scope.n += 1
            ts = [st.enter_context(nc.sbuf_tensor("%s_%d" % (n, scope.n), list(s), d)) for n, s, d in specs]
            yield ts
            S.barrier()

    scope.n = 0
    V = nc.vector
    A = nc.scalar

    def vop(fn, reads, writes, e="dve"):
        S.op(e, fn, reads=[B(x) if isinstance(x, str) else x for x in reads],
             writes=[B(x) if isinstance(x, str) else x for x in writes])

    S.dma("sp", cst[:], consts[:, 1024:NCONST], writes=[B("cst")])
    S.dma("pool", maskb[:], consts[:, 0:1024], writes=[B("maskb")])
    S.dma("sp", nrm[:], norms[:, :, :, :], writes=[B("nrm")])
    S.dma("sp", fnrm[:], fnorm[:, :], writes=[B("fnrm")])
    S.dma("sp", skc[:], sinkc[:, :, :], writes=[B("skc")])
    S.dma("sp", swapf[:], consts[:, SWAPP:SWAPP + 128], writes=[B("swapf")])
    vop(lambda: V.tensor_copy(identb[:], cst[:, IDENT - 1024:IDENT - 1024 + 128]), ["cst"], ["identb"])
    vop(lambda: V.memset(onesb[:], 1.0), [], ["onesb"])
    vop(lambda: V.memset(onesb[:, 64:128], 0.0), [], ["onesb"])
    vop(lambda: V.memset(onesf_t[:], 1.0), [], ["onesf"])
    vop(lambda: V.memset(epsc[:], EPS), [], ["epsc"])
    vop(lambda: V.memset(vaug[:], 0.0), [], ["vaug"])
    vop(lambda: V.memset(kT2[:], 0.0), [], ["kT2"])
    vop(lambda: V.memset(BA[:], 0.0), [], ["BA"])
    vop(lambda: V.memset(BB[:], 0.0), [], ["BB"])
    vop(lambda: V.memset(C1[:], 0.0), [], ["C1"])
    vop(lambda: V.memset(C2[:], 0.0), [], ["C2"])

    wslot = [0]

    def wslot_next():
        i = wslot[0] % 4
        wslot[0] += 1
        return i, B("wgu%d" % i)

    def load_w(dram_view):
        i, wb = wslot_next()
        S.dma("pool", wgu[:, i, :, :], dram_view, writes=[wb])
        return wb, wgu[:, i, :, :]

    def colview(wl, c0, n=128):
        assert c0 % 128 == 0 and n == 128
        return wl[c0 // 128].rearrange("p (kt c) -> p kt c", c=128)

    def rmsnorm(gain_ap, gbuf, src, srcbuf, dst, dstbuf, nk=KD, ncol=TT):
        pb, pa = PS()
        for k in range(nk):
            sl = k % 2
            S.op("act", lambda: A.activation(out=sq[:, sl, 0:ncol], in_=src[:, k, :], func=AF.Square),
                 reads=[srcbuf], writes=[B("sq%d" % sl)])
            S.op("pe", lambda: nc.tensor.matmul(pa[:, 0:ncol], onesf, sq[:, sl, 0:ncol], start=(k == 0), stop=(k == nk - 1)),
                 reads=[B("sq%d" % sl), B("onesf")], writes=[pb])
        S.op("act", lambda: A.activation(out=rs[:, 0:ncol], in_=pa[:, 0:ncol], func=AF.Sqrt, bias=epsc[:, 0:1],
                                         scale=1.0 / D), reads=[pb, B("epsc")], writes=[B("rs")])
        S.op("dve", lambda: V.reciprocal(rs[:, 0:ncol], rs[:, 0:ncol]), reads=[B("rs")], writes=[B("rs")])
        for k in range(nk):
            S.op("dve", lambda: V.scalar_tensor_tensor(out=dst[:, k, :], in0=src[:, k, :], scalar=gain_ap[:, k:k + 1],
                                                       in1=rs[:, 0:ncol], op0=ALU.mult, op1=ALU.mult),
                 reads=[srcbuf, B("rs"), gbuf], writes=[dstbuf])

    def ffn(l, w_in_name, w_out_name, which):
        win = W[w_in_name][l]
        wout = W[w_out_name][l].rearrange("(j p) n -> p j n", p=128)
        rmsnorm(nrm[:, l, which, :], B("nrm"), hT, B("hT"), uT, B("uT"))
        with scope(("hid", [128, 2, 4, TT], BF16), ("sg", [128, 2, TT], F32), ("wo", [128, 2, 4, D], BF16)) as (hid, sg, wo):
            for c in range(NFF // 4):
                hs_ = c % 2
                wob = B("wo%d" % hs_)
                S.dma("pool", wo[:, hs_, :, :], wout[:, 4 * c:4 * c + 4, :], writes=[wob])
                woa = wo[:, hs_, :, :]
                for jj in range(4):
                    j = 4 * c + jj
                    wgb, wga = load_w(colview(win, j * 128))
                    wub, wua = load_w(colview(win, DFF + j * 128))
                    pgb, pg = PS()
                    pub, pu = PS()

                    def mm_g():
                        last = None
                        for k in range(KD):
                            last = nc.tensor.matmul(pg, wga[:, k, :], uT[:, k, :], start=(k == 0), stop=(k == KD - 1))
                        return last

                    def mm_u():
                        last = None
                        for k in range(KD):
                            last = nc.tensor.matmul(pu, wua[:, k, :], uT[:, k, :], start=(k == 0), stop=(k == KD - 1))
                        return last
                    S.op("pe", mm_g, reads=[wgb, B("uT")], writes=[pgb])
                    S.op("pe", mm_u, reads=[wub, B("uT")], writes=[pub])
                    sl = j % 2
                    S.op("act", lambda: A.activation(out=sg[:, sl, :], in_=pg, func=AF.Silu), reads=[pgb], writes=[B("sg%d" % sl)])
                    S.op("dve", lambda: V.tensor_tensor(out=hid[:, hs_, jj, :], in0=pu, in1=sg[:, sl, :], op=ALU.mult),
                         reads=[pub, B("sg%d" % sl)], writes=[B("hid%d" % hs_)])
                for i in range(KD):
                    pob, po = PS()

                    def mm_o():
                        last = None
                        for jj in range(4):
                            last = nc.tensor.matmul(po, woa[:, jj, i * 128:(i + 1) * 128], hid[:, hs_, jj, :], start=(jj == 0),
                                                    stop=(jj == 3))
                        return last
                    S.op("pe", mm_o, reads=[wob, B("hid%d" % hs_)], writes=[pob])
                    S.op("dve", lambda: V.scalar_tensor_tensor(out=hT[:, i, :], in0=po, scalar=0.5, in1=hT[:, i, :],
                                                               op0=ALU.mult, op1=ALU.add), reads=[pob], writes=[B("hT")])

    def mem_prep(l):
        with scope(("memf", [128, KD, NMEM], F32), ("memn", [128, KD, NMEM], BF16)) as (memf, memn):
            S.dma("sp", memf[:], memT.rearrange("(kt p) n -> p kt n", p=128), writes=[B("memf")])
            rmsnorm(nrm[:, l, 2, :], B("nrm"), memf, B("memf"), memn, B("memn"), ncol=NMEM)
            wk = W["w_mem_kv"][l]
            for i in range(8):
                wb, wa = load_w(colview(wk, i * 128))
                pb, pa = PS()

                def mm():
                    last = None
                    for k in range(KD):
                        last = nc.tensor.matmul(pa[:, 0:NMEM], wa[:, k, :], memn[:, k, :], start=(k == 0), stop=(k == KD - 1))
                    return last
                S.op("pe", mm, reads=[wb, B("memn")], writes=[pb])
                S.op("act", lambda: A.copy(mkT[:, i, :], pa[:, 0:NMEM]), reads=[pb], writes=[B("mkT")])
            for i in range(8):
                wb, wa = load_w(colview(wk, 1024 + i * 128))
                for mt in range(2):
                    pb, pa = PS()

                    def mm():
                        last = None
                        for k in range(KD):
                            last = nc.tensor.matmul(pa[:, 0:128], memn[:, k, mt * 128:(mt + 1) * 128], wa[:, k, :], start=(k == 0),
                                                    stop=(k == KD - 1))
                        return last
                    S.op("pe", mm, reads=[wb, B("memn")], writes=[pb])
                    S.op("dve", lambda: V.tensor_copy(mv[:, mt, i * 128:(i + 1) * 128], pa[:, 0:128]), reads=[pb], writes=[B("mv")])

    def disc(prm, tt, R, Wt):
        vop(lambda: V.tensor_scalar_min(out=tt[0], in0=prm[0], scalar1=-1e-4), R, Wt)
        vop(lambda: A.activation(out=tt[1], in_=prm[2], func=AF.Exp), R, Wt, e="act")
        vop(lambda: V.tensor_tensor(out=tt[8], in0=tt[0], in1=tt[1], op=ALU.mult), Wt, Wt)
        vop(lambda: A.activation(out=tt[2], in_=tt[8], func=AF.Exp), Wt, Wt, e="act")
        vop(lambda: V.scalar_tensor_tensor(out=tt[3], in0=prm[1], scalar=1.0 / (2 * math.pi), in1=tt[1], op0=ALU.mult,
                                           op1=ALU.mult), R + Wt, Wt)
        sincos(tt[3], tt[5], tt[4], tt[8], tt[9], Wt)

    def sincos(f, s_out, c_out, t1, t2, Wt):
        vop(lambda: V.tensor_scalar(out=t1, in0=f, scalar1=MAGIC, scalar2=MAGIC, op0=ALU.add, op1=ALU.subtract), Wt, Wt)
        vop(lambda: V.tensor_tensor(out=t1, in0=f, in1=t1, op=ALU.subtract), Wt, Wt)
        vop(lambda: A.activation(out=s_out, in_=t1, func=AF.Sin, scale=2 * math.pi), Wt, Wt, e="act")
        vop(lambda: V.tensor_scalar_add(out=t2, in0=f, scalar1=0.25), Wt, Wt)
        vop(lambda: V.tensor_scalar(out=t1, in0=t2, scalar1=MAGIC, scalar2=MAGIC, op0=ALU.add, op1=ALU.subtract), Wt, Wt)
        vop(lambda: V.tensor_tensor(out=t1, in0=t2, in1=t1, op=ALU.subtract), Wt, Wt)
        vop(lambda: A.activation(out=c_out, in_=t1, func=AF.Sin, scale=2 * math.pi), Wt, Wt, e="act")

    def ssm_prep(l):
        with scope(("prmB", [128, 5, 8, 64], F32), ("tB", [128, 10, 8, 64], F32), ("cc", [128, 2, 64, 16], F32),
                   ("targ", [128, 2, TT], F32), ("tabw", [128, 2, 2, TT], F32)) as (prmB, tB, cc, targ, tabw):
            S.dma("sp", prmB[:], ssmB[:, l, :, :, :], writes=[B("Bp")])
            S.dma("sp", prmS[:], ssmS[:, l, :, :], writes=[B("Sp")])
            S.dma("sp", cc[:], ssmC[:, l, :, :, :], writes=[B("cc")])
            S.dma("sp", dsk[:], dskip[:, l, :], writes=[B("dsk")])
            pB = [prmB[:, i, :, :] for i in range(5)]
            tb = [tB[:, i, :, :] for i in range(10)]
            R, Wt = ["Bp", "Bt"], ["Bt"]
            disc(pB, tb, ["Bp"], ["Bt"])
            vop(lambda: V.tensor_tensor(out=tb[4], in0=tb[4], in1=tb[2], op=ALU.mult), Wt, Wt)
            vop(lambda: V.tensor_tensor(out=tb[5], in0=tb[5], in1=tb[2], op=ALU.mult), Wt, Wt)
            vop(lambda: V.tensor_scalar_add(out=tb[4], in0=tb[4], scalar1=-1.0), Wt, Wt)
            vop(lambda: V.tensor_tensor(out=tb[8], in0=tb[0], in1=tb[0], op=ALU.mult), Wt, Wt)
            vop(lambda: V.tensor_tensor(out=tb[9], in0=pB[1], in1=pB[1], op=ALU.mult), R, Wt)
            vop(lambda: V.tensor_tensor(out=tb[8], in0=tb[8], in1=tb[9], op=ALU.add), Wt, Wt)
            vop(lambda: V.reciprocal(tb[8], tb[8]), Wt, Wt)
            vop(lambda: V.tensor_tensor(out=tb[6], in0=tb[4], in1=tb[0], op=ALU.mult), Wt, Wt)
            vop(lambda: V.tensor_tensor(out=tb[9], in0=tb[5], in1=pB[1], op=ALU.mult), R, Wt)
            vop(lambda: V.tensor_tensor(out=tb[6], in0=tb[6], in1=tb[9], op=ALU.add), Wt, Wt)
            vop(lambda: V.tensor_tensor(out=tb[6], in0=tb[6], in1=tb[8], op=ALU.mult), Wt, Wt)
            vop(lambda: V.tensor_tensor(out=tb[7], in0=tb[5], in1=tb[0], op=ALU.mult), Wt, Wt)
            vop(lambda: V.tensor_tensor(out=tb[9], in0=tb[4], in1=pB[1], op=ALU.mult), R, Wt)
            vop(lambda: V.tensor_tensor(out=tb[7], in0=tb[7], in1=tb[9], op=ALU.subtract), Wt, Wt)
            vop(lambda: V.tensor_tensor(out=tb[7], in0=tb[7], in1=tb[8], op=ALU.mult), Wt, Wt)
            vop(lambda: V.tensor_tensor(out=tb[2], in0=tb[6], in1=pB[3], op=ALU.mult), R, Wt)
            vop(lambda: V.tensor_tensor(out=tb[9], in0=tb[7], in1=pB[4], op=ALU.mult), R, Wt)
            vop(lambda: V.tensor_tensor(out=tb[2], in0=tb[2], in1=tb[9], op=ALU.subtract), Wt, Wt)
            vop(lambda: V.tensor_tensor(out=tb[3], in0=tb[6], in1=pB[4], op=ALU.mult), R, Wt)
            vop(lambda: V.tensor_tensor(out=tb[9], in0=tb[7], in1=pB[3], op=ALU.mult), R, Wt)
            vop(lambda: V.tensor_tensor(out=tb[3], in0=tb[3], in1=tb[9], op=ALU.add), Wt, Wt)
            vop(lambda: V.tensor_scalar_mul(out=tb[9], in0=tb[2], scalar1=-1.0), Wt, Wt)
            for v in range(4):
                qm = cst[:, QM - 1024 + v:QM - 1024 + v + 1]
                vop(lambda: V.tensor_scalar(out=BA[:, :, v, 0:64], in0=tb[2], scalar1=qm, scalar2=None, op0=ALU.mult), ["Bt", "cst"], ["BA"])
                vop(lambda: V.tensor_scalar(out=BA[:, :, v, 64:128], in0=tb[3], scalar1=qm, scalar2=None, op0=ALU.mult), ["Bt", "cst"], ["BA"])
                vop(lambda: V.tensor_scalar(out=BB[:, :, v, 0:64], in0=tb[3], scalar1=qm, scalar2=None, op0=ALU.mult), ["Bt", "cst"], ["BB"])
                vop(lambda: V.tensor_scalar(out=BB[:, :, v, 64:128], in0=tb[9], scalar1=qm, scalar2=None, op0=ALU.mult), ["Bt", "cst"], ["BB"])
            pS = [prmS[:, i, :] for i in range(3)]
            ts_ = [tS[:, i, :] for i in range(10)]
            disc(pS, ts_, ["Sp"], ["St"])
            Wt = ["St"]
            vop(lambda: V.tensor_scalar_mul(out=ts_[6], in0=ts_[3], scalar1=float(TT)), Wt, Wt)
            sincos(ts_[6], ts_[5], ts_[4], ts_[8], ts_[9], Wt)
            vop(lambda: V.tensor_scalar(out=ts_[5], in0=ts_[5], scalar1=cst[:, QM - 1024 + 4:QM - 1024 + 5], scalar2=None, op0=ALU.mult), ["St", "cst"], Wt)
            for v in range(4):
                vop(lambda: V.tensor_scalar(out=C1[:, v::4, 16 * v:16 * v + 16], in0=cc[:, 0, v::4, :], scalar1=cst[:, QM - 1024 + 5:QM - 1024 + 6],
                                            scalar2=None, op0=ALU.mult), ["cc", "cst"], ["C1"])
                vop(lambda: V.tensor_scalar_mul(out=C2[:, v::4, 16 * v:16 * v + 16], in0=cc[:, 1, v::4, :], scalar1=-1.0), ["cc"], ["C2"])
            for g in range(64):
                sl = g % 2
                fcol = tS[:, 3, g:g + 1]
                tw = ["targ"]
                vop(lambda: V.tensor_scalar(out=targ[:, 0, :], in0=cst[:, IOTA - 1024:IOTA - 1024 + TT], scalar1=fcol, scalar2=None, op0=ALU.mult),
                    ["St", "cst", "targ"], tw)
                vop(lambda: V.tensor_scalar(out=targ[:, 1, :], in0=targ[:, 0, :], scalar1=MAGIC, scalar2=MAGIC, op0=ALU.add,
                                            op1=ALU.subtract), tw, tw)
                vop(lambda: V.tensor_tensor(out=targ[:, 1, :], in0=targ[:, 0, :], in1=targ[:, 1, :], op=ALU.subtract), tw, tw)
                vop(lambda: A.activation(out=tabw[:, sl, 1, :], in_=targ[:, 1, :], func=AF.Sin, scale=2 * math.pi), tw,
                    ["tabw%d" % sl], e="act")
                vop(lambda: V.tensor_scalar_add(out=targ[:, 0, :], in0=targ[:, 0, :], scalar1=0.25), tw, tw)
                vop(lambda: V.tensor_scalar(out=targ[:, 1, :], in0=targ[:, 0, :], scalar1=MAGIC, scalar2=MAGIC, op0=ALU.add,
                                            op1=ALU.subtract), tw, tw)
                vop(lambda: V.tensor_tensor(out=targ[:, 1, :], in0=targ[:, 0, :], in1=targ[:, 1, :], op=ALU.subtract), tw, tw)
                vop(lambda: A.activation(out=tabw[:, sl, 0, :], in_=targ[:, 1, :], func=AF.Sin, scale=2 * math.pi), tw,
                    ["tabw%d" % sl], e="act")
                S.dma("sp", tabs.ap()[g], tabw[:, sl, :, :], reads=[B("tabw%d" % sl)], writes=[B("tabs")], owner=B("tabs"))
            vop(lambda: V.memset(state[:], 0.0), [], ["state"])

    def proj_fm(wl, c0, dst_ap, dstbuf, evac, dst2=None):
        wb, wa = load_w(colview(wl, c0))
        pb, pa = PS()

        def mm():
            last = None
            for k in range(KD):
                last = nc.tensor.matmul(pa, wa[:, k, :], uT[:, k, :], start=(k == 0), stop=(k == KD - 1))
            return last
        S.op("pe", mm, reads=[wb, B("uT")], writes=[pb])
        if evac == "act":
            S.op("act", lambda: A.copy(dst_ap, pa), reads=[pb], writes=[dstbuf])
        else:
            S.op("dve", lambda: V.tensor_copy(dst_ap, pa), reads=[pb], writes=[dstbuf])
        if dst2 is not None:
            S.op("dve", lambda: V.tensor_scalar(out=dst2[0], in0=pa, scalar1=1.0, scalar2=None, op0=ALU.mult), reads=[pb], writes=[dst2[1]])

    def dump(idx, src, t):
        S.dma("pool", dbg_o[idx, :, t * TT:(t + 1) * TT].rearrange("(kt p) n -> p kt n", p=128), src[:],
              reads=[B("qT%d" % q_) for q_ in range(8)] + [B("mqT%d" % q_) for q_ in range(8)] + [B("sinT")], writes=[B("dbgo")], owner=B("dbgo"))

    def mixer(l, t, seq_start):
        wl = W["w_in"][l]
        rmsnorm(nrm[:, l, 1, :], B("nrm"), hT, B("hT"), uT, B("uT"))
        with scope(("qT", [128, 8, TT], BF16), ("mqT", [128, 8, TT], BF16), ("sinT", [128, 8, TT], BF16),
                   ("sinF", [128, 8, TT], F32)) as (qT, mqT, sinT, sinF):
            with scope(("pT", [128, 2, TT], BF16), ("rden", [128, 2, TT], F32), ("wk", [128, KD, 128], BF16)) as (pT, rden, wk):
                if stop == 39:
                    return True
                for i in range(8):
                    proj_fm(wl, i * 128, qT[:, i, :], B("qT%d" % i), "act")
                if stop == 38:
                    return True
                for i in range(8):
                    proj_fm(wl, 2560 + i * 128, mqT[:, i, :], B("mqT%d" % i), "dve")
                if stop == 37:
                    return True
                for i in range(8):
                    proj_fm(wl, 1536 + i * 128, sinT[:, i, :], B("sinT"), "act", dst2=(sinF[:, i, :], B("sinF")))
                if stop == 40:
                    return True
                for gp in range(2):
                    S.dma("pool", wk[:], colview(wl, 1024 + gp * 128), writes=[B("wk")])
                    for gg in range(2):
                        g = 2 * gp + gg
                        for x in range(2):
                            i, wb = wslot_next()
                            S.op("dve", lambda: V.memset(wgu[:, i, :, (1 - x) * 64:(1 - x) * 64 + 64], 0.0), writes=[wb])
                            S.op("dve", lambda: V.tensor_scalar(out=wgu[:, i, :, x * 64:x * 64 + 64], in0=wk[:, :, gg * 64:(gg + 1) * 64],
                                                                scalar1=1.0, scalar2=None, op0=ALU.mult), reads=[B("wk")], writes=[wb])
                            pb, pa = PS()

                            def mm():
                                last = None
                                for k in range(KD):
                                    last = nc.tensor.matmul(pa, wgu[:, i, k, :], uT[:, k, :], start=(k == 0), stop=(k == KD - 1))
                                return last
                            S.op("pe", mm, reads=[wb, B("uT")], writes=[pb])
                            S.op("act", lambda: A.copy(kT2[:, x, g, 128:128 + TT], pa), reads=[pb], writes=[B("kT2")])
                if stop == 41:
                    return True
                i0, wbv0 = wslot_next()
                S.dma("pool", wgu[:, i0, :, :], colview(wl, 1280, 128), writes=[wbv0])
                i1, wbv1 = wslot_next()
                S.dma("pool", wgu[:, i1, :, :], colview(wl, 1408, 128), writes=[wbv1])
                for bq in range(4):
                    pb, pa = PS()

                    def mm():
                        last = None
                        for half, wi in ((0, i0), (1, i1)):
                            for k in range(KD):
                                last = nc.tensor.matmul(pa[:, half * 128:(half + 1) * 128], uT[:, k, bq * 128:(bq + 1) * 128],
                                                        wgu[:, wi, k, :], start=(k == 0), stop=(k == KD - 1))
                        return last
                    S.op("pe", mm, reads=[wbv0, wbv1, B("uT")], writes=[pb])
                    pv = pa[:, 0:256].rearrange("p (g d) -> p g d", g=4)
                    S.op("act", lambda: A.copy(vaug[:, 1 + bq, :, 0:64], pv), reads=[pb], writes=[B("vaug")])
                    S.op("dve", lambda: V.tensor_copy(vaug[:, 1 + bq, :, 128:192], pv), reads=[pb], writes=[B("vaug")])
                if stop == 42:
                    return True
                S.op("act", lambda: A.activation(out=esk[:], in_=skc[:, l, :], func=AF.Exp), reads=[B("skc")], writes=[B("esk")])
                units = [(bq, g, hp) for bq in range(4) for g in range(4) for hp in range(2)]

                def swa_s(u):
                    bq, g, hp = units[u]
                    psb_, ps_ = PS()
                    m0 = MASK_FIRST if (seq_start and bq == 0) else MASK_ROW
                    qt = 2 * g + hp
                    sl = u % 2

                    def mm_s():
                        nc.tensor.matmul(ps_, identb[:], maskb[:, m0:m0 + 512], start=True, stop=False)
                        last = None
                        for kb in range(2):
                            for r in range(2):
                                c0 = kb * 256 + r * 128
                                last = nc.tensor.matmul(ps_[:, c0:c0 + 128], kT2[:, r, g, (bq + kb) * 128:(bq + kb + 1) * 128],
                                                        qT[:, qt, bq * 128:(bq + 1) * 128], start=False, stop=(kb == 1 and r == 1))
                        return last
                    S.op("pe", mm_s, reads=[B("identb"), B("maskb"), B("kT2"), B("qT%d" % qt)], writes=[psb_])
                    S.op("act", lambda: A.activation(out=pT[:, sl, :], in_=ps_, func=AF.Exp, scale=0.125), reads=[psb_],
                         writes=[B("pT%d" % sl)])

                def swa_o(u):
                    bq, g, hp = units[u]
                    qt = 2 * g + hp
                    sl = u % 2
                    pob, po = PS()
                    pdb, pd = PS()

                    def mm_o():
                        n = 0
                        for kb in range(2):
                            for r in range(2):
                                lo = 0 if r == 0 else 64
                                c0 = kb * 256 + r * 128
                                nc.tensor.matmul(po[:, 0:128], vaug[:, bq + kb, g, lo:lo + 128], pT[:, sl, c0:c0 + 128],
                                                 start=(n == 0), stop=(n == 3))
                                n += 1
                        n = 0
                        last = None
                        for kb in range(2):
                            for r in range(2):
                                lo = 0 if r == 0 else 64
                                c0 = kb * 256 + r * 128
                                last = nc.tensor.matmul(pd[:, 0:128], onesb[:, lo:lo + 128], pT[:, sl, c0:c0 + 128],
                                                        start=(n == 0), stop=(n == 3))
                                n += 1
                        return last
                    S.op("pe", mm_o, reads=[B("vaug"), B("onesb"), B("pT%d" % sl)], writes=[pob, pdb])
                    idx = g * 2 + hp
                    S.op("dve", lambda: V.tensor_scalar(out=rden[:, sl, 0:128], in0=pd[:, 0:128], scalar1=esk[:, idx:idx + 1],
                                                        scalar2=None, op0=ALU.add), reads=[pdb, B("esk")], writes=[B("rden%d" % sl)])
                    S.op("dve", lambda: V.reciprocal(rden[:, sl, 0:128], rden[:, sl, 0:128]), reads=[B("rden%d" % sl)],
                         writes=[B("rden%d" % sl)])
                    S.op("dve", lambda: V.tensor_tensor(out=qT[:, qt, bq * 128:(bq + 1) * 128], in0=po[:, 0:128],
                                                        in1=rden[:, sl, 0:128], op=ALU.mult),
                         reads=[pob, B("rden%d" % sl)], writes=[B("qT%d" % qt)])

                for it in range(len(units) + 1):
                    if it < len(units):
                        swa_s(it)
                    if it >= 1:
                        swa_o(it - 1)
                if stop == 43:
                    return True
                vop(lambda: V.tensor_scalar(out=kT2[:, 0, :, 0:128], in0=kT2[:, 0, :, TT:TT + 128], scalar1=1.0, scalar2=None, op0=ALU.mult), ["kT2"], ["kT2"])
                vop(lambda: V.tensor_scalar(out=kT2[:, 1, :, 0:128], in0=kT2[:, 1, :, TT:TT + 128], scalar1=1.0, scalar2=None, op0=ALU.mult), ["kT2"], ["kT2"])
                vop(lambda: V.tensor_scalar(out=vaug[:, 0, :, :], in0=vaug[:, 4, :, :], scalar1=1.0, scalar2=None, op0=ALU.mult), ["vaug"], ["vaug"])
                for hm in range(4):
                    for mt in range(2):
                        pb, pa = PS()

                        def mm():
                            last = None
                            for dt_ in range(2):
                                last = nc.tensor.matmul(pa, mkT[:, 2 * hm + dt_, mt * 128:(mt + 1) * 128], mqT[:, 2 * hm + dt_, :],
                                                        start=(dt_ == 0), stop=(dt_ == 1))
                            return last
                        S.op("pe", mm, reads=[B("mkT"), B("mqT%d" % (2 * hm)), B("mqT%d" % (2 * hm + 1))], writes=[pb])
                        S.op("act", lambda: A.activation(out=pT[:, mt, :], in_=pa, func=AF.Exp, scale=1.0 / 16.0), reads=[pb],
                             writes=[B("pT%d" % mt)])
                    pdb, pd = PS()

                    def mm_d():
                        last = None
                        for mt in range(2):
                            last = nc.tensor.matmul(pd, onesf, pT[:, mt, :], start=(mt == 0), stop=(mt == 1))
                        return last
                    S.op("pe", mm_d, reads=[B("onesf"), B("pT0"), B("pT1")], writes=[pdb])
                    S.op("dve", lambda: V.reciprocal(rden[:, 0, :], pd), reads=[pdb], writes=[B("rden0")])
                    for dt_ in range(2):
                        pob, po = PS()

                        def mm_o():
                            last = None
                            for mt in range(2):
                                last = nc.tensor.matmul(po, mv[:, mt, (2 * hm + dt_) * 128:(2 * hm + dt_ + 1) * 128], pT[:, mt, :],
                                                        start=(mt == 0), stop=(mt == 1))
                            return last
                        S.op("pe", mm_o, reads=[B("mv"), B("pT0"), B("pT1")], writes=[pob])
                        S.op("dve", lambda: V.tensor_tensor(out=mqT[:, 2 * hm + dt_, :], in0=po, in1=rden[:, 0, :], op=ALU.mult),
                             reads=[pob, B("rden0")], writes=[B("mqT%d" % (2 * hm + dt_))])
            if stop == 44:
                return True
            if stop == 4:
                if dbg:
                    dump(0, qT, t)
                    dump(1, mqT, t)
                    dump(2, sinT, t)
                return True
            with scope(("tab", [128, 4, 2, TT], F32), ("xm", [128, 2, TT], F32), ("d12", [128, 2, 2, TT], BF16),
                       ("xs", [128, 2, TT], F32)) as (tab, xm, d12, xs):
                grp = {}
                ycur = [None]

                def geo(g):
                    j, m = g // 8, g % 8
                    return j, m // 4, m % 4, 64 * (m // 4), g % 2, g % 4

                def stA(g):
                    j, qd, v, r0, sl, ts4 = geo(g)
                    tb_ = B("tab%d" % ts4)
                    S.dma("sp", tab[:, ts4, :, :], tabs.ap()[g], reads=[B("tabs")], writes=[tb_], owner=tb_)
                    pab, pa = PS()
                    pbb, pb_ = PS()
                    S.op("pe", lambda: nc.tensor.matmul(pa, BA[r0:r0 + 64, j, v, :], sinT[r0:r0 + 64, j, :], start=True, stop=True),
                         reads=[B("BA"), B("sinT")], writes=[pab])
                    S.op("pe", lambda: nc.tensor.matmul(pb_, BB[r0:r0 + 64, j, v, :], sinT[r0:r0 + 64, j, :], start=True, stop=True),
                         reads=[B("BB"), B("sinT")], writes=[pbb])
                    grp[g] = (pab, pa, pbb, pb_)

                def stB(g):
                    j, qd, v, r0, sl, ts4 = geo(g)
                    tb_ = B("tab%d" % ts4)
                    pab, pa, pbb, pb_ = grp.pop(g)
                    S.op("dve", lambda: V.tensor_tensor(out=xm[:, 1, :], in0=pb_, in1=tab[:, ts4, 1, :], op=ALU.mult), reads=[pbb, tb_],
                         writes=[B("xm1")])
                    S.op("dve", lambda: V.tensor_tensor(out=pa, in0=pa, in1=tab[:, ts4, 0, :], op=ALU.mult), reads=[tb_], writes=[pab])
                    S.op("dve", lambda: V.tensor_tensor(out=xm[:, 0, :], in0=pa, in1=xm[:, 1, :], op=ALU.add), reads=[pab, B("xm1")],
                         writes=[B("xm0")])
                    S.op("dve", lambda: V.tensor_tensor_scan(out=xs[:, sl, :], data0=tS[:, 2, g:g + 1].to_broadcast([128, TT]),
                                                             data1=xm[:, 0, :], initial=state[:, g:g + 1], op0=ALU.mult, op1=ALU.add),
                         reads=[B("St"), B("xm0"), B("state")], writes=[B("xs%d" % sl)])
                    S.op("act", lambda: A.copy(slast[:, g:g + 1], xs[:, sl, TT - 1:TT]), reads=[B("xs%d" % sl)], writes=[B("slast")])

                def stC(g):
                    j, qd, v, r0, sl, ts4 = geo(g)
                    tb_ = B("tab%d" % ts4)
                    S.op("pool", lambda: nc.gpsimd.tensor_tensor(out=d12[:, sl, 0, :], in0=xs[:, sl, :], in1=tab[:, ts4, 0, :], op=ALU.mult),
                         reads=[B("xs%d" % sl), tb_], writes=[B("d12%d" % sl)])
                    S.op("pool", lambda: nc.gpsimd.tensor_tensor(out=d12[:, sl, 1, :], in0=xs[:, sl, :], in1=tab[:, ts4, 1, :], op=ALU.mult),
                         reads=[B("xs%d" % sl), tb_], writes=[B("d12%d" % sl)])

                def stD(g):
                    j, qd, v, r0, sl, ts4 = geo(g)
                    if v == 0:
                        ycur[0] = PSL()
                    pyb, py = ycur[0]

                    def mm_y():
                        nc.tensor.matmul(py[r0:r0 + 64, :], C1[:, g, :], d12[:, sl, 0, :], start=(v == 0), stop=False)
                        return nc.tensor.matmul(py[r0:r0 + 64, :], C2[:, g, :], d12[:, sl, 1, :], start=False, stop=(v == 3))
                    S.op("pe", mm_y, reads=[B("C1"), B("C2"), B("d12%d" % sl)], writes=[pyb])
                    if v == 3:
                        S.op("dve", lambda: V.scalar_tensor_tensor(out=sinT[r0:r0 + 64, j, :], in0=sinF[r0:r0 + 64, j, :],
                                                                   scalar=dsk[r0:r0 + 64, j:j + 1], in1=py[r0:r0 + 64, :],
                                                                   op0=ALU.mult, op1=ALU.add),
                             reads=[B("sinF"), B("dsk"), pyb], writes=[B("sinT")])

                for it in range(64 + 2):
                    if it < 64:
                        stA(it)
                    if 0 <= it - 1 < 64:
                        stB(it - 1)
                        stC(it - 1)
                    if 0 <= it - 2 < 64:
                        stD(it - 2)
                pb, pa = PS()
                S.op("pe", lambda: nc.tensor.matmul(pa[:, 0:64], swapf[:], slast[:], start=True, stop=True), reads=[B("slast"), B("swapf")],
                     writes=[pb])
                vop(lambda: V.tensor_tensor(out=state[:], in0=slast[:], in1=tS[:, 4, :], op=ALU.mult), ["slast", "St"], ["state"])
                S.op("dve", lambda: V.tensor_tensor(out=slast[:], in0=pa[:, 0:64], in1=tS[:, 5, :], op=ALU.mult), reads=[pb, B("St")],
                     writes=[B("slast")])
                vop(lambda: V.tensor_tensor(out=state[:], in0=state[:], in1=slast[:], op=ALU.add), ["slast"], ["state"])
            if dbg:
                dump(0, qT, t)
                dump(1, mqT, t)
                dump(2, sinT, t)
            if stop == 5:
                return True
            wglu = W["w_ssm_glu"][l]
            wsw = W["w_swa_up"][l]
            wmu = W["w_mem_up"][l]
            with scope(("mrg", [128, KD, TT], BF16), ("gsb", [128, 2, TT], F32), ("sg2", [128, 2, TT], F32)) as (mrg, gsb, sg2):
                def up(wview, c0, src, srcbuf):
                    i, wb = wslot_next()
                    S.dma("pool", wgu[:, i, 0:8, :], wview[c0 // 128].rearrange("p (kt c) -> p kt c", c=128), writes=[wb])
                    pb, pa = PS()

                    def mm():
                        last = None
                        for k in range(8):
                            last = nc.tensor.matmul(pa, wgu[:, i, k, :], src[:, k, :], start=(k == 0), stop=(k == 7))
                        return last
                    S.op("pe", mm, reads=[wb] + (srcbuf if isinstance(srcbuf, list) else [srcbuf]), writes=[pb])
                    return pb, pa

                def gate(bidx, i):
                    wb, wa = load_w(colview(wl, 3584 + bidx * D + i * 128))
                    pb, pa = PS()

                    def mm():
                        last = None
                        for k in range(KD):
                            last = nc.tensor.matmul(pa, wa[:, k, :], uT[:, k, :], start=(k == 0), stop=(k == KD - 1))
                        return last
                    S.op("pe", mm, reads=[wb, B("uT")], writes=[pb])
                    S.op("act", lambda: A.activation(out=gsb[:, 0, :], in_=pa, func=AF.Sigmoid), reads=[pb], writes=[B("gsb0")])

                for i in range(KD):
                    gate(0, i)
                    pb, pa = up(wsw, i * 128, qT, [B("qT%d" % q_) for q_ in range(8)])
                    S.op("dve", lambda: V.tensor_tensor(out=gsb[:, 1, :], in0=pa, in1=gsb[:, 0, :], op=ALU.mult),
                         reads=[pb, B("gsb0")], writes=[B("gsb1")])
                    gate(1, i)
                    pb2, pa2 = up(wglu, 2048 + i * 128, sinT, B("sinT"))
                    S.op("act", lambda: A.activation(out=sg2[:, 0, :], in_=pa2, func=AF.Sigmoid), reads=[pb2], writes=[B("sg0")])
                    S.op("dve", lambda: V.tensor_tensor(out=sg2[:, 0, :], in0=sg2[:, 0, :], in1=gsb[:, 0, :], op=ALU.mult),
                         reads=[B("sg0"), B("gsb0")], writes=[B("sg0")])
                    pb3, pa3 = up(wglu, i * 128, sinT, B("sinT"))
                    S.op("dve", lambda: V.tensor_tensor(out=sg2[:, 0, :], in0=pa3, in1=sg2[:, 0, :], op=ALU.mult),
                         reads=[pb3, B("sg0")], writes=[B("sg0")])
                    S.op("dve", lambda: V.tensor_tensor(out=gsb[:, 1, :], in0=gsb[:, 1, :], in1=sg2[:, 0, :], op=ALU.add),
                         reads=[B("sg0"), B("gsb1")], writes=[B("gsb1")])
                    gate(2, i)
                    pb4, pa4 = up(wmu, i * 128, mqT, [B("mqT%d" % q_) for q_ in range(8)])
                    S.op("dve", lambda: V.tensor_tensor(out=sg2[:, 1, :], in0=pa4, in1=gsb[:, 0, :], op=ALU.mult),
                         reads=[pb4, B("gsb0")], writes=[B("sg1")])
                    S.op("dve", lambda: V.tensor_tensor(out=mrg[:, i, :], in0=gsb[:, 1, :], in1=sg2[:, 1, :], op=ALU.add),
                         reads=[B("sg1"), B("gsb1")], writes=[B("mrg")])
                wov = W["w_out"][l]
                for i in range(KD):
                    wb, wa = load_w(colview(wov, i * 128))
                    pb, pa = PS()

                    def mm():
                        last = None
                        for k in range(KD):
                            last = nc.tensor.matmul(pa, wa[:, k, :], mrg[:, k, :], start=(k == 0), stop=(k == KD - 1))
                        return last
                    S.op("pe", mm, reads=[wb, B("mrg")], writes=[pb])
                    S.op("dve", lambda: V.tensor_tensor(out=hT[:, i, :], in0=pa, in1=hT[:, i, :], op=ALU.add), reads=[pb],
                         writes=[B("hT")])

    def emit_all():
        for l in range(NL):
            for t in range(NT):
                src = xT if l == 0 else hs.ap()
                S.dma("sp", hT[:], src[:, t * TT:(t + 1) * TT].rearrange("(kt p) n -> p kt n", p=128), writes=[B("hT")],
                      reads=[B("hs")] if l > 0 else [])
                if stop == 0:
                    raise _Stop()
                if t == 0:
                    mem_prep(l)
                    if stop == 1:
                        raise _Stop()
                    ssm_prep(l)
                    if stop == 2:
                        raise _Stop()
                ffn(l, "ffn1_w_in", "ffn1_w_out", 0)
                if stop == 3:
                    raise _Stop()
                if mixer(l, t, seq_start=(t == 0)):
                    raise _Stop()
                ffn(l, "ffn2_w_in", "ffn2_w_out", 3)
                if l < NL - 1:
                    S.dma("sp", hs.ap()[:, t * TT:(t + 1) * TT].rearrange("(kt p) n -> p kt n", p=128), hT[:], reads=[B("hT")],
                          writes=[B("hs")], owner=B("hs"))
                else:
                    with scope(("outF", [128, KD, TT], F32)) as (outF,):
                        rmsnorm(fnrm, B("fnrm"), hT, B("hT"), outF, B("outF"))
                        S.dma("sp", outT[:, t * TT:(t + 1) * TT].rearrange("(kt p) n -> p kt n", p=128), outF[:], reads=[B("outF")],
                              writes=[B("outT")], owner=B("outT"))
    try:
        emit_all()
    except _Stop:
        S.barrier()
        S.dma("sp", outT[:, 0:TT].rearrange("(kt p) n -> p kt n", p=128), hT[:], reads=[B("hT")], writes=[B("outT")], owner=B("outT"))
    S.wait_everything("sp")
    es.close()
    return nc


def _lay_kt(v):
    v = np.asarray(v, np.float32)
    lead = v.shape[:-1]
    return np.ascontiguousarray(np.moveaxis(v.reshape(*lead, KD, 128), -1, 0))


def host_small(NL, ffn1_norm, mix_norm, mem_norm, ffn2_norm, final_norm, sinks, lam_re, lam_im, log_dt, b_re, b_im, c_re, c_im,
               d_skip):
    norms = np.stack([_lay_kt(ffn1_norm), _lay_kt(mix_norm), _lay_kt(mem_norm), _lay_kt(ffn2_norm)], axis=2)
    fnorm = _lay_kt(final_norm)
    p = np.arange(128)
    sk = np.asarray(sinks, np.float32)
    sinkc = np.stack([sk[:, 4 * (i // 2) + 2 * (i % 2) + (p // 64)] for i in range(8)], axis=-1).transpose(1, 0, 2)
    P_, CH = 64, 16

    def blay(a):
        a = np.asarray(a, np.float32).reshape(NL, 8, 8, P_)
        a = np.repeat(a[:, :, :, None, :], CH, axis=3)
        return a.transpose(2, 3, 0, 1, 4).reshape(128, NL, 8, P_)

    def bmat(a):
        a = np.asarray(a, np.float32).reshape(NL, 8, 8, P_, CH)
        return a.transpose(2, 4, 0, 1, 3).reshape(128, NL, 8, P_)
    ldt = np.repeat(np.asarray(log_dt, np.float32)[:, :, None], P_, axis=2)
    ssmB = np.stack([blay(lam_re), blay(lam_im), blay(ldt), bmat(b_re), bmat(b_im)], axis=2)

    def slay(a):
        a = np.asarray(a, np.float32).transpose(2, 0, 1)
        return np.concatenate([a, a], axis=0)
    ssmS = np.stack([slay(lam_re), slay(lam_im), slay(ldt)], axis=2)
    cr = np.asarray(c_re, np.float32).transpose(3, 0, 1, 2)
    ci = np.asarray(c_im, np.float32).transpose(3, 0, 1, 2)
    ssmC = np.stack([np.concatenate([cr, ci], 0), np.concatenate([ci, cr], 0)], axis=2)
    dsk = np.ascontiguousarray(np.asarray(d_skip, np.float32).reshape(NL, 8, 128).transpose(2, 0, 1))
    consts = np.zeros((128, NCONST), np.float32)
    kk = np.arange(128)[:, None]
    qq = np.arange(128)[None, :]
    mprev = np.where(kk > qq, 0.0, -30000.0).astype(np.float32)
    mcur = np.where(kk <= qq, 0.0, -30000.0).astype(np.float32)
    neg = np.full((128, 128), -30000.0, np.float32)
    consts[:, 0:512] = np.concatenate([mprev, mprev, mcur, mcur], 1)
    consts[:, 512:1024] = np.concatenate([neg, neg, mcur, mcur], 1)
    consts[:, IOTA:IOTA + TT] = np.arange(1, TT + 1, dtype=np.float32)[None, :]
    consts[:, IDENT:IDENT + 128] = np.eye(128, dtype=np.float32)
    for v in range(4):
        consts[:, QM + v] = ((p // 16) % 4 == v).astype(np.float32)
    consts[:, QM + 4] = np.where(p < 64, -1.0, 1.0)
    consts[:, QM + 5] = np.where(p < 64, 1.0, -1.0)
    consts[p, SWAPP + (p + 64) % 128] = 1.0
    return {"norms": np.ascontiguousarray(norms), "fnorm": fnorm, "sinkc": np.ascontiguousarray(sinkc),
            "ssmB": np.ascontiguousarray(ssmB), "ssmS": np.ascontiguousarray(ssmS), "ssmC": np.ascontiguousarray(ssmC),
            "dskip": dsk, "consts": consts}


def _tile_lay(w):
    w = np.asarray(w, np.float32)
    NL_, K_, N_ = w.shape
    return np.ascontiguousarray(w.reshape(NL_, K_ // 128, 128, N_ // 128, 128).transpose(0, 3, 2, 1, 4)).reshape(NL_, N_ // 128, 128, K_)


def host_weights(ffn1_w_in, ffn1_w_out, w_in, w_mem_kv, w_ssm_glu, w_swa_up, w_mem_up, w_out, ffn2_w_in, ffn2_w_out):
    f = lambda a: np.ascontiguousarray(np.asarray(a, np.float32))
    return {"ffn1_w_in": _tile_lay(ffn1_w_in), "ffn1_w_out": f(ffn1_w_out), "w_in": _tile_lay(w_in), "w_mem_kv": _tile_lay(w_mem_kv),
            "w_ssm_glu": _tile_lay(w_ssm_glu), "w_swa_up": _tile_lay(w_swa_up), "w_mem_up": _tile_lay(w_mem_up), "w_out": _tile_lay(w_out),
            "ffn2_w_in": _tile_lay(ffn2_w_in), "ffn2_w_out": f(ffn2_w_out)}


def kernel(x, mem, ffn1_norm, ffn1_w_in, ffn1_w_out, mix_norm, mem_norm, w_in, sinks, w_mem_kv, lam_re, lam_im, log_dt,
           b_re, b_im, c_re, c_im, d_skip, w_ssm_glu, w_swa_up, w_mem_up, w_out, ffn2_norm, ffn2_w_in, ffn2_w_out, final_norm):
    x = np.asarray(x, np.float32)
    Bsz, L, _ = x.shape
    NL = np.asarray(ffn1_w_in).shape[0]
    NT = L // TT
    nc = build(NT, NL)
    small = host_small(NL, ffn1_norm, mix_norm, mem_norm, ffn2_norm, final_norm, sinks, lam_re, lam_im, log_dt, b_re, b_im,
                       c_re, c_im, d_skip)
    wts = host_weights(ffn1_w_in, ffn1_w_out, w_in, w_mem_kv, w_ssm_glu, w_swa_up, w_mem_up, w_out, ffn2_w_in, ffn2_w_out)
    memf_ = np.asarray(mem, np.float32)
    in_maps = []
    for b_ in range(Bsz):
        m = {"xT": np.ascontiguousarray(x[b_].T), "memT": np.ascontiguousarray(memf_[b_].T)}
        m.update(wts)
        m.update(small)
        in_maps.append(m)
    res = run_bass_kernel_spmd(nc, in_maps, core_ids=list(range(Bsz)))
    return np.stack([res.results[b_]["outT"].T for b_ in range(Bsz)], axis=0).astype(np.float32)
```

```python
import contextlib
import math
import numpy as np
import concourse.bass as bass
import concourse.mybir as mybir
from concourse.bass_utils import run_bass_kernel_spmd

F32 = mybir.dt.float32
BF16 = mybir.dt.bfloat16
AF = mybir.ActivationFunctionType
ALU = mybir.AluOpType

D = 2048
KD = 16
DFF = 5632
NFF = 44
TT = 512
NMEM = 256
INW = 9728
EPS = 1e-5
MAGIC = 12582912.0
NCONST = 1024 + 512 + 128 + 8 + 128
MASK_ROW, MASK_FIRST, IOTA, IDENT, QM, SWAPP = 0, 512, 1024, 1536, 1664, 1672
SEM_ROLL = 30000


class Buf:
    __slots__ = ("w", "r", "dsem", "dcnt", "dkey", "excl")

    def __init__(self):
        self.excl = False
        self.w = None
        self.r = []
        self.dsem = None
        self.dcnt = 0
        self.dkey = None


class Sched:
    def __init__(self, nc):
        self.nc = nc
        self.eng = {"pe": nc.tensor, "act": nc.scalar, "dve": nc.vector, "pool": nc.gpsimd, "sp": nc.sync}
        self.sem = {}
        self.key = {}
        self.cnt = {}
        self.nsem = 0
        for k in ("pe", "act", "dve", "pool"):
            self._roll(k)
        self.waited = {k: {} for k in self.eng}
        self.bufs = []
        self.old = []

    def _newsem(self):
        self.nsem += 1
        return self.nc.alloc_semaphore("sm%d" % self.nsem), "K%d" % self.nsem

    def _roll(self, k):
        if k in self.sem:
            self.old.append((self.key[k], self.sem[k], self.cnt[k]))
        self.sem[k], self.key[k] = self._newsem()
        self.cnt[k] = 0

    def buf(self):
        b = Buf()
        self.bufs.append(b)
        return b

    def _need(self, e, toks):
        eng = self.eng[e]
        best = {}
        for t in toks:
            if t is None:
                continue
            key, sem, val = t
            if e == "pe" and key == self.key.get("pe"):
                continue
            if best.get(key, (None, 0))[1] < val:
                best[key] = (sem, val)
        for key, (sem, val) in best.items():
            if self.waited[e].get(key, 0) < val:
                eng.wait_ge(sem, val)
                self.waited[e][key] = val

    @staticmethod
    def _deps(reads, writes):
        toks = []
        for b in reads:
            toks.append(b.w)
            if b.excl:
                toks.extend(b.r)
        for b in writes:
            toks.append(b.w)
            toks.extend(b.r)
        return toks

    def op(self, e, fn, reads=(), writes=()):
        self._need(e, self._deps(reads, writes))
        if self.cnt[e] >= SEM_ROLL:
            self._roll(e)
        inst = fn()
        self.cnt[e] += 1
        inst.then_inc(self.sem[e], 1)
        tok = (self.key[e], self.sem[e], self.cnt[e])
        for b in writes:
            b.w = tok
            b.r = []
        for b in reads:
            b.r.append(tok)
            if len(b.r) > 64:
                b.r = self._compact(b.r)
        return tok

    @staticmethod
    def _compact(toks):
        best = {}
        for key, sem, val in toks:
            if best.get(key, (None, 0))[1] < val:
                best[key] = (sem, val)
        return [(k, s, v) for k, (s, v) in best.items()]

    def dma(self, e, out, in_, reads=(), writes=(), owner=None, **kw):
        self._need(e, self._deps(reads, writes))
        b = owner if owner is not None else (writes[0] if writes else reads[0])
        if b.dsem is None or b.dcnt + 16 > SEM_ROLL:
            if b.dsem is not None:
                self.old.append((b.dkey, b.dsem, b.dcnt))
            b.dsem, b.dkey = self._newsem()
            b.dcnt = 0
        b.dcnt += 16
        self.eng[e].dma_start(out=out, in_=in_, **kw).then_inc(b.dsem, 16)
        tok = (b.dkey, b.dsem, b.dcnt)
        for w in writes:
            w.w = tok
            w.r = []
        for r in reads:
            r.r.append(tok)
            if len(r.r) > 64:
                r.r = self._compact(r.r)
        return tok

    def all_tokens(self):
        toks = [(self.key[k], self.sem[k], self.cnt[k]) for k in self.sem if self.cnt[k] > 0]
        toks += [t for t in self.old if t[2] > 0]
        for b in self.bufs:
            if b.dsem is not None and b.dcnt > 0:
                toks.append((b.dkey, b.dsem, b.dcnt))
        return toks

    def barrier(self):
        toks = self.all_tokens()
        for e in self.eng:
            self._need(e, toks)

    def wait_everything(self, e):
        self._need(e, self.all_tokens())


class _Stop(Exception):
    pass


def build(NT, NL, dbg=False, stop=99):
    LT = NT * TT
    nc = bass.Bass("TRN2", target_bir_lowering=False)
    S = Sched(nc)

    def din(name, shape, dt=F32):
        return nc.dram_tensor(name, list(shape), dt, kind="ExternalInput").ap()

    xT = din("xT", [D, LT])
    memT = din("memT", [D, NMEM])
    outT = nc.dram_tensor("outT", [D, LT], F32, kind="ExternalOutput").ap()
    if dbg:
        dbg_o = nc.dram_tensor("dbg_o", [3, 1024, LT], F32, kind="ExternalOutput").ap()
    W = {}
    for nm, shp in [("ffn1_w_in", [NL, 88, 128, 2048]), ("ffn1_w_out", [NL, DFF, D]), ("w_in", [NL, 76, 128, 2048]),
                    ("w_mem_kv", [NL, 16, 128, 2048]), ("w_ssm_glu", [NL, 32, 128, 1024]), ("w_swa_up", [NL, 16, 128, 1024]),
                    ("w_mem_up", [NL, 16, 128, 1024]), ("w_out", [NL, 16, 128, 2048]), ("ffn2_w_in", [NL, 88, 128, 2048]),
                    ("ffn2_w_out", [NL, DFF, D])]:
        W[nm] = din(nm, shp)
    norms = din("norms", [128, NL, 4, KD])
    fnorm = din("fnorm", [128, KD])
    sinkc = din("sinkc", [128, NL, 8])
    ssmB = din("ssmB", [128, NL, 5, 8, 64])
    ssmS = din("ssmS", [128, NL, 3, 64])
    ssmC = din("ssmC", [128, NL, 2, 64, 16])
    dskip = din("dskip", [128, NL, 8])
    consts = din("consts", [128, NCONST])
    hs = nc.dram_tensor("hs", [D, LT], F32)
    tabs = nc.dram_tensor("tabs", [64, 128, 2, TT], F32)

    es = contextlib.ExitStack()

    def sb(name, shape, dt):
        return es.enter_context(nc.sbuf_tensor(name, list(shape), dt))

    hT = sb("hT", [128, KD, TT], F32)
    uT = sb("uT", [128, KD, TT], BF16)
    cst = sb("cst", [128, NCONST - 1024], F32)
    maskb = sb("maskb", [128, 1024], BF16)
    identb = sb("identb", [128, 128], BF16)
    onesb = sb("onesb", [128, 192], BF16)
    onesf_t = sb("onesf", [128, 128], BF16)
    onesf = onesf_t[:]
    epsc = sb("epsc", [128, 1], F32)
    nrm = sb("nrm", [128, NL, 4, KD], F32)
    fnrm = sb("fnrm", [128, KD], F32)
    skc = sb("skc", [128, NL, 8], F32)
    esk = sb("esk", [128, 8], F32)
    sq = sb("sq", [128, 2, TT], BF16)
    rs = sb("rs", [128, TT], F32)
    wgu = sb("wgu", [128, 4, KD, 128], BF16)
    kT2 = sb("kT2", [128, 2, 4, 128 + TT], BF16)
    vaug = sb("vaug", [128, 5, 4, 192], BF16)
    mkT = sb("mkT", [128, 8, NMEM], BF16)
    mv = sb("mv", [128, 2, 1024], BF16)
    BA = sb("BA", [128, 8, 4, 128], BF16)
    BB = sb("BB", [128, 8, 4, 128], BF16)
    C1 = sb("C1", [128, 64, 64], BF16)
    C2 = sb("C2", [128, 64, 64], BF16)
    prmS = sb("prmS", [128, 3, 64], F32)
    tS = sb("tS", [128, 10, 64], F32)
    dsk = sb("dsk", [128, 8], F32)
    state = sb("state", [128, 64], F32)
    slast = sb("slast", [128, 64], F32)
    swapf = sb("swapf", [128, 128], F32)
    psum = es.enter_context(nc.psum_tensor("psum", [128, 8, TT], F32))

    b = {}

    def B(name):
        if name not in b:
            b[name] = S.buf()
        return b[name]

    psb = [S.buf() for _ in range(8)]
    for _b in psb:
        _b.excl = True
    psi = [0]
    psl = [0]

    def PS():
        i = psi[0] % 6
        psi[0] += 1
        return psb[i], psum[:, i, :]

    def PSL():
        i = 6 + psl[0] % 2
        psl[0] += 1
        return psb[i], psum[:, i, :]

    @contextlib.contextmanager
    def scope(*specs):
        with contextlib.ExitStack() as st:
            scope.n += 1
            ts = [st.enter_context(nc.sbuf_tensor("%s_%d" % (n, scope.n), list(s), d)) for n, s, d in specs]
            yield ts
            S.barrier()

    scope.n = 0
    V = nc.vector
    A = nc.scalar

    def vop(fn, reads, writes, e="dve"):
        S.op(e, fn, reads=[B(x) if isinstance(x, str) else x for x in reads],
             writes=[B(x) if isinstance(x, str) else x for x in writes])

    S.dma("sp", cst[:], consts[:, 1024:NCONST], writes=[B("cst")])
    S.dma("pool", maskb[:], consts[:, 0:1024], writes=[B("maskb")])
    S.dma("sp", nrm[:], norms[:, :, :, :], writes=[B("nrm")])
    S.dma("sp", fnrm[:], fnorm[:, :], writes=[B("fnrm")])
    S.dma("sp", skc[:], sinkc[:, :, :], writes=[B("skc")])
    S.dma("sp", swapf[:], consts[:, SWAPP:SWAPP + 128], writes=[B("swapf")])
    vop(lambda: V.tensor_copy(identb[:], cst[:, IDENT - 1024:IDENT - 1024 + 128]), ["cst"], ["identb"])
    vop(lambda: V.memset(onesb[:], 1.0), [], ["onesb"])
    vop(lambda: V.memset(onesb[:, 64:128], 0.0), [], ["onesb"])
    vop(lambda: V.memset(onesf_t[:], 1.0), [], ["onesf"])
    vop(lambda: V.memset(epsc[:], EPS), [], ["epsc"])
    vop(lambda: V.memset(vaug[:], 0.0), [], ["vaug"])
    vop(lambda: V.memset(kT2[:], 0.0), [], ["kT2"])
    vop(lambda: V.memset(BA[:], 0.0), [], ["BA"])
    vop(lambda: V.memset(BB[:], 0.0), [], ["BB"])
    vop(lambda: V.memset(C1[:], 0.0), [], ["C1"])
    vop(lambda: V.memset(C2[:], 0.0), [], ["C2"])

    wslot = [0]

    def wslot_next():
        i = wslot[0] % 4
        wslot[0] += 1
        return i, B("wgu%d" % i)

    def load_w(dram_view):
        i, wb = wslot_next()
        S.dma("pool", wgu[:, i, :, :], dram_view, writes=[wb])
        return wb, wgu[:, i, :, :]

    def colview(wl, c0, n=128):
        assert c0 % 128 == 0 and n == 128
        return wl[c0 // 128].rearrange("p (kt c) -> p kt c", c=128)

    def rmsnorm(gain_ap, gbuf, src, srcbuf, dst, dstbuf, nk=KD, ncol=TT):
        pb, pa = PS()
        for k in range(nk):
            sl = k % 2
            S.op("act", lambda: A.activation(out=sq[:, sl, 0:ncol], in_=src[:, k, :], func=AF.Square),
                 reads=[srcbuf], writes=[B("sq%d" % sl)])
            S.op("pe", lambda: nc.tensor.matmul(pa[:, 0:ncol], onesf, sq[:, sl, 0:ncol], start=(k == 0), stop=(k == nk - 1)),
                 reads=[B("sq%d" % sl), B("onesf")], writes=[pb])
        S.op("act", lambda: A.activation(out=rs[:, 0:ncol], in_=pa[:, 0:ncol], func=AF.Sqrt, bias=epsc[:, 0:1],
                                         scale=1.0 / D), reads=[pb, B("epsc")], writes=[B("rs")])
        S.op("dve", lambda: V.reciprocal(rs[:, 0:ncol], rs[:, 0:ncol]), reads=[B("rs")], writes=[B("rs")])
        for k in range(nk):
            S.op("dve", lambda: V.scalar_tensor_tensor(out=dst[:, k, :], in0=src[:, k, :], scalar=gain_ap[:, k:k + 1],
                                                       in1=rs[:, 0:ncol], op0=ALU.mult, op1=ALU.mult),
                 reads=[srcbuf, B("rs"), gbuf], writes=[dstbuf])

    def ffn(l, w_in_name, w_out_name, which):
        win = W[w_in_name][l]
        wout = W[w_out_name][l].rearrange("(j p) n -> p j n", p=128)
        rmsnorm(nrm[:, l, which, :], B("nrm"), hT, B("hT"), uT, B("uT"))
        with scope(("hid", [128, 2, 4, TT], BF16), ("sg", [128, 2, TT], F32), ("wo", [128, 2, 4, D], BF16)) as (hid, sg, wo):
            for c in range(NFF // 4):
                hs_ = c % 2
                wob = B("wo%d" % hs_)
                S.dma("pool", wo[:, hs_, :, :], wout[:, 4 * c:4 * c + 4, :], writes=[wob])
                woa = wo[:, hs_, :, :]
                for jj in range(4):
                    j = 4 * c + jj
                    wgb, wga = load_w(colview(win, j * 128))
                    wub, wua = load_w(colview(win, DFF + j * 128))
                    pgb, pg = PS()
                    pub, pu = PS()

                    def mm_g():
                        last = None
                        for k in range(KD):
                            last = nc.tensor.matmul(pg, wga[:, k, :], uT[:, k, :], start=(k == 0), stop=(k == KD - 1))
                        return last

                    def mm_u():
                        last = None
                        for k in range(KD):
                            last = nc.tensor.matmul(pu, wua[:, k, :], uT[:, k, :], start=(k == 0), stop=(k == KD - 1))
                        return last
                    S.op("pe", mm_g, reads=[wgb, B("uT")], writes=[pgb])
                    S.op("pe", mm_u, reads=[wub, B("uT")], writes=[pub])
                    sl = j % 2
                    S.op("act", lambda: A.activation(out=sg[:, sl, :], in_=pg, func=AF.Silu), reads=[pgb], writes=[B("sg%d" % sl)])
                    S.op("dve", lambda: V.tensor_tensor(out=hid[:, hs_, jj, :], in0=pu, in1=sg[:, sl, :], op=ALU.mult),
                         reads=[pub, B("sg%d" % sl)], writes=[B("hid%d" % hs_)])
                for i in range(KD):
                    pob, po = PS()

                    def mm_o():
                        last = None
                        for jj in range(4):
                            last = nc.tensor.matmul(po, woa[:, jj, i * 128:(i + 1) * 128], hid[:, hs_, jj, :], start=(jj == 0),
                                                    stop=(jj == 3))
                        return last
                    S.op("pe", mm_o, reads=[wob, B("hid%d" % hs_)], writes=[pob])
                    S.op("dve", lambda: V.scalar_tensor_tensor(out=hT[:, i, :], in0=po, scalar=0.5, in1=hT[:, i, :],
                                                               op0=ALU.mult, op1=ALU.add), reads=[pob], writes=[B("hT")])

    def mem_prep(l):
        with scope(("memf", [128, KD, NMEM], F32), ("memn", [128, KD, NMEM], BF16)) as (memf, memn):
            S.dma("sp", memf[:], memT.rearrange("(kt p) n -> p kt n", p=128), writes=[B("memf")])
            rmsnorm(nrm[:, l, 2, :], B("nrm"), memf, B("memf"), memn, B("memn"), ncol=NMEM)
            wk = W["w_mem_kv"][l]
            for i in range(8):
                wb, wa = load_w(colview(wk, i * 128))
                pb, pa = PS()

                def mm():
                    last = None
                    for k in range(KD):
                        last = nc.tensor.matmul(pa[:, 0:NMEM], wa[:, k, :], memn[:, k, :], start=(k == 0), stop=(k == KD - 1))
                    return last
                S.op("pe", mm, reads=[wb, B("memn")], writes=[pb])
                S.op("act", lambda: A.copy(mkT[:, i, :], pa[:, 0:NMEM]), reads=[pb], writes=[B("mkT")])
            for i in range(8):
                wb, wa = load_w(colview(wk, 1024 + i * 128))
                for mt in range(2):
                    pb, pa = PS()

                    def mm():
                        last = None
                        for k in range(KD):
                            last = nc.tensor.matmul(pa[:, 0:128], memn[:, k, mt * 128:(mt + 1) * 128], wa[:, k, :], start=(k == 0),
                                                    stop=(k == KD - 1))
                        return last
                    S.op("pe", mm, reads=[wb, B("memn")], writes=[pb])
                    S.op("dve", lambda: V.tensor_copy(mv[:, mt, i * 128:(i + 1) * 128], pa[:, 0:128]), reads=[pb], writes=[B("mv")])

    def disc(prm, tt, R, Wt):
        vop(lambda: V.tensor_scalar_min(out=tt[0], in0=prm[0], scalar1=-1e-4), R, Wt)
        vop(lambda: A.activation(out=tt[1], in_=prm[2], func=AF.Exp), R, Wt, e="act")
        vop(lambda: V.tensor_tensor(out=tt[8], in0=tt[0], in1=tt[1], op=ALU.mult), Wt, Wt)
        vop(lambda: A.activation(out=tt[2], in_=tt[8], func=AF.Exp), Wt, Wt, e="act")
        vop(lambda: V.scalar_tensor_tensor(out=tt[3], in0=prm[1], scalar=1.0 / (2 * math.pi), in1=tt[1], op0=ALU.mult,
                                           op1=ALU.mult), R + Wt, Wt)
        sincos(tt[3], tt[5], tt[4], tt[8], tt[9], Wt)

    def sincos(f, s_out, c_out, t1, t2, Wt):
        vop(lambda: V.tensor_scalar(out=t1, in0=f, scalar1=MAGIC, scalar2=MAGIC, op0=ALU.add, op1=ALU.subtract), Wt, Wt)
        vop(lambda: V.tensor_tensor(out=t1, in0=f, in1=t1, op=ALU.subtract), Wt, Wt)
        vop(lambda: A.activation(out=s_out, in_=t1, func=AF.Sin, scale=2 * math.pi), Wt, Wt, e="act")
        vop(lambda: V.tensor_scalar_add(out=t2, in0=f, scalar1=0.25), Wt, Wt)
        vop(lambda: V.tensor_scalar(out=t1, in0=t2, scalar1=MAGIC, scalar2=MAGIC, op0=ALU.add, op1=ALU.subtract), Wt, Wt)
        vop(lambda: V.tensor_tensor(out=t1, in0=t2, in1=t1, op=ALU.subtract), Wt, Wt)
        vop(lambda: A.activation(out=c_out, in_=t1, func=AF.Sin, scale=2 * math.pi), Wt, Wt, e="act")

    def ssm_prep(l):
        with scope(("prmB", [128, 5, 8, 64], F32), ("tB", [128, 10, 8, 64], F32), ("cc", [128, 2, 64, 16], F32),
                   ("targ", [128, 2, TT], F32), ("tabw", [128, 2, 2, TT], F32)) as (prmB, tB, cc, targ, tabw):
            S.dma("sp", prmB[:], ssmB[:, l, :, :, :], writes=[B("Bp")])
            S.dma("sp", prmS[:], ssmS[:, l, :, :], writes=[B("Sp")])
            S.dma("sp", cc[:], ssmC[:, l, :, :, :], writes=[B("cc")])
            S.dma("sp", dsk[:], dskip[:, l, :], writes=[B("dsk")])
            pB = [prmB[:, i, :, :] for i in range(5)]
            tb = [tB[:, i, :, :] for i in range(10)]
            R, Wt = ["Bp", "Bt"], ["Bt"]
            disc(pB, tb, ["Bp"], ["Bt"])
            vop(lambda: V.tensor_tensor(out=tb[4], in0=tb[4], in1=tb[2], op=ALU.mult), Wt, Wt)
            vop(lambda: V.tensor_tensor(out=tb[5], in0=tb[5], in1=tb[2], op=ALU.mult), Wt, Wt)
            vop(lambda: V.tensor_scalar_add(out=tb[4], in0=tb[4], scalar1=-1.0), Wt, Wt)
            vop(lambda: V.tensor_tensor(out=tb[8], in0=tb[0], in1=tb[0], op=ALU.mult), Wt, Wt)
            vop(lambda: V.tensor_tensor(out=tb[9], in0=pB[1], in1=pB[1], op=ALU.mult), R, Wt)
            vop(lambda: V.tensor_tensor(out=tb[8], in0=tb[8], in1=tb[9], op=ALU.add), Wt, Wt)
            vop(lambda: V.reciprocal(tb[8], tb[8]), Wt, Wt)
            vop(lambda: V.tensor_tensor(out=tb[6], in0=tb[4], in1=tb[0], op=ALU.mult), Wt, Wt)
            vop(lambda: V.tensor_tensor(out=tb[9], in0=tb[5], in1=pB[1], op=ALU.mult), R, Wt)
            vop(lambda: V.tensor_tensor(out=tb[6], in0=tb[6], in1=tb[9], op=ALU.add), Wt, Wt)
            vop(lambda: V.tensor_tensor(out=tb[6], in0=tb[6], in1=tb[8], op=ALU.mult), Wt, Wt)
            vop(lambda: V.tensor_tensor(out=tb[7], in0=tb[5], in1=tb[0], op=ALU.mult), Wt, Wt)
            vop(lambda: V.tensor_tensor(out=tb[9], in0=tb[4], in1=pB[1], op=ALU.mult), R, Wt)
            vop(lambda: V.tensor_tensor(out=tb[7], in0=tb[7], in1=tb[9], op=ALU.subtract), Wt, Wt)
            vop(lambda: V.tensor_tensor(out=tb[7], in0=tb[7], in1=tb[8], op=ALU.mult), Wt, Wt)
            vop(lambda: V.tensor_tensor(out=tb[2], in0=tb[6], in1=pB[3], op=ALU.mult), R, Wt)
            vop(lambda: V.tensor_tensor(out=tb[9], in0=tb[7], in1=pB[4], op=ALU.mult), R, Wt)
            vop(lambda: V.tensor_tensor(out=tb[2], in0=tb[2], in1=tb[9], op=ALU.subtract), Wt, Wt)
            vop(lambda: V.tensor_tensor(out=tb[3], in0=tb[6], in1=pB[4], op=ALU.mult), R, Wt)
            vop(lambda: V.tensor_tensor(out=tb[9], in0=tb[7], in1=pB[3], op=ALU.mult), R, Wt)
            vop(lambda: V.tensor_tensor(out=tb[3], in0=tb[3], in1=tb[9], op=ALU.add), Wt, Wt)
            vop(lambda: V.tensor_scalar_mul(out=tb[9], in0=tb[2], scalar1=-1.0), Wt, Wt)
            for v in range(4):
                qm = cst[:, QM - 1024 + v:QM - 1024 + v + 1]
                vop(lambda: V.tensor_scalar(out=BA[:, :, v, 0:64], in0=tb[2], scalar1=qm, scalar2=None, op0=ALU.mult), ["Bt", "cst"], ["BA"])
                vop(lambda: V.tensor_scalar(out=BA[:, :, v, 64:128], in0=tb[3], scalar1=qm, scalar2=None, op0=ALU.mult), ["Bt", "cst"], ["BA"])
                vop(lambda: V.tensor_scalar(out=BB[:, :, v, 0:64], in0=tb[3], scalar1=qm, scalar2=None, op0=ALU.mult), ["Bt", "cst"], ["BB"])
                vop(lambda: V.tensor_scalar(out=BB[:, :, v, 64:128], in0=tb[9], scalar1=qm, scalar2=None, op0=ALU.mult), ["Bt", "cst"], ["BB"])
            pS = [prmS[:, i, :] for i in range(3)]
            ts_ = [tS[:, i, :] for i in range(10)]
            disc(pS, ts_, ["Sp"], ["St"])
            Wt = ["St"]
            vop(lambda: V.tensor_scalar_mul(out=ts_[6], in0=ts_[3], scalar1=float(TT)), Wt, Wt)
            sincos(ts_[6], ts_[5], ts_[4], ts_[8], ts_[9], Wt)
            vop(lambda: V.tensor_scalar(out=ts_[5], in0=ts_[5], scalar1=cst[:, QM - 1024 + 4:QM - 1024 + 5], scalar2=None, op0=ALU.mult), ["St", "cst"], Wt)
            for v in range(4):
                vop(lambda: V.tensor_scalar(out=C1[:, v::4, 16 * v:16 * v + 16], in0=cc[:, 0, v::4, :], scalar1=cst[:, QM - 1024 + 5:QM - 1024 + 6],
                                            scalar2=None, op0=ALU.mult), ["cc", "cst"], ["C1"])
                vop(lambda: V.tensor_scalar_mul(out=C2[:, v::4, 16 * v:16 * v + 16], in0=cc[:, 1, v::4, :], scalar1=-1.0), ["cc"], ["C2"])
            for g in range(64):
                sl = g % 2
                fcol = tS[:, 3, g:g + 1]
                tw = ["targ"]
                vop(lambda: V.tensor_scalar(out=targ[:, 0, :], in0=cst[:, IOTA - 1024:IOTA - 1024 + TT], scalar1=fcol, scalar2=None, op0=ALU.mult),
                    ["St", "cst", "targ"], tw)
                vop(lambda: V.tensor_scalar(out=targ[:, 1, :], in0=targ[:, 0, :], scalar1=MAGIC, scalar2=MAGIC, op0=ALU.add,
                                            op1=ALU.subtract), tw, tw)
                vop(lambda: V.tensor_tensor(out=targ[:, 1, :], in0=targ[:, 0, :], in1=targ[:, 1, :], op=ALU.subtract), tw, tw)
                vop(lambda: A.activation(out=tabw[:, sl, 1, :], in_=targ[:, 1, :], func=AF.Sin, scale=2 * math.pi), tw,
                    ["tabw%d" % sl], e="act")
                vop(lambda: V.tensor_scalar_add(out=targ[:, 0, :], in0=targ[:, 0, :], scalar1=0.25), tw, tw)
                vop(lambda: V.tensor_scalar(out=targ[:, 1, :], in0=targ[:, 0, :], scalar1=MAGIC, scalar2=MAGIC, op0=ALU.add,
                                            op1=ALU.subtract), tw, tw)
                vop(lambda: V.tensor_tensor(out=targ[:, 1, :], in0=targ[:, 0, :], in1=targ[:, 1, :], op=ALU.subtract), tw, tw)
                vop(lambda: A.activation(out=tabw[:, sl, 0, :], in_=targ[:, 1, :], func=AF.Sin, scale=2 * math.pi), tw,
                    ["tabw%d" % sl], e="act")
                S.dma("sp", tabs.ap()[g], tabw[:, sl, :, :], reads=[B("tabw%d" % sl)], writes=[B("tabs")], owner=B("tabs"))
            vop(lambda: V.memset(state[:], 0.0), [], ["state"])

    def proj_fm(wl, c0, dst_ap, dstbuf, evac, dst2=None):
        wb, wa = load_w(colview(wl, c0))
        pb, pa = PS()

        def mm():
            last = None
            for k in range(KD):
                last = nc.tensor.matmul(pa, wa[:, k, :], uT[:, k, :], start=(k == 0), stop=(k == KD - 1))
            return last
        S.op("pe", mm, reads=[wb, B("uT")], writes=[pb])
        if evac == "act":
            S.op("act", lambda: A.copy(dst_ap, pa), reads=[pb], writes=[dstbuf])
        else:
            S.op("dve", lambda: V.tensor_copy(dst_ap, pa), reads=[pb], writes=[dstbuf])
        if dst2 is not None:
            S.op("dve", lambda: V.tensor_scalar(out=dst2[0], in0=pa, scalar1=1.0, scalar2=None, op0=ALU.mult), reads=[pb], writes=[dst2[1]])

    def dump(idx, src, t):
        S.dma("pool", dbg_o[idx, :, t * TT:(t + 1) * TT].rearrange("(kt p) n -> p kt n", p=128), src[:],
              reads=[B("qT%d" % q_) for q_ in range(8)] + [B("mqT%d" % q_) for q_ in range(8)] + [B("sinT")], writes=[B("dbgo")], owner=B("dbgo"))

    def mixer(l, t, seq_start):
        wl = W["w_in"][l]
        rmsnorm(nrm[:, l, 1, :], B("nrm"), hT, B("hT"), uT, B("uT"))
        with scope(("qT", [128, 8, TT], BF16), ("mqT", [128, 8, TT], BF16), ("sinT", [128, 8, TT], BF16),
                   ("sinF", [128, 8, TT], F32)) as (qT, mqT, sinT, sinF):
            with scope(("wk", [128, KD, 128], BF16)) as (wk,):
                for i in range(8):
                    proj_fm(wl, 1536 + i * 128, sinT[:, i, :], B("sinT"), "act", dst2=(sinF[:, i, :], B("sinF")))
                for gp in range(2):
                    S.dma("pool", wk[:], colview(wl, 1024 + gp * 128), writes=[B("wk")])
                    for gg in range(2):
                        g = 2 * gp + gg
                        for x in range(2):
                            i, wb = wslot_next()
                            S.op("dve", lambda: V.memset(wgu[:, i, :, (1 - x) * 64:(1 - x) * 64 + 64], 0.0), writes=[wb])
                            S.op("dve", lambda: V.tensor_scalar(out=wgu[:, i, :, x * 64:x * 64 + 64], in0=wk[:, :, gg * 64:(gg + 1) * 64],
                                                                scalar1=1.0, scalar2=None, op0=ALU.mult), reads=[B("wk")], writes=[wb])
                            pb, pa = PS()

                            def mm():
                                last = None
                                for k in range(KD):
                                    last = nc.tensor.matmul(pa, wgu[:, i, k, :], uT[:, k, :], start=(k == 0), stop=(k == KD - 1))
                                return last
                            S.op("pe", mm, reads=[wb, B("uT")], writes=[pb])
                            S.op("act", lambda: A.copy(kT2[:, x, g, 128:128 + TT], pa), reads=[pb], writes=[B("kT2")])
            items = []

            def proj_run(w, dst_ap, dstbuf):
                wb, wa = w
                pb, pa = PS()

                def mm():
                    last = None
                    for k in range(KD):
                        last = nc.tensor.matmul(pa, wa[:, k, :], uT[:, k, :], start=(k == 0), stop=(k == KD - 1))
                    return last
                S.op("pe", mm, reads=[wb, B("uT")], writes=[pb])
                S.op("act", lambda: A.copy(dst_ap, pa), reads=[pb], writes=[dstbuf])
            for i in range(8):
                items.append([lambda i=i: load_w(colview(wl, i * 128)), lambda w, i=i: proj_run(w, qT[:, i, :], B("qT%d" % i)), None])
            for i in range(8):
                items.append([lambda i=i: load_w(colview(wl, 2560 + i * 128)), lambda w, i=i: proj_run(w, mqT[:, i, :], B("mqT%d" % i)), None])
            vst = {}

            def v_load():
                vst["i0"], vst["b0"] = wslot_next()
                S.dma("pool", wgu[:, vst["i0"], :, :], colview(wl, 1280, 128), writes=[vst["b0"]])
                vst["i1"], vst["b1"] = wslot_next()
                S.dma("pool", wgu[:, vst["i1"], :, :], colview(wl, 1408, 128), writes=[vst["b1"]])

            def v_bq(bq):
                pb, pa = PS()

                def mm():
                    last = None
                    for half, wi in ((0, vst["i0"]), (1, vst["i1"])):
                        for k in range(KD):
                            last = nc.tensor.matmul(pa[:, half * 128:(half + 1) * 128], uT[:, k, bq * 128:(bq + 1) * 128],
                                                    wgu[:, wi, k, :], start=(k == 0), stop=(k == KD - 1))
                    return last
                S.op("pe", mm, reads=[vst["b0"], vst["b1"], B("uT")], writes=[pb])
                pv = pa[:, 0:256].rearrange("p (g d) -> p g d", g=4)
                S.op("act", lambda: A.copy(vaug[:, 1 + bq, :, 0:64], pv), reads=[pb], writes=[B("vaug")])
                S.op("act", lambda: A.copy(vaug[:, 1 + bq, :, 128:192], pv), reads=[pb], writes=[B("vaug")])
            items.append([v_load, lambda w: None, None])
            for bq in range(4):
                items.append([None, lambda w, bq=bq: v_bq(bq), None])
            nload, nrun = [0], [0]

            def issue_loads(upto):
                while nload[0] < min(upto, len(items)):
                    it_ = items[nload[0]]
                    if it_[0] is not None:
                        it_[2] = it_[0]()
                    nload[0] += 1

            def run_next():
                k = nrun[0]
                issue_loads(k + 3)
                items[k][1](items[k][2])
                nrun[0] += 1
            with scope(("tab", [128, 4, 2, TT], F32), ("xm", [128, 2, TT], F32), ("d12", [128, 2, 2, TT], BF16),
                       ("xs", [128, 2, TT], F32)) as (tab, xm, d12, xs):
                grp = {}
                ycur = [None]
                issue_loads(2)

                def geo(g):
                    j, m = g // 8, g % 8
                    return j, m // 4, m % 4, 64 * (m // 4), g % 2, g % 4

                def stA(g):
                    j, qd, v, r0, sl, ts4 = geo(g)
                    tb_ = B("tab%d" % ts4)
                    S.dma("sp", tab[:, ts4, :, :], tabs.ap()[g], reads=[B("tabs")], writes=[tb_], owner=tb_)
                    pab, pa = PS()
                    pbb, pb_ = PS()
                    S.op("pe", lambda: nc.tensor.matmul(pa, BA[r0:r0 + 64, j, v, :], sinT[r0:r0 + 64, j, :], start=True, stop=True),
                         reads=[B("BA"), B("sinT")], writes=[pab])
                    S.op("pe", lambda: nc.tensor.matmul(pb_, BB[r0:r0 + 64, j, v, :], sinT[r0:r0 + 64, j, :], start=True, stop=True),
                         reads=[B("BB"), B("sinT")], writes=[pbb])
                    grp[g] = (pab, pa, pbb, pb_)

                def stB(g):
                    j, qd, v, r0, sl, ts4 = geo(g)
                    tb_ = B("tab%d" % ts4)
                    pab, pa, pbb, pb_ = grp.pop(g)
                    S.op("dve", lambda: V.tensor_tensor(out=xm[:, 1, :], in0=pb_, in1=tab[:, ts4, 1, :], op=ALU.mult), reads=[pbb, tb_],
                         writes=[B("xm1")])
                    S.op("dve", lambda: V.tensor_tensor(out=pa, in0=pa, in1=tab[:, ts4, 0, :], op=ALU.mult), reads=[tb_], writes=[pab])
                    S.op("dve", lambda: V.tensor_tensor(out=xm[:, 0, :], in0=pa, in1=xm[:, 1, :], op=ALU.add), reads=[pab, B("xm1")],
                         writes=[B("xm0")])
                    S.op("dve", lambda: V.tensor_tensor_scan(out=xs[:, sl, :], data0=tS[:, 2, g:g + 1].to_broadcast([128, TT]),
                                                             data1=xm[:, 0, :], initial=state[:, g:g + 1], op0=ALU.mult, op1=ALU.add),
                         reads=[B("St"), B("xm0"), B("state")], writes=[B("xs%d" % sl)])
                    S.op("act", lambda: A.copy(slast[:, g:g + 1], xs[:, sl, TT - 1:TT]), reads=[B("xs%d" % sl)], writes=[B("slast")])

                def stC(g):
                    j, qd, v, r0, sl, ts4 = geo(g)
                    tb_ = B("tab%d" % ts4)
                    S.op("pool", lambda: nc.gpsimd.tensor_tensor(out=d12[:, sl, 0, :], in0=xs[:, sl, :], in1=tab[:, ts4, 0, :], op=ALU.mult),
                         reads=[B("xs%d" % sl), tb_], writes=[B("d12%d" % sl)])
                    S.op("pool", lambda: nc.gpsimd.tensor_tensor(out=d12[:, sl, 1, :], in0=xs[:, sl, :], in1=tab[:, ts4, 1, :], op=ALU.mult),
                         reads=[B("xs%d" % sl), tb_], writes=[B("d12%d" % sl)])

                def stD(g):
                    j, qd, v, r0, sl, ts4 = geo(g)
                    if v == 0:
                        ycur[0] = PSL()
                    pyb, py = ycur[0]

                    def mm_y():
                        nc.tensor.matmul(py[r0:r0 + 64, :], C1[:, g, :], d12[:, sl, 0, :], start=(v == 0), stop=False)
                        return nc.tensor.matmul(py[r0:r0 + 64, :], C2[:, g, :], d12[:, sl, 1, :], start=False, stop=(v == 3))
                    S.op("pe", mm_y, reads=[B("C1"), B("C2"), B("d12%d" % sl)], writes=[pyb])
                    if v == 3:
                        S.op("dve", lambda: V.scalar_tensor_tensor(out=sinT[r0:r0 + 64, j, :], in0=sinF[r0:r0 + 64, j, :],
                                                                   scalar=dsk[r0:r0 + 64, j:j + 1], in1=py[r0:r0 + 64, :],
                                                                   op0=ALU.mult, op1=ALU.add),
                             reads=[B("sinF"), B("dsk"), pyb], writes=[B("sinT")])

                for it in range(64 + 2):
                    if it < 64:
                        stA(it)
                    if 0 <= it - 1 < 64:
                        stB(it - 1)
                        stC(it - 1)
                    if 0 <= it - 2 < 64:
                        stD(it - 2)
                    if it % 3 == 2 and nrun[0] < len(items):
                        run_next()
                while nrun[0] < len(items):
                    run_next()
                pb, pa = PS()
                S.op("pe", lambda: nc.tensor.matmul(pa[:, 0:64], swapf[:], slast[:], start=True, stop=True), reads=[B("slast"), B("swapf")],
                     writes=[pb])
                vop(lambda: V.tensor_tensor(out=state[:], in0=slast[:], in1=tS[:, 4, :], op=ALU.mult), ["slast", "St"], ["state"])
                S.op("dve", lambda: V.tensor_tensor(out=slast[:], in0=pa[:, 0:64], in1=tS[:, 5, :], op=ALU.mult), reads=[pb, B("St")],
                     writes=[B("slast")])
                vop(lambda: V.tensor_tensor(out=state[:], in0=state[:], in1=slast[:], op=ALU.add), ["slast"], ["state"])
            with scope(("pT", [128, 2, TT], BF16), ("rden", [128, 2, TT], F32)) as (pT, rden):
                S.op("act", lambda: A.activation(out=esk[:], in_=skc[:, l, :], func=AF.Exp), reads=[B("skc")], writes=[B("esk")])
                units = [(bq, g, hp) for bq in range(4) for g in range(4) for hp in range(2)]

                def swa_s(u):
                    bq, g, hp = units[u]
                    psb_, ps_ = PS()
                    m0 = MASK_FIRST if (seq_start and bq == 0) else MASK_ROW
                    qt = 2 * g + hp
                    sl = u % 2

                    def mm_s():
                        nc.tensor.matmul(ps_, identb[:], maskb[:, m0:m0 + 512], start=True, stop=False)
                        last = None
                        for kb in range(2):
                            for r in range(2):
                                c0 = kb * 256 + r * 128
                                last = nc.tensor.matmul(ps_[:, c0:c0 + 128], kT2[:, r, g, (bq + kb) * 128:(bq + kb + 1) * 128],
                                                        qT[:, qt, bq * 128:(bq + 1) * 128], start=False, stop=(kb == 1 and r == 1))
                        return last
                    S.op("pe", mm_s, reads=[B("identb"), B("maskb"), B("kT2"), B("qT%d" % qt)], writes=[psb_])
                    S.op("act", lambda: A.activation(out=pT[:, sl, :], in_=ps_, func=AF.Exp, scale=0.125), reads=[psb_],
                         writes=[B("pT%d" % sl)])

                def swa_o(u):
                    bq, g, hp = units[u]
                    qt = 2 * g + hp
                    sl = u % 2
                    pob, po = PS()
                    pdb, pd = PS()

                    def mm_o():
                        n = 0
                        for kb in range(2):
                            for r in range(2):
                                lo = 0 if r == 0 else 64
                                c0 = kb * 256 + r * 128
                                nc.tensor.matmul(po[:, 0:128], vaug[:, bq + kb, g, lo:lo + 128], pT[:, sl, c0:c0 + 128],
                                                 start=(n == 0), stop=(n == 3))
                                n += 1
                        n = 0
                        last = None
                        for kb in range(2):
                            for r in range(2):
                                lo = 0 if r == 0 else 64
                                c0 = kb * 256 + r * 128
                                last = nc.tensor.matmul(pd[:, 0:128], onesb[:, lo:lo + 128], pT[:, sl, c0:c0 + 128],
                                                        start=(n == 0), stop=(n == 3))
                                n += 1
                        return last
                    S.op("pe", mm_o, reads=[B("vaug"), B("onesb"), B("pT%d" % sl)], writes=[pob, pdb])
                    idx = g * 2 + hp
                    S.op("dve", lambda: V.tensor_scalar(out=rden[:, sl, 0:128], in0=pd[:, 0:128], scalar1=esk[:, idx:idx + 1],
                                                        scalar2=None, op0=ALU.add), reads=[pdb, B("esk")], writes=[B("rden%d" % sl)])
                    S.op("dve", lambda: V.reciprocal(rden[:, sl, 0:128], rden[:, sl, 0:128]), reads=[B("rden%d" % sl)],
                         writes=[B("rden%d" % sl)])
                    S.op("dve", lambda: V.tensor_tensor(out=qT[:, qt, bq * 128:(bq + 1) * 128], in0=po[:, 0:128],
                                                        in1=rden[:, sl, 0:128], op=ALU.mult),
                         reads=[pob, B("rden%d" % sl)], writes=[B("qT%d" % qt)])

                for it in range(len(units) + 1):
                    if it < len(units):
                        swa_s(it)
                    if it >= 1:
                        swa_o(it - 1)
                if stop == 43:
                    return True
                vop(lambda: V.tensor_scalar(out=kT2[:, 0, :, 0:128], in0=kT2[:, 0, :, TT:TT + 128], scalar1=1.0, scalar2=None, op0=ALU.mult), ["kT2"], ["kT2"])
                vop(lambda: V.tensor_scalar(out=kT2[:, 1, :, 0:128], in0=kT2[:, 1, :, TT:TT + 128], scalar1=1.0, scalar2=None, op0=ALU.mult), ["kT2"], ["kT2"])
                vop(lambda: V.tensor_scalar(out=vaug[:, 0, :, :], in0=vaug[:, 4, :, :], scalar1=1.0, scalar2=None, op0=ALU.mult), ["vaug"], ["vaug"])
                for hm in range(4):
                    for mt in range(2):
                        pb, pa = PS()

                        def mm():
                            last = None
                            for dt_ in range(2):
                                last = nc.tensor.matmul(pa, mkT[:, 2 * hm + dt_, mt * 128:(mt + 1) * 128], mqT[:, 2 * hm + dt_, :],
                                                        start=(dt_ == 0), stop=(dt_ == 1))
                            return last
                        S.op("pe", mm, reads=[B("mkT"), B("mqT%d" % (2 * hm)), B("mqT%d" % (2 * hm + 1))], writes=[pb])
                        S.op("act", lambda: A.activation(out=pT[:, mt, :], in_=pa, func=AF.Exp, scale=1.0 / 16.0), reads=[pb],
                             writes=[B("pT%d" % mt)])
                    pdb, pd = PS()

                    def mm_d():
                        last = None
                        for mt in range(2):
                            last = nc.tensor.matmul(pd, onesf, pT[:, mt, :], start=(mt == 0), stop=(mt == 1))
                        return last
                    S.op("pe", mm_d, reads=[B("onesf"), B("pT0"), B("pT1")], writes=[pdb])
                    S.op("dve", lambda: V.reciprocal(rden[:, 0, :], pd), reads=[pdb], writes=[B("rden0")])
                    for dt_ in range(2):
                        pob, po = PS()

                        def mm_o():
                            last = None
                            for mt in range(2):
                                last = nc.tensor.matmul(po, mv[:, mt, (2 * hm + dt_) * 128:(2 * hm + dt_ + 1) * 128], pT[:, mt, :],
                                                        start=(mt == 0), stop=(mt == 1))
                            return last
                        S.op("pe", mm_o, reads=[B("mv"), B("pT0"), B("pT1")], writes=[pob])
                        S.op("dve", lambda: V.tensor_tensor(out=mqT[:, 2 * hm + dt_, :], in0=po, in1=rden[:, 0, :], op=ALU.mult),
                             reads=[pob, B("rden0")], writes=[B("mqT%d" % (2 * hm + dt_))])
            if dbg:
                dump(0, qT, t)
                dump(1, mqT, t)
                dump(2, sinT, t)
            if stop == 5:
                return True
            wglu = W["w_ssm_glu"][l]
            wsw = W["w_swa_up"][l]
            wmu = W["w_mem_up"][l]
            with scope(("mrg", [128, KD, TT], BF16), ("gsb", [128, 2, TT], F32), ("sg2", [128, 2, TT], F32)) as (mrg, gsb, sg2):
                def up(wview, c0, src, srcbuf):
                    i, wb = wslot_next()
                    S.dma("pool", wgu[:, i, 0:8, :], wview[c0 // 128].rearrange("p (kt c) -> p kt c", c=128), writes=[wb])
                    pb, pa = PS()

                    def mm():
                        last = None
                        for k in range(8):
                            last = nc.tensor.matmul(pa, wgu[:, i, k, :], src[:, k, :], start=(k == 0), stop=(k == 7))
                        return last
                    S.op("pe", mm, reads=[wb] + (srcbuf if isinstance(srcbuf, list) else [srcbuf]), writes=[pb])
                    return pb, pa

                def gate(bidx, i):
                    wb, wa = load_w(colview(wl, 3584 + bidx * D + i * 128))
                    pb, pa = PS()

                    def mm():
                        last = None
                        for k in range(KD):
                            last = nc.tensor.matmul(pa, wa[:, k, :], uT[:, k, :], start=(k == 0), stop=(k == KD - 1))
                        return last
                    S.op("pe", mm, reads=[wb, B("uT")], writes=[pb])
                    S.op("act", lambda: A.activation(out=gsb[:, 0, :], in_=pa, func=AF.Sigmoid), reads=[pb], writes=[B("gsb0")])

                for i in range(KD):
                    gate(0, i)
                    pb, pa = up(wsw, i * 128, qT, [B("qT%d" % q_) for q_ in range(8)])
                    S.op("dve", lambda: V.tensor_tensor(out=gsb[:, 1, :], in0=pa, in1=gsb[:, 0, :], op=ALU.mult),
                         reads=[pb, B("gsb0")], writes=[B("gsb1")])
                    gate(1, i)
                    pb2, pa2 = up(wglu, 2048 + i * 128, sinT, B("sinT"))
                    S.op("act", lambda: A.activation(out=sg2[:, 0, :], in_=pa2, func=AF.Sigmoid), reads=[pb2], writes=[B("sg0")])
                    S.op("dve", lambda: V.tensor_tensor(out=sg2[:, 0, :], in0=sg2[:, 0, :], in1=gsb[:, 0, :], op=ALU.mult),
                         reads=[B("sg0"), B("gsb0")], writes=[B("sg0")])
                    pb3, pa3 = up(wglu, i * 128, sinT, B("sinT"))
                    S.op("dve", lambda: V.tensor_tensor(out=sg2[:, 0, :], in0=pa3, in1=sg2[:, 0, :], op=ALU.mult),
                         reads=[pb3, B("sg0")], writes=[B("sg0")])
                    S.op("dve", lambda: V.tensor_tensor(out=gsb[:, 1, :], in0=gsb[:, 1, :], in1=sg2[:, 0, :], op=ALU.add),
                         reads=[B("sg0"), B("gsb1")], writes=[B("gsb1")])
                    gate(2, i)
                    pb4, pa4 = up(wmu, i * 128, mqT, [B("mqT%d" % q_) for q_ in range(8)])
                    S.op("dve", lambda: V.tensor_tensor(out=sg2[:, 1, :], in0=pa4, in1=gsb[:, 0, :], op=ALU.mult),
                         reads=[pb4, B("gsb0")], writes=[B("sg1")])
                    S.op("dve", lambda: V.tensor_tensor(out=mrg[:, i, :], in0=gsb[:, 1, :], in1=sg2[:, 1, :], op=ALU.add),
                         reads=[B("sg1"), B("gsb1")], writes=[B("mrg")])
                wov = W["w_out"][l]
                for i in range(KD):
                    wb, wa = load_w(colview(wov, i * 128))
                    pb, pa = PS()

                    def mm():
                        last = None
                        for k in range(KD):
                            last = nc.tensor.matmul(pa, wa[:, k, :], mrg[:, k, :], start=(k == 0), stop=(k == KD - 1))
                        return last
                    S.op("pe", mm, reads=[wb, B("mrg")], writes=[pb])
                    S.op("dve", lambda: V.tensor_tensor(out=hT[:, i, :], in0=pa, in1=hT[:, i, :], op=ALU.add), reads=[pb],
                         writes=[B("hT")])

    def emit_all():
        for l in range(NL):
            for t in range(NT):
                src = xT if l == 0 else hs.ap()
                S.dma("sp", hT[:], src[:, t * TT:(t + 1) * TT].rearrange("(kt p) n -> p kt n", p=128), writes=[B("hT")],
                      reads=[B("hs")] if l > 0 else [])
                if stop == 0:
                    raise _Stop()
                if t == 0:
                    mem_prep(l)
                    if stop == 1:
                        raise _Stop()
                    ssm_prep(l)
                    if stop == 2:
                        raise _Stop()
                ffn(l, "ffn1_w_in", "ffn1_w_out", 0)
                if stop == 3:
                    raise _Stop()
                if mixer(l, t, seq_start=(t == 0)):
                    raise _Stop()
                ffn(l, "ffn2_w_in", "ffn2_w_out", 3)
                if l < NL - 1:
                    S.dma("sp", hs.ap()[:, t * TT:(t + 1) * TT].rearrange("(kt p) n -> p kt n", p=128), hT[:], reads=[B("hT")],
                          writes=[B("hs")], owner=B("hs"))
                else:
                    with scope(("outF", [128, KD, TT], F32)) as (outF,):
                        rmsnorm(fnrm, B("fnrm"), hT, B("hT"), outF, B("outF"))
                        S.dma("sp", outT[:, t * TT:(t + 1) * TT].rearrange("(kt p) n -> p kt n", p=128), outF[:], reads=[B("outF")],
                              writes=[B("outT")], owner=B("outT"))
    try:
        emit_all()
    except _Stop:
        S.barrier()
        S.dma("sp", outT[:, 0:TT].rearrange("(kt p) n -> p kt n", p=128), hT[:], reads=[B("hT")], writes=[B("outT")], owner=B("outT"))
    S.wait_everything("sp")
    es.close()
    return nc


def _lay_kt(v):
    v = np.asarray(v, np.float32)
    lead = v.shape[:-1]
    return np.ascontiguousarray(np.moveaxis(v.reshape(*lead, KD, 128), -1, 0))


def host_small(NL, ffn1_norm, mix_norm, mem_norm, ffn2_norm, final_norm, sinks, lam_re, lam_im, log_dt, b_re, b_im, c_re, c_im,
               d_skip):
    norms = np.stack([_lay_kt(ffn1_norm), _lay_kt(mix_norm), _lay_kt(mem_norm), _lay_kt(ffn2_norm)], axis=2)
    fnorm = _lay_kt(final_norm)
    p = np.arange(128)
    sk = np.asarray(sinks, np.float32)
    sinkc = np.stack([sk[:, 4 * (i // 2) + 2 * (i % 2) + (p // 64)] for i in range(8)], axis=-1).transpose(1, 0, 2)
    P_, CH = 64, 16

    def blay(a):
        a = np.asarray(a, np.float32).reshape(NL, 8, 8, P_)
        a = np.repeat(a[:, :, :, None, :], CH, axis=3)
        return a.transpose(2, 3, 0, 1, 4).reshape(128, NL, 8, P_)

    def bmat(a):
        a = np.asarray(a, np.float32).reshape(NL, 8, 8, P_, CH)
        return a.transpose(2, 4, 0, 1, 3).reshape(128, NL, 8, P_)
    ldt = np.repeat(np.asarray(log_dt, np.float32)[:, :, None], P_, axis=2)
    ssmB = np.stack([blay(lam_re), blay(lam_im), blay(ldt), bmat(b_re), bmat(b_im)], axis=2)

    def slay(a):
        a = np.asarray(a, np.float32).transpose(2, 0, 1)
        return np.concatenate([a, a], axis=0)
    ssmS = np.stack([slay(lam_re), slay(lam_im), slay(ldt)], axis=2)
    cr = np.asarray(c_re, np.float32).transpose(3, 0, 1, 2)
    ci = np.asarray(c_im, np.float32).transpose(3, 0, 1, 2)
    ssmC = np.stack([np.concatenate([cr, ci], 0), np.concatenate([ci, cr], 0)], axis=2)
    dsk = np.ascontiguousarray(np.asarray(d_skip, np.float32).reshape(NL, 8, 128).transpose(2, 0, 1))
    consts = np.zeros((128, NCONST), np.float32)
    kk = np.arange(128)[:, None]
    qq = np.arange(128)[None, :]
    mprev = np.where(kk > qq, 0.0, -30000.0).astype(np.float32)
    mcur = np.where(kk <= qq, 0.0, -30000.0).astype(np.float32)
    neg = np.full((128, 128), -30000.0, np.float32)
    consts[:, 0:512] = np.concatenate([mprev, mprev, mcur, mcur], 1)
    consts[:, 512:1024] = np.concatenate([neg, neg, mcur, mcur], 1)
    consts[:, IOTA:IOTA + TT] = np.arange(1, TT + 1, dtype=np.float32)[None, :]
    consts[:, IDENT:IDENT + 128] = np.eye(128, dtype=np.float32)
    for v in range(4):
        consts[:, QM + v] = ((p // 16) % 4 == v).astype(np.float32)
    consts[:, QM + 4] = np.where(p < 64, -1.0, 1.0)
    consts[:, QM + 5] = np.where(p < 64, 1.0, -1.0)
    consts[p, SWAPP + (p + 64) % 128] = 1.0
    return {"norms": np.ascontiguousarray(norms), "fnorm": fnorm, "sinkc": np.ascontiguousarray(sinkc),
            "ssmB": np.ascontiguousarray(ssmB), "ssmS": np.ascontiguousarray(ssmS), "ssmC": np.ascontiguousarray(ssmC),
            "dskip": dsk, "consts": consts}


def _tile_lay(w):
    w = np.asarray(w, np.float32)
    NL_, K_, N_ = w.shape
    return np.ascontiguousarray(w.reshape(NL_, K_ // 128, 128, N_ // 128, 128).transpose(0, 3, 2, 1, 4)).reshape(NL_, N_ // 128, 128, K_)


def host_weights(ffn1_w_in, ffn1_w_out, w_in, w_mem_kv, w_ssm_glu, w_swa_up, w_mem_up, w_out, ffn2_w_in, ffn2_w_out):
    f = lambda a: np.ascontiguousarray(np.asarray(a, np.float32))
    return {"ffn1_w_in": _tile_lay(ffn1_w_in), "ffn1_w_out": f(ffn1_w_out), "w_in": _tile_lay(w_in), "w_mem_kv": _tile_lay(w_mem_kv),
            "w_ssm_glu": _tile_lay(w_ssm_glu), "w_swa_up": _tile_lay(w_swa_up), "w_mem_up": _tile_lay(w_mem_up), "w_out": _tile_lay(w_out),
            "ffn2_w_in": _tile_lay(ffn2_w_in), "ffn2_w_out": f(ffn2_w_out)}


def kernel(x, mem, ffn1_norm, ffn1_w_in, ffn1_w_out, mix_norm, mem_norm, w_in, sinks, w_mem_kv, lam_re, lam_im, log_dt,
           b_re, b_im, c_re, c_im, d_skip, w_ssm_glu, w_swa_up, w_mem_up, w_out, ffn2_norm, ffn2_w_in, ffn2_w_out, final_norm):
    x = np.asarray(x, np.float32)
    Bsz, L, _ = x.shape
    NL = np.asarray(ffn1_w_in).shape[0]
    NT = L // TT
    nc = build(NT, NL)
    small = host_small(NL, ffn1_norm, mix_norm, mem_norm, ffn2_norm, final_norm, sinks, lam_re, lam_im, log_dt, b_re, b_im,
                       c_re, c_im, d_skip)
    wts = host_weights(ffn1_w_in, ffn1_w_out, w_in, w_mem_kv, w_ssm_glu, w_swa_up, w_mem_up, w_out, ffn2_w_in, ffn2_w_out)
    memf_ = np.asarray(mem, np.float32)
    in_maps = []
    for b_ in range(Bsz):
        m = {"xT": np.ascontiguousarray(x[b_].T), "memT": np.ascontiguousarray(memf_[b_].T)}
        m.update(wts)
        m.update(small)
        in_maps.append(m)
    res = run_bass_kernel_spmd(nc, in_maps, core_ids=list(range(Bsz)))
    return np.stack([res.results[b_]["outT"].T for b_ in range(Bsz)], axis=0).astype(np.float32)
```

```python
import contextlib
import math
import numpy as np
import concourse.bass as bass
import concourse.mybir as mybir
from concourse.bass_utils import run_bass_kernel_spmd

F32 = mybir.dt.float32
BF16 = mybir.dt.bfloat16
AF = mybir.ActivationFunctionType
ALU = mybir.AluOpType

D = 2048
KD = 16
DFF = 5632
NFF = 44
TT = 512
NMEM = 256
INW = 9728
EPS = 1e-5
MAGIC = 12582912.0
NCONST = 1024 + 512 + 128 + 8 + 128
MASK_ROW, MASK_FIRST, IOTA, IDENT, QM, SWAPP = 0, 512, 1024, 1536, 1664, 1672
SEM_ROLL = 30000


class Buf:
    __slots__ = ("w", "r", "dsem", "dcnt", "dkey", "excl")

    def __init__(self):
        self.excl = False
        self.w = None
        self.r = []
        self.dsem = None
        self.dcnt = 0
        self.dkey = None


class Sched:
    def __init__(self, nc):
        self.nc = nc
        self.eng = {"pe": nc.tensor, "act": nc.scalar, "dve": nc.vector, "pool": nc.gpsimd, "sp": nc.sync}
        self.sem = {}
        self.key = {}
        self.cnt = {}
        self.nsem = 0
        for k in ("pe", "act", "dve", "pool"):
            self._roll(k)
        self.waited = {k: {} for k in self.eng}
        self.bufs = []
        self.old = []

    def _newsem(self):
        self.nsem += 1
        return self.nc.alloc_semaphore("sm%d" % self.nsem), "K%d" % self.nsem

    def _roll(self, k):
        if k in self.sem:
            self.old.append((self.key[k], self.sem[k], self.cnt[k]))
        self.sem[k], self.key[k] = self._newsem()
        self.cnt[k] = 0

    def buf(self):
        b = Buf()
        self.bufs.append(b)
        return b

    def _need(self, e, toks):
        eng = self.eng[e]
        best = {}
        for t in toks:
            if t is None:
                continue
            key, sem, val = t
            if e == "pe" and key == self.key.get("pe"):
                continue
            if best.get(key, (None, 0))[1] < val:
                best[key] = (sem, val)
        for key, (sem, val) in best.items():
            if self.waited[e].get(key, 0) < val:
                eng.wait_ge(sem, val)
                self.waited[e][key] = val

    @staticmethod
    def _deps(reads, writes):
        toks = []
        for b in reads:
            toks.append(b.w)
            if b.excl:
                toks.extend(b.r)
        for b in writes:
            toks.append(b.w)
            toks.extend(b.r)
        return toks

    def op(self, e, fn, reads=(), writes=()):
        self._need(e, self._deps(reads, writes))
        if self.cnt[e] >= SEM_ROLL:
            self._roll(e)
        inst = fn()
        self.cnt[e] += 1
        inst.then_inc(self.sem[e], 1)
        tok = (self.key[e], self.sem[e], self.cnt[e])
        for b in writes:
            b.w = tok
            b.r = []
        for b in reads:
            b.r.append(tok)
            if len(b.r) > 64:
                b.r = self._compact(b.r)
        return tok

    @staticmethod
    def _compact(toks):
        best = {}
        for key, sem, val in toks:
            if best.get(key, (None, 0))[1] < val:
                best[key] = (sem, val)
        return [(k, s, v) for k, (s, v) in best.items()]

    def dma(self, e, out, in_, reads=(), writes=(), owner=None, **kw):
        self._need(e, self._deps(reads, writes))
        b = owner if owner is not None else (writes[0] if writes else reads[0])
        if b.dsem is None or b.dcnt + 16 > SEM_ROLL:
            if b.dsem is not None:
                self.old.append((b.dkey, b.dsem, b.dcnt))
            b.dsem, b.dkey = self._newsem()
            b.dcnt = 0
        b.dcnt += 16
        self.eng[e].dma_start(out=out, in_=in_, **kw).then_inc(b.dsem, 16)
        tok = (b.dkey, b.dsem, b.dcnt)
        for w in writes:
            w.w = tok
            w.r = []
        for r in reads:
            r.r.append(tok)
            if len(r.r) > 64:
                r.r = self._compact(r.r)
        return tok

    def all_tokens(self):
        toks = [(self.key[k], self.sem[k], self.cnt[k]) for k in self.sem if self.cnt[k] > 0]
        toks += [t for t in self.old if t[2] > 0]
        for b in self.bufs:
            if b.dsem is not None and b.dcnt > 0:
                toks.append((b.dkey, b.dsem, b.dcnt))
        return toks

    def barrier(self):
        toks = self.all_tokens()
        for e in self.eng:
            self._need(e, toks)

    def wait_everything(self, e):
        self._need(e, self.all_tokens())


class _Stop(Exception):
    pass


def build(NT, NL, dbg=False, stop=99):
    LT = NT * TT
    nc = bass.Bass("TRN2", target_bir_lowering=False)
    S = Sched(nc)

    def din(name, shape, dt=F32):
        return nc.dram_tensor(name, list(shape), dt, kind="ExternalInput").ap()

    xT = din("xT", [D, LT])
    memT = din("memT", [D, NMEM])
    outT = nc.dram_tensor("outT", [D, LT], F32, kind="ExternalOutput").ap()
    if dbg:
        dbg_o = nc.dram_tensor("dbg_o", [3, 1024, LT], F32, kind="ExternalOutput").ap()
    W = {}
    for nm, shp in [("ffn1_w_in", [NL, 88, 128, 2048]), ("ffn1_w_out", [NL, DFF, D]), ("w_in", [NL, 76, 128, 2048]),
                    ("w_mem_kv", [NL, 16, 128, 2048]), ("w_ssm_glu", [NL, 32, 128, 1024]), ("w_swa_up", [NL, 16, 128, 1024]),
                    ("w_mem_up", [NL, 16, 128, 1024]), ("w_out", [NL, 16, 128, 2048]), ("ffn2_w_in", [NL, 88, 128, 2048]),
                    ("ffn2_w_out", [NL, DFF, D])]:
        W[nm] = din(nm, shp)
    norms = din("norms", [128, NL, 4, KD])
    fnorm = din("fnorm", [128, KD])
    sinkc = din("sinkc", [128, NL, 8])
    ssmB = din("ssmB", [128, NL, 5, 8, 64])
    ssmS = din("ssmS", [128, NL, 3, 64])
    ssmC = din("ssmC", [128, NL, 2, 64, 16])
    dskip = din("dskip", [128, NL, 8])
    consts = din("consts", [128, NCONST])
    hs = nc.dram_tensor("hs", [D, LT], F32)
    tabs = nc.dram_tensor("tabs", [64, 128, 2, TT], F32)

    es = contextlib.ExitStack()

    def sb(name, shape, dt):
        return es.enter_context(nc.sbuf_tensor(name, list(shape), dt))

    hT = sb("hT", [128, KD, TT], F32)
    uT = sb("uT", [128, KD, TT], BF16)
    cst = sb("cst", [128, NCONST - 1024], F32)
    maskb = sb("maskb", [128, 1024], BF16)
    identb = sb("identb", [128, 128], BF16)
    onesb = sb("onesb", [128, 192], BF16)
    onesf_t = sb("onesf", [128, 128], BF16)
    onesf = onesf_t[:]
    epsc = sb("epsc", [128, 1], F32)
    halfpi = sb("halfpi", [128, 1], F32)
    nrm = sb("nrm", [128, NL, 4, KD], F32)
    fnrm = sb("fnrm", [128, KD], F32)
    skc = sb("skc", [128, NL, 8], F32)
    esk = sb("esk", [128, 8], F32)
    sq = sb("sq", [128, 2, TT], BF16)
    rs = sb("rs", [128, TT], F32)
    wgu = sb("wgu", [128, 4, KD, 128], BF16)
    kT2 = sb("kT2", [128, 2, 4, 128 + TT], BF16)
    vaug = sb("vaug", [128, 5, 4, 192], BF16)
    mkT = sb("mkT", [128, 8, NMEM], BF16)
    mv = sb("mv", [128, 2, 1024], BF16)
    BA = sb("BA", [128, 8, 4, 128], BF16)
    BB = sb("BB", [128, 8, 4, 128], BF16)
    C1 = sb("C1", [128, 64, 64], BF16)
    C2 = sb("C2", [128, 64, 64], BF16)
    prmS = sb("prmS", [128, 3, 64], F32)
    tS = sb("tS", [128, 10, 64], F32)
    dsk = sb("dsk", [128, 8], F32)
    state = sb("state", [128, 64], F32)
    slast = sb("slast", [128, 64], F32)
    swapf = sb("swapf", [128, 128], F32)
    psum = es.enter_context(nc.psum_tensor("psum", [128, 8, TT], F32))

    b = {}

    def B(name):
        if name not in b:
            b[name] = S.buf()
        return b[name]

    psb = [S.buf() for _ in range(8)]
    for _b in psb:
        _b.excl = True
    psi = [0]
    psl = [0]

    def PS():
        i = psi[0] % 6
        psi[0] += 1
        return psb[i], psum[:, i, :]

    def PSL():
        i = 6 + psl[0] % 2
        psl[0] += 1
        return psb[i], psum[:, i, :]

    @contextlib.contextmanager
    def scope(*specs):
        with contextlib.ExitStack() as st:
            scope.n += 1
            ts = [st.enter_context(nc.sbuf_tensor("%s_%d" % (n, scope.n), list(s), d)) for n, s, d in specs]
            yield ts
            S.barrier()

    scope.n = 0
    V = nc.vector
    A = nc.scalar

    def vop(fn, reads, writes, e="dve"):
        S.op(e, fn, reads=[B(x) if isinstance(x, str) else x for x in reads],
             writes=[B(x) if isinstance(x, str) else x for x in writes])

    S.dma("sp", cst[:], consts[:, 1024:NCONST], writes=[B("cst")])
    S.dma("pool", maskb[:], consts[:, 0:1024], writes=[B("maskb")])
    S.dma("sp", nrm[:], norms[:, :, :, :], writes=[B("nrm")])
    S.dma("sp", fnrm[:], fnorm[:, :], writes=[B("fnrm")])
    S.dma("sp", skc[:], sinkc[:, :, :], writes=[B("skc")])
    S.dma("sp", swapf[:], consts[:, SWAPP:SWAPP + 128], writes=[B("swapf")])
    vop(lambda: V.tensor_copy(identb[:], cst[:, IDENT - 1024:IDENT - 1024 + 128]), ["cst"], ["identb"])
    vop(lambda: V.memset(onesb[:], 1.0), [], ["onesb"])
    vop(lambda: V.memset(onesb[:, 64:128], 0.0), [], ["onesb"])
    vop(lambda: V.memset(onesf_t[:], 1.0), [], ["onesf"])
    vop(lambda: V.memset(epsc[:], EPS), [], ["epsc"])
    vop(lambda: V.memset(halfpi[:], math.pi / 2), [], ["halfpi"])
    vop(lambda: V.memset(vaug[:], 0.0), [], ["vaug"])
    vop(lambda: V.memset(kT2[:], 0.0), [], ["kT2"])
    vop(lambda: V.memset(BA[:], 0.0), [], ["BA"])
    vop(lambda: V.memset(BB[:], 0.0), [], ["BB"])
    vop(lambda: V.memset(C1[:], 0.0), [], ["C1"])
    vop(lambda: V.memset(C2[:], 0.0), [], ["C2"])

    wslot = [0]

    def wslot_next():
        i = wslot[0] % 4
        wslot[0] += 1
        return i, B("wgu%d" % i)

    def load_w(dram_view):
        i, wb = wslot_next()
        S.dma("pool", wgu[:, i, :, :], dram_view, writes=[wb])
        return wb, wgu[:, i, :, :]

    def colview(wl, c0, n=128):
        assert c0 % 128 == 0 and n == 128
        return wl[c0 // 128].rearrange("p (kt c) -> p kt c", c=128)

    def rmsnorm(gain_ap, gbuf, src, srcbuf, dst, dstbuf, nk=KD, ncol=TT):
        pb, pa = PS()
        for k in range(nk):
            sl = k % 2
            S.op("act", lambda: A.activation(out=sq[:, sl, 0:ncol], in_=src[:, k, :], func=AF.Square),
                 reads=[srcbuf], writes=[B("sq%d" % sl)])
            S.op("pe", lambda: nc.tensor.matmul(pa[:, 0:ncol], onesf, sq[:, sl, 0:ncol], start=(k == 0), stop=(k == nk - 1)),
                 reads=[B("sq%d" % sl), B("onesf")], writes=[pb])
        S.op("act", lambda: A.activation(out=rs[:, 0:ncol], in_=pa[:, 0:ncol], func=AF.Sqrt, bias=epsc[:, 0:1],
                                         scale=1.0 / D), reads=[pb, B("epsc")], writes=[B("rs")])
        S.op("dve", lambda: V.reciprocal(rs[:, 0:ncol], rs[:, 0:ncol]), reads=[B("rs")], writes=[B("rs")])
        for k in range(nk):
            S.op("dve", lambda: V.scalar_tensor_tensor(out=dst[:, k, :], in0=src[:, k, :], scalar=gain_ap[:, k:k + 1],
                                                       in1=rs[:, 0:ncol], op0=ALU.mult, op1=ALU.mult),
                 reads=[srcbuf, B("rs"), gbuf], writes=[dstbuf])

    def ffn(l, w_in_name, w_out_name, which):
        win = W[w_in_name][l]
        wout = W[w_out_name][l].rearrange("(j p) n -> p j n", p=128)
        rmsnorm(nrm[:, l, which, :], B("nrm"), hT, B("hT"), uT, B("uT"))
        with scope(("hid", [128, 2, 4, TT], BF16), ("sg", [128, 2, TT], F32), ("wo", [128, 2, 4, D], BF16)) as (hid, sg, wo):
            for c in range(NFF // 4):
                hs_ = c % 2
                wob = B("wo%d" % hs_)
                S.dma("pool", wo[:, hs_, :, :], wout[:, 4 * c:4 * c + 4, :], writes=[wob])
                woa = wo[:, hs_, :, :]
                for jj in range(4):
                    j = 4 * c + jj
                    wgb, wga = load_w(colview(win, j * 128))
                    wub, wua = load_w(colview(win, DFF + j * 128))
                    pgb, pg = PS()
                    pub, pu = PS()

                    def mm_g():
                        last = None
                        for k in range(KD):
                            last = nc.tensor.matmul(pg, wga[:, k, :], uT[:, k, :], start=(k == 0), stop=(k == KD - 1))
                        return last

                    def mm_u():
                        last = None
                        for k in range(KD):
                            last = nc.tensor.matmul(pu, wua[:, k, :], uT[:, k, :], start=(k == 0), stop=(k == KD - 1))
                        return last
                    S.op("pe", mm_g, reads=[wgb, B("uT")], writes=[pgb])
                    S.op("pe", mm_u, reads=[wub, B("uT")], writes=[pub])
                    sl = j % 2
                    S.op("act", lambda: A.activation(out=sg[:, sl, :], in_=pg, func=AF.Silu), reads=[pgb], writes=[B("sg%d" % sl)])
                    S.op("dve", lambda: V.tensor_tensor(out=hid[:, hs_, jj, :], in0=pu, in1=sg[:, sl, :], op=ALU.mult),
                         reads=[pub, B("sg%d" % sl)], writes=[B("hid%d" % hs_)])
                for i in range(KD):
                    pob, po = PS()

                    def mm_o():
                        last = None
                        for jj in range(4):
                            last = nc.tensor.matmul(po, woa[:, jj, i * 128:(i + 1) * 128], hid[:, hs_, jj, :], start=(jj == 0),
                                                    stop=(jj == 3))
                        return last
                    S.op("pe", mm_o, reads=[wob, B("hid%d" % hs_)], writes=[pob])
                    S.op("dve", lambda: V.scalar_tensor_tensor(out=hT[:, i, :], in0=po, scalar=0.5, in1=hT[:, i, :],
                                                               op0=ALU.mult, op1=ALU.add), reads=[pob], writes=[B("hT")])

    def mem_prep(l):
        with scope(("memf", [128, KD, NMEM], F32), ("memn", [128, KD, NMEM], BF16)) as (memf, memn):
            S.dma("sp", memf[:], memT.rearrange("(kt p) n -> p kt n", p=128), writes=[B("memf")])
            rmsnorm(nrm[:, l, 2, :], B("nrm"), memf, B("memf"), memn, B("memn"), ncol=NMEM)
            wk = W["w_mem_kv"][l]
            for i in range(8):
                wb, wa = load_w(colview(wk, i * 128))
                pb, pa = PS()

                def mm():
                    last = None
                    for k in range(KD):
                        last = nc.tensor.matmul(pa[:, 0:NMEM], wa[:, k, :], memn[:, k, :], start=(k == 0), stop=(k == KD - 1))
                    return last
                S.op("pe", mm, reads=[wb, B("memn")], writes=[pb])
                S.op("act", lambda: A.copy(mkT[:, i, :], pa[:, 0:NMEM]), reads=[pb], writes=[B("mkT")])
            for i in range(8):
                wb, wa = load_w(colview(wk, 1024 + i * 128))
                for mt in range(2):
                    pb, pa = PS()

                    def mm():
                        last = None
                        for k in range(KD):
                            last = nc.tensor.matmul(pa[:, 0:128], memn[:, k, mt * 128:(mt + 1) * 128], wa[:, k, :], start=(k == 0),
                                                    stop=(k == KD - 1))
                        return last
                    S.op("pe", mm, reads=[wb, B("memn")], writes=[pb])
                    S.op("dve", lambda: V.tensor_copy(mv[:, mt, i * 128:(i + 1) * 128], pa[:, 0:128]), reads=[pb], writes=[B("mv")])

    def disc(prm, tt, R, Wt):
        vop(lambda: V.tensor_scalar_min(out=tt[0], in0=prm[0], scalar1=-1e-4), R, Wt)
        vop(lambda: A.activation(out=tt[1], in_=prm[2], func=AF.Exp), R, Wt, e="act")
        vop(lambda: V.tensor_tensor(out=tt[8], in0=tt[0], in1=tt[1], op=ALU.mult), Wt, Wt)
        vop(lambda: A.activation(out=tt[2], in_=tt[8], func=AF.Exp), Wt, Wt, e="act")
        vop(lambda: V.scalar_tensor_tensor(out=tt[3], in0=prm[1], scalar=1.0 / (2 * math.pi), in1=tt[1], op0=ALU.mult,
                                           op1=ALU.mult), R + Wt, Wt)
        sincos(tt[3], tt[5], tt[4], tt[8], tt[9], Wt)

    def sincos(f, s_out, c_out, t1, t2, Wt):
        vop(lambda: V.tensor_scalar(out=t1, in0=f, scalar1=MAGIC, scalar2=MAGIC, op0=ALU.add, op1=ALU.subtract), Wt, Wt)
        vop(lambda: V.tensor_tensor(out=t1, in0=f, in1=t1, op=ALU.subtract), Wt, Wt)
        vop(lambda: A.activation(out=s_out, in_=t1, func=AF.Sin, scale=2 * math.pi), Wt, Wt, e="act")
        vop(lambda: V.tensor_scalar_add(out=t2, in0=f, scalar1=0.25), Wt, Wt)
        vop(lambda: V.tensor_scalar(out=t1, in0=t2, scalar1=MAGIC, scalar2=MAGIC, op0=ALU.add, op1=ALU.subtract), Wt, Wt)
        vop(lambda: V.tensor_tensor(out=t1, in0=t2, in1=t1, op=ALU.subtract), Wt, Wt)
        vop(lambda: A.activation(out=c_out, in_=t1, func=AF.Sin, scale=2 * math.pi), Wt, Wt, e="act")

    def ssm_prep(l):
        with scope(("prmB", [128, 5, 8, 64], F32), ("tB", [128, 10, 8, 64], F32), ("cc", [128, 2, 64, 16], F32),
                   ("targ", [128, 2, 3, TT], F32), ("tabw", [128, 2, 2, TT], F32)) as (prmB, tB, cc, targ, tabw):
            S.dma("sp", prmB[:], ssmB[:, l, :, :, :], writes=[B("Bp")])
            S.dma("sp", prmS[:], ssmS[:, l, :, :], writes=[B("Sp")])
            S.dma("sp", cc[:], ssmC[:, l, :, :, :], writes=[B("cc")])
            S.dma("sp", dsk[:], dskip[:, l, :], writes=[B("dsk")])
            pB = [prmB[:, i, :, :] for i in range(5)]
            tb = [tB[:, i, :, :] for i in range(10)]
            R, Wt = ["Bp", "Bt"], ["Bt"]
            disc(pB, tb, ["Bp"], ["Bt"])
            vop(lambda: V.tensor_tensor(out=tb[4], in0=tb[4], in1=tb[2], op=ALU.mult), Wt, Wt)
            vop(lambda: V.tensor_tensor(out=tb[5], in0=tb[5], in1=tb[2], op=ALU.mult), Wt, Wt)
            vop(lambda: V.tensor_scalar_add(out=tb[4], in0=tb[4], scalar1=-1.0), Wt, Wt)
            vop(lambda: V.tensor_tensor(out=tb[8], in0=tb[0], in1=tb[0], op=ALU.mult), Wt, Wt)
            vop(lambda: V.tensor_tensor(out=tb[9], in0=pB[1], in1=pB[1], op=ALU.mult), R, Wt)
            vop(lambda: V.tensor_tensor(out=tb[8], in0=tb[8], in1=tb[9], op=ALU.add), Wt, Wt)
            vop(lambda: V.reciprocal(tb[8], tb[8]), Wt, Wt)
            vop(lambda: V.tensor_tensor(out=tb[6], in0=tb[4], in1=tb[0], op=ALU.mult), Wt, Wt)
            vop(lambda: V.tensor_tensor(out=tb[9], in0=tb[5], in1=pB[1], op=ALU.mult), R, Wt)
            vop(lambda: V.tensor_tensor(out=tb[6], in0=tb[6], in1=tb[9], op=ALU.add), Wt, Wt)
            vop(lambda: V.tensor_tensor(out=tb[6], in0=tb[6], in1=tb[8], op=ALU.mult), Wt, Wt)
            vop(lambda: V.tensor_tensor(out=tb[7], in0=tb[5], in1=tb[0], op=ALU.mult), Wt, Wt)
            vop(lambda: V.tensor_tensor(out=tb[9], in0=tb[4], in1=pB[1], op=ALU.mult), R, Wt)
            vop(lambda: V.tensor_tensor(out=tb[7], in0=tb[7], in1=tb[9], op=ALU.subtract), Wt, Wt)
            vop(lambda: V.tensor_tensor(out=tb[7], in0=tb[7], in1=tb[8], op=ALU.mult), Wt, Wt)
            vop(lambda: V.tensor_tensor(out=tb[2], in0=tb[6], in1=pB[3], op=ALU.mult), R, Wt)
            vop(lambda: V.tensor_tensor(out=tb[9], in0=tb[7], in1=pB[4], op=ALU.mult), R, Wt)
            vop(lambda: V.tensor_tensor(out=tb[2], in0=tb[2], in1=tb[9], op=ALU.subtract), Wt, Wt)
            vop(lambda: V.tensor_tensor(out=tb[3], in0=tb[6], in1=pB[4], op=ALU.mult), R, Wt)
            vop(lambda: V.tensor_tensor(out=tb[9], in0=tb[7], in1=pB[3], op=ALU.mult), R, Wt)
            vop(lambda: V.tensor_tensor(out=tb[3], in0=tb[3], in1=tb[9], op=ALU.add), Wt, Wt)
            vop(lambda: V.tensor_scalar_mul(out=tb[9], in0=tb[2], scalar1=-1.0), Wt, Wt)
            for v in range(4):
                qm = cst[:, QM - 1024 + v:QM - 1024 + v + 1]
                vop(lambda: V.tensor_scalar(out=BA[:, :, v, 0:64], in0=tb[2], scalar1=qm, scalar2=None, op0=ALU.mult), ["Bt", "cst"], ["BA"])
                vop(lambda: V.tensor_scalar(out=BA[:, :, v, 64:128], in0=tb[3], scalar1=qm, scalar2=None, op0=ALU.mult), ["Bt", "cst"], ["BA"])
                vop(lambda: V.tensor_scalar(out=BB[:, :, v, 0:64], in0=tb[3], scalar1=qm, scalar2=None, op0=ALU.mult), ["Bt", "cst"], ["BB"])
                vop(lambda: V.tensor_scalar(out=BB[:, :, v, 64:128], in0=tb[9], scalar1=qm, scalar2=None, op0=ALU.mult), ["Bt", "cst"], ["BB"])
            pS = [prmS[:, i, :] for i in range(3)]
            ts_ = [tS[:, i, :] for i in range(10)]
            disc(pS, ts_, ["Sp"], ["St"])
            Wt = ["St"]
            vop(lambda: V.tensor_scalar_mul(out=ts_[6], in0=ts_[3], scalar1=float(TT)), Wt, Wt)
            sincos(ts_[6], ts_[5], ts_[4], ts_[8], ts_[9], Wt)
            vop(lambda: V.tensor_scalar(out=ts_[5], in0=ts_[5], scalar1=cst[:, QM - 1024 + 4:QM - 1024 + 5], scalar2=None, op0=ALU.mult), ["St", "cst"], Wt)
            for v in range(4):
                vop(lambda: V.tensor_scalar(out=C1[:, v::4, 16 * v:16 * v + 16], in0=cc[:, 0, v::4, :], scalar1=cst[:, QM - 1024 + 5:QM - 1024 + 6],
                                            scalar2=None, op0=ALU.mult), ["cc", "cst"], ["C1"])
                vop(lambda: V.tensor_scalar_mul(out=C2[:, v::4, 16 * v:16 * v + 16], in0=cc[:, 1, v::4, :], scalar1=-1.0), ["cc"], ["C2"])
            for g in range(64):
                sl = g % 2
                fcol = tS[:, 3, g:g + 1]
                tw = ["targ%d" % sl]
                tx, tr, ta = targ[:, sl, 0, :], targ[:, sl, 1, :], targ[:, sl, 2, :]
                vop(lambda: V.tensor_scalar(out=tx, in0=cst[:, IOTA - 1024:IOTA - 1024 + TT], scalar1=fcol, scalar2=None, op0=ALU.mult),
                    ["St", "cst"] + tw, tw)
                vop(lambda: V.tensor_scalar(out=tr, in0=tx, scalar1=MAGIC, scalar2=MAGIC, op0=ALU.add, op1=ALU.subtract), tw, tw)
                vop(lambda: V.tensor_tensor(out=tr, in0=tx, in1=tr, op=ALU.subtract), tw, tw)
                vop(lambda: A.activation(out=tabw[:, sl, 1, :], in_=tr, func=AF.Sin, scale=2 * math.pi), tw, ["tabw%d" % sl], e="act")
                vop(lambda: V.scalar_tensor_tensor(out=ta, in0=tr, scalar=-1.0, in1=tr, op0=ALU.mult, op1=ALU.max), tw, tw)
                vop(lambda: A.activation(out=tabw[:, sl, 0, :], in_=ta, func=AF.Sin, scale=-2 * math.pi, bias=halfpi[:, 0:1]),
                    tw + ["halfpi"], ["tabw%d" % sl], e="act")
                S.dma("sp", tabs.ap()[g], tabw[:, sl, :, :], reads=[B("tabw%d" % sl)], writes=[B("tabs")], owner=B("tabs"))
            vop(lambda: V.memset(state[:], 0.0), [], ["state"])

    def proj_fm(wl, c0, dst_ap, dstbuf, evac, dst2=None):
        wb, wa = load_w(colview(wl, c0))
        pb, pa = PS()

        def mm():
            last = None
            for k in range(KD):
                last = nc.tensor.matmul(pa, wa[:, k, :], uT[:, k, :], start=(k == 0), stop=(k == KD - 1))
            return last
        S.op("pe", mm, reads=[wb, B("uT")], writes=[pb])
        if evac == "act":
            S.op("act", lambda: A.copy(dst_ap, pa), reads=[pb], writes=[dstbuf])
        else:
            S.op("dve", lambda: V.tensor_copy(dst_ap, pa), reads=[pb], writes=[dstbuf])
        if dst2 is not None:
            S.op("dve", lambda: V.tensor_scalar(out=dst2[0], in0=pa, scalar1=1.0, scalar2=None, op0=ALU.mult), reads=[pb], writes=[dst2[1]])

    def dump(idx, src, t):
        S.dma("pool", dbg_o[idx, :, t * TT:(t + 1) * TT].rearrange("(kt p) n -> p kt n", p=128), src[:],
              reads=[B("qT%d" % q_) for q_ in range(8)] + [B("mqT%d" % q_) for q_ in range(8)] + [B("sinT")], writes=[B("dbgo")], owner=B("dbgo"))

    def mixer(l, t, seq_start):
        wl = W["w_in"][l]
        rmsnorm(nrm[:, l, 1, :], B("nrm"), hT, B("hT"), uT, B("uT"))
        with scope(("qT", [128, 8, TT], BF16), ("mqT", [128, 8, TT], BF16), ("sinT", [128, 8, TT], BF16),
                   ("sinF", [128, 8, TT], F32)) as (qT, mqT, sinT, sinF):
            with scope(("wk", [128, KD, 128], BF16)) as (wk,):
                for i in range(8):
                    proj_fm(wl, 1536 + i * 128, sinT[:, i, :], B("sinT"), "act", dst2=(sinF[:, i, :], B("sinF")))
                for gp in range(2):
                    S.dma("pool", wk[:], colview(wl, 1024 + gp * 128), writes=[B("wk")])
                    for gg in range(2):
                        g = 2 * gp + gg
                        for x in range(2):
                            i, wb = wslot_next()
                            S.op("dve", lambda: V.memset(wgu[:, i, :, (1 - x) * 64:(1 - x) * 64 + 64], 0.0), writes=[wb])
                            S.op("dve", lambda: V.tensor_scalar(out=wgu[:, i, :, x * 64:x * 64 + 64], in0=wk[:, :, gg * 64:(gg + 1) * 64],
                                                                scalar1=1.0, scalar2=None, op0=ALU.mult), reads=[B("wk")], writes=[wb])
                            pb, pa = PS()

                            def mm():
                                last = None
                                for k in range(KD):
                                    last = nc.tensor.matmul(pa, wgu[:, i, k, :], uT[:, k, :], start=(k == 0), stop=(k == KD - 1))
                                return last
                            S.op("pe", mm, reads=[wb, B("uT")], writes=[pb])
                            S.op("act", lambda: A.copy(kT2[:, x, g, 128:128 + TT], pa), reads=[pb], writes=[B("kT2")])
            items = []

            def proj_run(w, dst_ap, dstbuf):
                wb, wa = w
                pb, pa = PS()

                def mm():
                    last = None
                    for k in range(KD):
                        last = nc.tensor.matmul(pa, wa[:, k, :], uT[:, k, :], start=(k == 0), stop=(k == KD - 1))
                    return last
                S.op("pe", mm, reads=[wb, B("uT")], writes=[pb])
                S.op("act", lambda: A.copy(dst_ap, pa), reads=[pb], writes=[dstbuf])
            for i in range(8):
                items.append([lambda i=i: load_w(colview(wl, i * 128)), lambda w, i=i: proj_run(w, qT[:, i, :], B("qT%d" % i)), None])
            for i in range(8):
                items.append([lambda i=i: load_w(colview(wl, 2560 + i * 128)), lambda w, i=i: proj_run(w, mqT[:, i, :], B("mqT%d" % i)), None])
            vst = {}

            def v_load():
                vst["i0"], vst["b0"] = wslot_next()
                S.dma("pool", wgu[:, vst["i0"], :, :], colview(wl, 1280, 128), writes=[vst["b0"]])
                vst["i1"], vst["b1"] = wslot_next()
                S.dma("pool", wgu[:, vst["i1"], :, :], colview(wl, 1408, 128), writes=[vst["b1"]])

            def v_bq(bq):
                pb, pa = PS()

                def mm():
                    last = None
                    for half, wi in ((0, vst["i0"]), (1, vst["i1"])):
                        for k in range(KD):
                            last = nc.tensor.matmul(pa[:, half * 128:(half + 1) * 128], uT[:, k, bq * 128:(bq + 1) * 128],
                                                    wgu[:, wi, k, :], start=(k == 0), stop=(k == KD - 1))
                    return last
                S.op("pe", mm, reads=[vst["b0"], vst["b1"], B("uT")], writes=[pb])
                pv = pa[:, 0:256].rearrange("p (g d) -> p g d", g=4)
                S.op("act", lambda: A.copy(vaug[:, 1 + bq, :, 0:64], pv), reads=[pb], writes=[B("vaug")])
                S.op("act", lambda: A.copy(vaug[:, 1 + bq, :, 128:192], pv), reads=[pb], writes=[B("vaug")])
            items.append([v_load, lambda w: None, None])
            for bq in range(4):
                items.append([None, lambda w, bq=bq: v_bq(bq), None])
            nload, nrun = [0], [0]

            def issue_loads(upto):
                while nload[0] < min(upto, len(items)):
                    it_ = items[nload[0]]
                    if it_[0] is not None:
                        it_[2] = it_[0]()
                    nload[0] += 1

            def run_next():
                k = nrun[0]
                issue_loads(k + 3)
                items[k][1](items[k][2])
                nrun[0] += 1
            with scope(("tab", [128, 4, 2, TT], F32), ("xm", [128, 2, TT], F32), ("d12", [128, 2, 2, TT], BF16),
                       ("xs", [128, 2, TT], F32)) as (tab, xm, d12, xs):
                grp = {}
                ycur = [None]
                issue_loads(2)

                def geo(g):
                    j, m = g // 8, g % 8
                    return j, m // 4, m % 4, 64 * (m // 4), g % 2, g % 4

                def stA(g):
                    j, qd, v, r0, sl, ts4 = geo(g)
                    tb_ = B("tab%d" % ts4)
                    S.dma("sp", tab[:, ts4, :, :], tabs.ap()[g], reads=[B("tabs")], writes=[tb_], owner=tb_)
                    pab, pa = PS()
                    pbb, pb_ = PS()
                    S.op("pe", lambda: nc.tensor.matmul(pa, BA[r0:r0 + 64, j, v, :], sinT[r0:r0 + 64, j, :], start=True, stop=True),
                         reads=[B("BA"), B("sinT")], writes=[pab])
                    S.op("pe", lambda: nc.tensor.matmul(pb_, BB[r0:r0 + 64, j, v, :], sinT[r0:r0 + 64, j, :], start=True, stop=True),
                         reads=[B("BB"), B("sinT")], writes=[pbb])
                    grp[g] = (pab, pa, pbb, pb_)

                def stB(g):
                    j, qd, v, r0, sl, ts4 = geo(g)
                    tb_ = B("tab%d" % ts4)
                    pab, pa, pbb, pb_ = grp.pop(g)
                    S.op("dve", lambda: V.tensor_tensor(out=xm[:, 1, :], in0=pb_, in1=tab[:, ts4, 1, :], op=ALU.mult), reads=[pbb, tb_],
                         writes=[B("xm1")])
                    S.op("dve", lambda: V.tensor_tensor(out=pa, in0=pa, in1=tab[:, ts4, 0, :], op=ALU.mult), reads=[tb_], writes=[pab])
                    S.op("dve", lambda: V.tensor_tensor(out=xm[:, 0, :], in0=pa, in1=xm[:, 1, :], op=ALU.add), reads=[pab, B("xm1")],
                         writes=[B("xm0")])
                    S.op("dve", lambda: V.tensor_tensor_scan(out=xs[:, sl, :], data0=tS[:, 2, g:g + 1].to_broadcast([128, TT]),
                                                             data1=xm[:, 0, :], initial=state[:, g:g + 1], op0=ALU.mult, op1=ALU.add),
                         reads=[B("St"), B("xm0"), B("state")], writes=[B("xs%d" % sl)])
                    S.op("act", lambda: A.copy(slast[:, g:g + 1], xs[:, sl, TT - 1:TT]), reads=[B("xs%d" % sl)], writes=[B("slast")])

                def stC(g):
                    j, qd, v, r0, sl, ts4 = geo(g)
                    tb_ = B("tab%d" % ts4)
                    S.op("pool", lambda: nc.gpsimd.tensor_tensor(out=d12[:, sl, 0, :], in0=xs[:, sl, :], in1=tab[:, ts4, 0, :], op=ALU.mult),
                         reads=[B("xs%d" % sl), tb_], writes=[B("d12%d" % sl)])
                    S.op("pool", lambda: nc.gpsimd.tensor_tensor(out=d12[:, sl, 1, :], in0=xs[:, sl, :], in1=tab[:, ts4, 1, :], op=ALU.mult),
                         reads=[B("xs%d" % sl), tb_], writes=[B("d12%d" % sl)])

                def stD(g):
                    j, qd, v, r0, sl, ts4 = geo(g)
                    if v == 0:
                        ycur[0] = PSL()
                    pyb, py = ycur[0]

                    def mm_y():
                        nc.tensor.matmul(py[r0:r0 + 64, :], C1[:, g, :], d12[:, sl, 0, :], start=(v == 0), stop=False)
                        return nc.tensor.matmul(py[r0:r0 + 64, :], C2[:, g, :], d12[:, sl, 1, :], start=False, stop=(v == 3))
                    S.op("pe", mm_y, reads=[B("C1"), B("C2"), B("d12%d" % sl)], writes=[pyb])
                    if v == 3:
                        S.op("dve", lambda: V.scalar_tensor_tensor(out=sinT[r0:r0 + 64, j, :], in0=sinF[r0:r0 + 64, j, :],
                                                                   scalar=dsk[r0:r0 + 64, j:j + 1], in1=py[r0:r0 + 64, :],
                                                                   op0=ALU.mult, op1=ALU.add),
                             reads=[B("sinF"), B("dsk"), pyb], writes=[B("sinT")])

                for it in range(64 + 2):
                    if it < 64:
                        stA(it)
                    if 0 <= it - 1 < 64:
                        stB(it - 1)
                        stC(it - 1)
                    if 0 <= it - 2 < 64:
                        stD(it - 2)
                    if it % 3 == 2 and nrun[0] < len(items):
                        run_next()
                while nrun[0] < len(items):
                    run_next()
                pb, pa = PS()
                S.op("pe", lambda: nc.tensor.matmul(pa[:, 0:64], swapf[:], slast[:], start=True, stop=True), reads=[B("slast"), B("swapf")],
                     writes=[pb])
                vop(lambda: V.tensor_tensor(out=state[:], in0=slast[:], in1=tS[:, 4, :], op=ALU.mult), ["slast", "St"], ["state"])
                S.op("dve", lambda: V.tensor_tensor(out=slast[:], in0=pa[:, 0:64], in1=tS[:, 5, :], op=ALU.mult), reads=[pb, B("St")],
                     writes=[B("slast")])
                vop(lambda: V.tensor_tensor(out=state[:], in0=state[:], in1=slast[:], op=ALU.add), ["slast"], ["state"])
            with scope(("pT", [128, 2, TT], BF16), ("rden", [128, 2, TT], F32)) as (pT, rden):
                S.op("act", lambda: A.activation(out=esk[:], in_=skc[:, l, :], func=AF.Exp), reads=[B("skc")], writes=[B("esk")])
                units = [(bq, g, hp) for bq in range(4) for g in range(4) for hp in range(2)]

                def swa_s(u):
                    bq, g, hp = units[u]
                    psb_, ps_ = PS()
                    m0 = MASK_FIRST if (seq_start and bq == 0) else MASK_ROW
                    qt = 2 * g + hp
                    sl = u % 2

                    def mm_s():
                        nc.tensor.matmul(ps_, identb[:], maskb[:, m0:m0 + 512], start=True, stop=False)
                        last = None
                        for kb in range(2):
                            for r in range(2):
                                c0 = kb * 256 + r * 128
                                last = nc.tensor.matmul(ps_[:, c0:c0 + 128], kT2[:, r, g, (bq + kb) * 128:(bq + kb + 1) * 128],
                                                        qT[:, qt, bq * 128:(bq + 1) * 128], start=False, stop=(kb == 1 and r == 1))
                        return last
                    S.op("pe", mm_s, reads=[B("identb"), B("maskb"), B("kT2"), B("qT%d" % qt)], writes=[psb_])
                    S.op("act", lambda: A.activation(out=pT[:, sl, :], in_=ps_, func=AF.Exp, scale=0.125), reads=[psb_],
                         writes=[B("pT%d" % sl)])

                def swa_o(u):
                    bq, g, hp = units[u]
                    qt = 2 * g + hp
                    sl = u % 2
                    pob, po = PS()
                    pdb, pd = PS()

                    def mm_o():
                        n = 0
                        for kb in range(2):
                            for r in range(2):
                                lo = 0 if r == 0 else 64
                                c0 = kb * 256 + r * 128
                                nc.tensor.matmul(po[:, 0:128], vaug[:, bq + kb, g, lo:lo + 128], pT[:, sl, c0:c0 + 128],
                                                 start=(n == 0), stop=(n == 3))
                                n += 1
                        n = 0
                        last = None
                        for kb in range(2):
                            for r in range(2):
                                lo = 0 if r == 0 else 64
                                c0 = kb * 256 + r * 128
                                last = nc.tensor.matmul(pd[:, 0:128], onesb[:, lo:lo + 128], pT[:, sl, c0:c0 + 128],
                                                        start=(n == 0), stop=(n == 3))
                                n += 1
                        return last
                    S.op("pe", mm_o, reads=[B("vaug"), B("onesb"), B("pT%d" % sl)], writes=[pob, pdb])
                    idx = g * 2 + hp
                    S.op("dve", lambda: V.tensor_scalar(out=rden[:, sl, 0:128], in0=pd[:, 0:128], scalar1=esk[:, idx:idx + 1],
                                                        scalar2=None, op0=ALU.add), reads=[pdb, B("esk")], writes=[B("rden%d" % sl)])
                    S.op("dve", lambda: V.reciprocal(rden[:, sl, 0:128], rden[:, sl, 0:128]), reads=[B("rden%d" % sl)],
                         writes=[B("rden%d" % sl)])
                    S.op("dve", lambda: V.tensor_tensor(out=qT[:, qt, bq * 128:(bq + 1) * 128], in0=po[:, 0:128],
                                                        in1=rden[:, sl, 0:128], op=ALU.mult),
                         reads=[pob, B("rden%d" % sl)], writes=[B("qT%d" % qt)])

                for it in range(len(units) + 1):
                    if it < len(units):
                        swa_s(it)
                    if it >= 1:
                        swa_o(it - 1)
                if stop == 43:
                    return True
                vop(lambda: V.tensor_scalar(out=kT2[:, 0, :, 0:128], in0=kT2[:, 0, :, TT:TT + 128], scalar1=1.0, scalar2=None, op0=ALU.mult), ["kT2"], ["kT2"])
                vop(lambda: V.tensor_scalar(out=kT2[:, 1, :, 0:128], in0=kT2[:, 1, :, TT:TT + 128], scalar1=1.0, scalar2=None, op0=ALU.mult), ["kT2"], ["kT2"])
                vop(lambda: V.tensor_scalar(out=vaug[:, 0, :, :], in0=vaug[:, 4, :, :], scalar1=1.0, scalar2=None, op0=ALU.mult), ["vaug"], ["vaug"])
                for hm in range(4):
                    for mt in range(2):
                        pb, pa = PS()

                        def mm():
                            last = None
                            for dt_ in range(2):
                                last = nc.tensor.matmul(pa, mkT[:, 2 * hm + dt_, mt * 128:(mt + 1) * 128], mqT[:, 2 * hm + dt_, :],
                                                        start=(dt_ == 0), stop=(dt_ == 1))
                            return last
                        S.op("pe", mm, reads=[B("mkT"), B("mqT%d" % (2 * hm)), B("mqT%d" % (2 * hm + 1))], writes=[pb])
                        S.op("act", lambda: A.activation(out=pT[:, mt, :], in_=pa, func=AF.Exp, scale=1.0 / 16.0), reads=[pb],
                             writes=[B("pT%d" % mt)])
                    pdb, pd = PS()

                    def mm_d():
                        last = None
                        for mt in range(2):
                            last = nc.tensor.matmul(pd, onesf, pT[:, mt, :], start=(mt == 0), stop=(mt == 1))
                        return last
                    S.op("pe", mm_d, reads=[B("onesf"), B("pT0"), B("pT1")], writes=[pdb])
                    S.op("dve", lambda: V.reciprocal(rden[:, 0, :], pd), reads=[pdb], writes=[B("rden0")])
                    for dt_ in range(2):
                        pob, po = PS()

                        def mm_o():
                            last = None
                            for mt in range(2):
                                last = nc.tensor.matmul(po, mv[:, mt, (2 * hm + dt_) * 128:(2 * hm + dt_ + 1) * 128], pT[:, mt, :],
                                                        start=(mt == 0), stop=(mt == 1))
                            return last
                        S.op("pe", mm_o, reads=[B("mv"), B("pT0"), B("pT1")], writes=[pob])
                        S.op("dve", lambda: V.tensor_tensor(out=mqT[:, 2 * hm + dt_, :], in0=po, in1=rden[:, 0, :], op=ALU.mult),
                             reads=[pob, B("rden0")], writes=[B("mqT%d" % (2 * hm + dt_))])
            if dbg:
                dump(0, qT, t)
                dump(1, mqT, t)
                dump(2, sinT, t)
            if stop == 5:
                return True
            wglu = W["w_ssm_glu"][l]
            wsw = W["w_swa_up"][l]
            wmu = W["w_mem_up"][l]
            with scope(("mrg", [128, KD, TT], BF16), ("gsb", [128, 2, TT], F32), ("sg2", [128, 2, TT], F32)) as (mrg, gsb, sg2):
                def up(wview, c0, src, srcbuf):
                    i, wb = wslot_next()
                    S.dma("pool", wgu[:, i, 0:8, :], wview[c0 // 128].rearrange("p (kt c) -> p kt c", c=128), writes=[wb])
                    pb, pa = PS()

                    def mm():
                        last = None
                        for k in range(8):
                            last = nc.tensor.matmul(pa, wgu[:, i, k, :], src[:, k, :], start=(k == 0), stop=(k == 7))
                        return last
                    S.op("pe", mm, reads=[wb] + (srcbuf if isinstance(srcbuf, list) else [srcbuf]), writes=[pb])
                    return pb, pa

                def gate(bidx, i):
                    wb, wa = load_w(colview(wl, 3584 + bidx * D + i * 128))
                    pb, pa = PS()

                    def mm():
                        last = None
                        for k in range(KD):
                            last = nc.tensor.matmul(pa, wa[:, k, :], uT[:, k, :], start=(k == 0), stop=(k == KD - 1))
                        return last
                    S.op("pe", mm, reads=[wb, B("uT")], writes=[pb])
                    S.op("act", lambda: A.activation(out=gsb[:, 0, :], in_=pa, func=AF.Sigmoid), reads=[pb], writes=[B("gsb0")])

                for i in range(KD):
                    gate(0, i)
                    pb, pa = up(wsw, i * 128, qT, [B("qT%d" % q_) for q_ in range(8)])
                    S.op("dve", lambda: V.tensor_tensor(out=gsb[:, 1, :], in0=pa, in1=gsb[:, 0, :], op=ALU.mult),
                         reads=[pb, B("gsb0")], writes=[B("gsb1")])
                    gate(1, i)
                    pb2, pa2 = up(wglu, 2048 + i * 128, sinT, B("sinT"))
                    S.op("act", lambda: A.activation(out=sg2[:, 0, :], in_=pa2, func=AF.Sigmoid), reads=[pb2], writes=[B("sg0")])
                    S.op("dve", lambda: V.tensor_tensor(out=sg2[:, 0, :], in0=sg2[:, 0, :], in1=gsb[:, 0, :], op=ALU.mult),
                         reads=[B("sg0"), B("gsb0")], writes=[B("sg0")])
                    pb3, pa3 = up(wglu, i * 128, sinT, B("sinT"))
                    S.op("dve", lambda: V.tensor_tensor(out=sg2[:, 0, :], in0=pa3, in1=sg2[:, 0, :], op=ALU.mult),
                         reads=[pb3, B("sg0")], writes=[B("sg0")])
                    S.op("dve", lambda: V.tensor_tensor(out=gsb[:, 1, :], in0=gsb[:, 1, :], in1=sg2[:, 0, :], op=ALU.add),
                         reads=[B("sg0"), B("gsb1")], writes=[B("gsb1")])
                    gate(2, i)
                    pb4, pa4 = up(wmu, i * 128, mqT, [B("mqT%d" % q_) for q_ in range(8)])
                    S.op("dve", lambda: V.tensor_tensor(out=sg2[:, 1, :], in0=pa4, in1=gsb[:, 0, :], op=ALU.mult),
                         reads=[pb4, B("gsb0")], writes=[B("sg1")])
                    S.op("dve", lambda: V.tensor_tensor(out=mrg[:, i, :], in0=gsb[:, 1, :], in1=sg2[:, 1, :], op=ALU.add),
                         reads=[B("sg1"), B("gsb1")], writes=[B("mrg")])
                wov = W["w_out"][l]
                for i in range(KD):
                    wb, wa = load_w(colview(wov, i * 128))
                    pb, pa = PS()

                    def mm():
                        last = None
                        for k in range(KD):
                            last = nc.tensor.matmul(pa, wa[:, k, :], mrg[:, k, :], start=(k == 0), stop=(k == KD - 1))
                        return last
                    S.op("pe", mm, reads=[wb, B("mrg")], writes=[pb])
                    S.op("dve", lambda: V.tensor_tensor(out=hT[:, i, :], in0=pa, in1=hT[:, i, :], op=ALU.add), reads=[pb],
                         writes=[B("hT")])

    def emit_all():
        for l in range(NL):
            for t in range(NT):
                src = xT if l == 0 else hs.ap()
                S.dma("sp", hT[:], src[:, t * TT:(t + 1) * TT].rearrange("(kt p) n -> p kt n", p=128), writes=[B("hT")],
                      reads=[B("hs")] if l > 0 else [])
                if stop == 0:
                    raise _Stop()
                if t == 0:
                    mem_prep(l)
                    if stop == 1:
                        raise _Stop()
                    ssm_prep(l)
                    if stop == 2:
                        raise _Stop()
                ffn(l, "ffn1_w_in", "ffn1_w_out", 0)
                if stop == 3:
                    raise _Stop()
                if mixer(l, t, seq_start=(t == 0)):
                    raise _Stop()
                ffn(l, "ffn2_w_in", "ffn2_w_out", 3)
                if l < NL - 1:
                    S.dma("sp", hs.ap()[:, t * TT:(t + 1) * TT].rearrange("(kt p) n -> p kt n", p=128), hT[:], reads=[B("hT")],
                          writes=[B("hs")], owner=B("hs"))
                else:
                    with scope(("outF", [128, KD, TT], F32)) as (outF,):
                        rmsnorm(fnrm, B("fnrm"), hT, B("hT"), outF, B("outF"))
                        S.dma("sp", outT[:, t * TT:(t + 1) * TT].rearrange("(kt p) n -> p kt n", p=128), outF[:], reads=[B("outF")],
                              writes=[B("outT")], owner=B("outT"))
    try:
        emit_all()
    except _Stop:
        S.barrier()
        S.dma("sp", outT[:, 0:TT].rearrange("(kt p) n -> p kt n", p=128), hT[:], reads=[B("hT")], writes=[B("outT")], owner=B("outT"))
    S.wait_everything("sp")
    es.close()
    return nc


def _lay_kt(v):
    v = np.asarray(v, np.float32)
    lead = v.shape[:-1]
    return np.ascontiguousarray(np.moveaxis(v.reshape(*lead, KD, 128), -1, 0))


def host_small(NL, ffn1_norm, mix_norm, mem_norm, ffn2_norm, final_norm, sinks, lam_re, lam_im, log_dt, b_re, b_im, c_re, c_im,
               d_skip):
    norms = np.stack([_lay_kt(ffn1_norm), _lay_kt(mix_norm), _lay_kt(mem_norm), _lay_kt(ffn2_norm)], axis=2)
    fnorm = _lay_kt(final_norm)
    p = np.arange(128)
    sk = np.asarray(sinks, np.float32)
    sinkc = np.stack([sk[:, 4 * (i // 2) + 2 * (i % 2) + (p // 64)] for i in range(8)], axis=-1).transpose(1, 0, 2)
    P_, CH = 64, 16

    def blay(a):
        a = np.asarray(a, np.float32).reshape(NL, 8, 8, P_)
        a = np.repeat(a[:, :, :, None, :], CH, axis=3)
        return a.transpose(2, 3, 0, 1, 4).reshape(128, NL, 8, P_)

    def bmat(a):
        a = np.asarray(a, np.float32).reshape(NL, 8, 8, P_, CH)
        return a.transpose(2, 4, 0, 1, 3).reshape(128, NL, 8, P_)
    ldt = np.repeat(np.asarray(log_dt, np.float32)[:, :, None], P_, axis=2)
    ssmB = np.stack([blay(lam_re), blay(lam_im), blay(ldt), bmat(b_re), bmat(b_im)], axis=2)

    def slay(a):
        a = np.asarray(a, np.float32).transpose(2, 0, 1)
        return np.concatenate([a, a], axis=0)
    ssmS = np.stack([slay(lam_re), slay(lam_im), slay(ldt)], axis=2)
    cr = np.asarray(c_re, np.float32).transpose(3, 0, 1, 2)
    ci = np.asarray(c_im, np.float32).transpose(3, 0, 1, 2)
    ssmC = np.stack([np.concatenate([cr, ci], 0), np.concatenate([ci, cr], 0)], axis=2)
    dsk = np.ascontiguousarray(np.asarray(d_skip, np.float32).reshape(NL, 8, 128).transpose(2, 0, 1))
    consts = np.zeros((128, NCONST), np.float32)
    kk = np.arange(128)[:, None]
    qq = np.arange(128)[None, :]
    mprev = np.where(kk > qq, 0.0, -30000.0).astype(np.float32)
    mcur = np.where(kk <= qq, 0.0, -30000.0).astype(np.float32)
    neg = np.full((128, 128), -30000.0, np.float32)
    consts[:, 0:512] = np.concatenate([mprev, mprev, mcur, mcur], 1)
    consts[:, 512:1024] = np.concatenate([neg, neg, mcur, mcur], 1)
    consts[:, IOTA:IOTA + TT] = np.arange(1, TT + 1, dtype=np.float32)[None, :]
    consts[:, IDENT:IDENT + 128] = np.eye(128, dtype=np.float32)
    for v in range(4):
        consts[:, QM + v] = ((p // 16) % 4 == v).astype(np.float32)
    consts[:, QM + 4] = np.where(p < 64, -1.0, 1.0)
    consts[:, QM + 5] = np.where(p < 64, 1.0, -1.0)
    consts[p, SWAPP + (p + 64) % 128] = 1.0
    return {"norms": np.ascontiguousarray(norms), "fnorm": fnorm, "sinkc": np.ascontiguousarray(sinkc),
            "ssmB": np.ascontiguousarray(ssmB), "ssmS": np.ascontiguousarray(ssmS), "ssmC": np.ascontiguousarray(ssmC),
            "dskip": dsk, "consts": consts}


def _tile_lay(w):
    w = np.asarray(w, np.float32)
    NL_, K_, N_ = w.shape
    return np.ascontiguousarray(w.reshape(NL_, K_ // 128, 128, N_ // 128, 128).transpose(0, 3, 2, 1, 4)).reshape(NL_, N_ // 128, 128, K_)


def host_weights(ffn1_w_in, ffn1_w_out, w_in, w_mem_kv, w_ssm_glu, w_swa_up, w_mem_up, w_out, ffn2_w_in, ffn2_w_out):
    f = lambda a: np.ascontiguousarray(np.asarray(a, np.float32))
    return {"ffn1_w_in": _tile_lay(ffn1_w_in), "ffn1_w_out": f(ffn1_w_out), "w_in": _tile_lay(w_in), "w_mem_kv": _tile_lay(w_mem_kv),
            "w_ssm_glu": _tile_lay(w_ssm_glu), "w_swa_up": _tile_lay(w_swa_up), "w_mem_up": _tile_lay(w_mem_up), "w_out": _tile_lay(w_out),
            "ffn2_w_in": _tile_lay(ffn2_w_in), "ffn2_w_out": f(ffn2_w_out)}


def kernel(x, mem, ffn1_norm, ffn1_w_in, ffn1_w_out, mix_norm, mem_norm, w_in, sinks, w_mem_kv, lam_re, lam_im, log_dt,
           b_re, b_im, c_re, c_im, d_skip, w_ssm_glu, w_swa_up, w_mem_up, w_out, ffn2_norm, ffn2_w_in, ffn2_w_out, final_norm):
    x = np.asarray(x, np.float32)
    Bsz, L, _ = x.shape
    NL = np.asarray(ffn1_w_in).shape[0]
    NT = L // TT
    nc = build(NT, NL)
    small = host_small(NL, ffn1_norm, mix_norm, mem_norm, ffn2_norm, final_norm, sinks, lam_re, lam_im, log_dt, b_re, b_im,
                       c_re, c_im, d_skip)
    wts = host_weights(ffn1_w_in, ffn1_w_out, w_in, w_mem_kv, w_ssm_glu, w_swa_up, w_mem_up, w_out, ffn2_w_in, ffn2_w_out)
    memf_ = np.asarray(mem, np.float32)
    in_maps = []
    for b_ in range(Bsz):
        m = {"xT": np.ascontiguousarray(x[b_].T), "memT": np.ascontiguousarray(memf_[b_].T)}
        m.update(wts)
        m.update(small)
        in_maps.append(m)
    res = run_bass_kernel_spmd(nc, in_maps, core_ids=list(range(Bsz)))
    return np.stack([res.results[b_]["outT"].T for b_ in range(Bsz)], axis=0).astype(np.float32)
```
